# Optimizing a Trainium2 kernel written in Bass

```python
import jax, jax.numpy as jnp
from jax import lax
import numpy as np

D_MODEL = 1024
BATCH = 8
SEQ = 4096
DEPTH = 1

D_MIX = D_MODEL
D_A = D_MIX // 2
D_B = D_MIX - D_A
HEAD_A = 64
G_A = D_A // HEAD_A
HEAD_B = 64
H_B = D_B // HEAD_B
CHUNK = 128
DECAY_RANK = 64
ICLR_RANK = 64
LN_EPS = 1e-5
GN_EPS = 64e-5
ALPHA = (2 * DEPTH) ** 0.25
BETA = (8 * DEPTH) ** -0.25

N_COLS_A = 3 * D_A
N_COLS_B = 4 * D_B + DECAY_RANK + ICLR_RANK
N_COLS = N_COLS_A + N_COLS_B

kernel_name = "hybrid_gmlp_rwkv7_deepnorm_adaln"


def _layer_norm(x, g, b, eps):
    xf = x.astype(jnp.float32)
    mu = jnp.mean(xf, axis=-1, keepdims=True)
    var = jnp.mean(jnp.square(xf - mu), axis=-1, keepdims=True)
    y = (xf - mu) * lax.rsqrt(var + eps)
    return (y * g.astype(jnp.float32) + b.astype(jnp.float32)).astype(x.dtype)


def _token_shift(p, mu):
    prev = jnp.pad(p, ((0, 0), (1, 0), (0, 0)))[:, :-1]
    return p + mu * (prev - p)


def _rwkv7_scan(r, w, k, v, a, b):
    Bsz, T, H, N = r.shape
    seq_first = lambda z: jnp.swapaxes(z, 0, 1)
    xs = tuple(seq_first(z) for z in (r, w, k, v, a, b))

    def step(S, inp):
        r_t, w_t, k_t, v_t, a_t, b_t = inp
        sa = jnp.einsum('bhij,bhj->bhi', S, a_t)
        S = S * w_t[:, :, None, :] + sa[..., None] * b_t[:, :, None, :] + v_t[..., None] * k_t[:, :, None, :]
        y = jnp.einsum('bhij,bhj->bhi', S, r_t)
        return S, y

    S0 = jnp.zeros((Bsz, H, N, N), jnp.float32)
    _, ys = lax.scan(step, S0, xs)
    return jnp.swapaxes(ys, 0, 1)


def _gmlp_branch(p_a, ln_v_g, ln_v_b, w_spatial, b_spatial):
    Bsz, T, _ = p_a.shape
    u = jax.nn.gelu(p_a[..., :D_A])
    v = jax.nn.gelu(p_a[..., D_A:2 * D_A])
    gate = p_a[..., 2 * D_A:]
    v = v.reshape(Bsz, T, G_A, HEAD_A)
    v = _layer_norm(v, ln_v_g, ln_v_b, LN_EPS)
    v = v.reshape(Bsz, T // CHUNK, CHUNK, G_A, HEAD_A)
    causal = jnp.tril(jnp.ones((CHUNK, CHUNK), dtype=bool))
    w_m = jnp.where(causal[None], w_spatial, jnp.zeros((), w_spatial.dtype))
    v = jnp.einsum('gts,bcsgd->bctgd', w_m, v) + jnp.swapaxes(b_spatial, 0, 1)[None, None, :, :, None]
    v = v.reshape(Bsz, T, D_A)
    return u * v * jax.nn.silu(gate)


def _rwkv7_branch(p_b, mu_b, w0, w_up, a0, a_up, k_k, k_a, r_k, gn_g, gn_b):
    Bsz, T, _ = p_b.shape
    p_b = _token_shift(p_b, mu_b)
    o = 0
    r = p_b[..., o:o + D_B]; o += D_B
    k = p_b[..., o:o + D_B]; o += D_B
    v = p_b[..., o:o + D_B]; o += D_B
    gate = p_b[..., o:o + D_B]; o += D_B
    wd = p_b[..., o:o + DECAY_RANK]; o += DECAY_RANK
    ad = p_b[..., o:o + ICLR_RANK]

    w_raw = -jax.nn.softplus(-(w0 + jnp.tanh(wd) @ w_up)) - 0.5
    a = jax.nn.sigmoid(a0 + ad @ a_up)

    heads = lambda z: z.reshape(Bsz, T, H_B, HEAD_B).astype(jnp.float32)
    kk = heads(k * k_k)
    kk = kk * lax.rsqrt(jnp.sum(kk * kk, axis=-1, keepdims=True) + 1e-12)
    k = k * (1 + (a - 1) * k_a)
    rh, kh, vh, ah = heads(r), heads(k), heads(v), heads(a)
    decay = jnp.exp(-jnp.exp(heads(w_raw)))

    y = _rwkv7_scan(rh, decay, kh, vh, -kk, kk * ah)
    y = _layer_norm(y, gn_g, gn_b, GN_EPS)
    bonus = jnp.sum(rh * kh * r_k.astype(jnp.float32), axis=-1, keepdims=True) * vh
    y = (y + bonus).reshape(Bsz, T, D_B).astype(p_b.dtype)
    return y * jax.nn.silu(gate)


def setup_inputs(seed: int = 0) -> dict:
    key = jax.random.key(seed)
    ks = jax.random.split(key, 24)
    f32 = jnp.float32
    nrm = lambda k, s, sc: jax.random.normal(k, s, f32) * sc
    return {
        "x": nrm(ks[0], (BATCH, SEQ, D_MODEL), 1.0),
        "c": nrm(ks[1], (BATCH, D_MODEL), 1.0),
        "w_ada": nrm(ks[2], (D_MODEL, 3 * D_MODEL), D_MODEL ** -0.5),
        "b_ada": nrm(ks[3], (3 * D_MODEL,), 0.02),
        "w_in": nrm(ks[4], (D_MODEL, N_COLS), D_MODEL ** -0.5),
        "mu_b": jax.random.uniform(ks[5], (N_COLS_B,), f32),
        "ln_v_g": 1.0 + nrm(ks[6], (G_A, HEAD_A), 0.05),
        "ln_v_b": nrm(ks[7], (G_A, HEAD_A), 0.02),
        "w_spatial": nrm(ks[8], (G_A, CHUNK, CHUNK), CHUNK ** -0.5),
        "b_spatial": 1.0 + nrm(ks[9], (G_A, CHUNK), 0.1),
        "w0": jax.random.uniform(ks[10], (D_B,), f32, -4.0, 1.0),
        "w_up": nrm(ks[11], (DECAY_RANK, D_B), 0.5 * DECAY_RANK ** -0.5),
        "a0": nrm(ks[12], (D_B,), 0.5),
        "a_up": nrm(ks[13], (ICLR_RANK, D_B), 0.5 * ICLR_RANK ** -0.5),
        "k_k": 0.85 + nrm(ks[14], (D_B,), 0.05),
        "k_a": 1.0 + nrm(ks[15], (D_B,), 0.05),
        "r_k": nrm(ks[16], (H_B, HEAD_B), 0.1),
        "gn_g": 1.0 + nrm(ks[17], (H_B, HEAD_B), 0.05),
        "gn_b": nrm(ks[18], (H_B, HEAD_B), 0.02),
        "w_out": nrm(ks[19], (D_MIX, D_MODEL), BETA * D_MIX ** -0.5),
        "ln_g": 1.0 + nrm(ks[20], (D_MODEL,), 0.05),
        "ln_b": nrm(ks[21], (D_MODEL,), 0.02),
    }


def reference(x, c, w_ada, b_ada, w_in, mu_b, ln_v_g, ln_v_b, w_spatial, b_spatial,
              w0, w_up, a0, a_up, k_k, k_a, r_k, gn_g, gn_b, w_out, ln_g, ln_b):
    mod = jax.nn.silu(c) @ w_ada + b_ada
    shift, scale, gate = [m[:, None, :] for m in jnp.split(mod, 3, axis=-1)]
    for _ in range(DEPTH):
        h = x * (1 + scale) + shift
        p = h @ w_in
        out_a = _gmlp_branch(p[..., :N_COLS_A], ln_v_g, ln_v_b, w_spatial, b_spatial)
        out_b = _rwkv7_branch(p[..., N_COLS_A:], mu_b, w0, w_up, a0, a_up, k_k, k_a, r_k, gn_g, gn_b)
        o = jnp.concatenate([out_a, out_b], axis=-1) @ w_out
        x = _layer_norm(ALPHA * x + gate * o, ln_g, ln_b, LN_EPS)
    return x
```

```python
import contextlib
import numpy as np
import concourse.bass as bass
import concourse.mybir as mybir
from concourse.bass_utils import run_bass_kernel_spmd

F32 = mybir.dt.float32
BF16 = mybir.dt.bfloat16
AF = mybir.ActivationFunctionType
ALU = mybir.AluOpType
AX = mybir.AxisListType

D = 1024
T = 4096
NCOL = 3712
NB0 = 1536
LN_EPS = 1e-5
GN_EPS = 64e-5
ALPHA = 2.0 ** 0.25
NEGC = -0.6065306597126334
SEM_CAP = 30000


class Buf:
    def __init__(self, t, name):
        self.t = t
        self.name = name
        self.w = [None, None]
        self.r = [[], []]
        self.psum = False

    def __getitem__(self, idx):
        return self.t[idx]


class Node:
    __slots__ = ("eng", "fn", "dma", "deps", "dur", "start", "end", "tok", "users", "ndep", "est")

    def __init__(self, eng, fn, dma, deps, dur):
        self.eng = eng
        self.fn = fn
        self.dma = dma
        self.deps = deps
        self.dur = dur
        self.start = None
        self.end = None
        self.tok = None
        self.users = []
        self.ndep = 0
        self.est = 0.0


import os
HOP_NS = float(os.environ.get("K_HOP", "900"))
SAME_NS = float(os.environ.get("K_SAME", "120"))
PE_SCALE = float(os.environ.get("K_PESCALE", "1.0"))


class Prog:
    def __init__(self, nc, es):
        self.nc = nc
        self.es = es
        self.names = ["pe", "act", "dve", "pool", "sp"]
        self.prog = {k: [] for k in self.names}
        self.nsem = 0
        self.dsems = {}
        self.bufs = {}
        self.nodes = []
        self.nops = 0

    def dsem(self, name):
        if name not in self.dsems:
            s = self.es.enter_context(self.nc.semaphore("d_" + name))
            self.dsems[name] = [s, "d_" + name, 0]
        return self.dsems[name]

    def sb(self, name, shape, dt=F32):
        t = self.es.enter_context(self.nc.sbuf_tensor(name, shape, dt))
        b = Buf(t, name)
        self.bufs[name] = b
        return b

    def ps(self, name, shape, dt=F32):
        t = self.es.enter_context(self.nc.psum_tensor(name, shape, dt))
        b = Buf(t, name)
        b.psum = True
        self.bufs[name] = b
        return b

    def _halves(self, ap):
        p0 = ap.start_partition() if callable(ap.start_partition) else ap.start_partition
        n = ap.partition_size() if callable(ap.partition_size) else ap.partition_size
        hs = []
        if p0 < 64:
            hs.append(0)
        if p0 + n > 64:
            hs.append(1)
        return hs

    def _regions(self, aps):
        out = []
        for ap in aps:
            nm = ap.tensor.name
            b = self.bufs.get(nm)
            if b is None:
                continue
            for h in self._halves(ap):
                out.append((b, h))
        return out

    def op(self, eng, fn, outs=(), ins=(), dma=None, dur=300.0):
        self.nops += 1
        if self.nops > getattr(self, "max_ops", 10 ** 9):
            return None
        rd = self._regions(ins)
        wr = self._regions(outs)
        deps = set()
        for (b, h) in rd:
            if b.w[h] is not None:
                deps.add(b.w[h])
            if b.psum:
                for t in b.r[h]:
                    if self.nodes[t].eng != eng:
                        deps.add(t)
        for (b, h) in wr:
            if b.w[h] is not None:
                deps.add(b.w[h])
            for t in b.r[h]:
                deps.add(t)
        nid = len(self.nodes)
        self.nodes.append(Node(eng, fn, dma, sorted(deps), float(dur)))
        for (b, h) in rd:
            b.r[h].append(nid)
        for (b, h) in wr:
            b.w[h] = nid
            b.r[h] = []
        return nid

    def schedule(self):
        nodes = self.nodes
        for i, nd in enumerate(nodes):
            nd.ndep = len(nd.deps)
            for d in nd.deps:
                nodes[d].users.append(i)
        free = {k: 0.0 for k in self.names}
        cands = {k: [] for k in self.names}
        order = {k: [] for k in self.names}
        for i, nd in enumerate(nodes):
            if nd.ndep == 0:
                cands[nd.eng].append(i)
        remaining = len(nodes)
        while remaining:
            best = None
            for k in self.names:
                cl = cands[k]
                if not cl:
                    continue
                f = free[k]
                pick = None
                for i in cl:
                    e = nodes[i].est
                    if e <= f:
                        if pick is None or pick[0] > f or i < pick[1]:
                            pick = (f, i) if (pick is None or pick[0] > f or i < pick[1]) else pick
                    else:
                        if pick is None or e < pick[0]:
                            pick = (e, i)
                if best is None or pick[0] < best[0] or (pick[0] == best[0] and pick[1] < best[1]):
                    best = (pick[0], pick[1], k)
            st, i, k = best
            nd = nodes[i]
            cands[k].remove(i)
            nd.start = st
            if nd.dma is not None:
                nd.end = st + nd.dur
                free[k] = st + 60.0
            else:
                nd.end = st + nd.dur
                free[k] = nd.end
            order[k].append(i)
            remaining -= 1
            for u in nd.users:
                un = nodes[u]
                lat = SAME_NS if (un.eng == k and nd.dma is None) else HOP_NS
                if un.eng == "pe" and k == "pe":
                    lat = 0.0
                t = nd.end + lat
                if t > un.est:
                    un.est = t
                un.ndep -= 1
                if un.ndep == 0:
                    cands[un.eng].append(u)
        self.makespan = max(nd.end for nd in nodes) if nodes else 0.0
        return order

    def emit(self, order):
        nodes = self.nodes
        sem = {}
        cnt = {}
        waited = {k: {} for k in self.names}

        def new_sem(k):
            self.nsem += 1
            nm = "s_%s_%d" % (k, self.nsem)
            s_ = self.es.enter_context(self.nc.semaphore(nm))
            sem[k] = (s_, nm)
            cnt[k] = 0
        for k in self.names:
            new_sem(k)
        allids = sorted(range(len(nodes)), key=lambda i: (nodes[i].start, i))
        pos = {k: 0 for k in self.names}
        for i in allids:
            nd = nodes[i]
            k = nd.eng
            assert order[k][pos[k]] == i
            pos[k] += 1
            for d in nd.deps:
                dn = nodes[d]
                if dn.eng == "pe" and k == "pe" and dn.dma is None:
                    continue
                semh, semn, val = dn.tok
                if waited[k].get(semn, 0) >= val:
                    continue
                waited[k][semn] = val
                self.prog[k].append(("wait", semh, val))
            if nd.dma is not None:
                ds = self.dsem(nd.dma)
                ds[2] += 16
                nd.tok = (ds[0], ds[1], ds[2])
                self.prog[k].append(("op", nd.fn, ds[0], 16))
            else:
                if cnt[k] >= SEM_CAP:
                    new_sem(k)
                cnt[k] += 1
                nd.tok = (sem[k][0], sem[k][1], cnt[k])
                self.prog[k].append(("op", nd.fn, sem[k][0], 1))
        self.waited = waited

    def wait_node(self, eng, nid):
        semh, semn, val = self.nodes[nid].tok
        if self.waited[eng].get(semn, 0) >= val:
            return
        self.waited[eng][semn] = val
        self.prog[eng].append(("wait", semh, val))

    def replay(self, eng, e):
        for item in self.prog[eng]:
            if item[0] == "wait":
                e.wait_ge(item[1], item[2])
            else:
                item[1](e).then_inc(item[2], item[3])


def build(NT=32, dbg=None, max_ops=None):
    nc = bass.Bass("TRN2", target_bir_lowering=False)
    dram = {}

    def din(name, shape):
        dram[name] = nc.dram_tensor(name, list(shape), F32, kind="ExternalInput").ap()
        return dram[name]

    xT_d = din("xT", [D, T])
    x_d = din("x", [T, D])
    cT_d = din("cT", [128, 8])
    wada_d = din("w_ada", [D, 3 * D])
    bada_row_d = din("b_ada_row", [1, 3 * D])
    bada_g_d = din("b_ada_g", [1, D])
    win_d = din("w_in", [D, NCOL])
    mu_d = din("mu_b", [2176])
    lnvg_d = din("ln_v_g", [512])
    lnvb_d = din("ln_v_b", [512])
    wsT_d = din("w_spT", [8, 128, 128])
    bsp_d = din("b_sp", [128, 8])
    w0a0_d = din("w0a0", [1, 1024])
    wua_d = din("wua", [128, 512])
    kk_d = din("k_k", [512])
    ka_d = din("k_a", [512])
    rk_d = din("r_k", [512])
    gng_d = din("gn_g", [512])
    gnb_d = din("gn_b", [512])
    wout_d = din("w_out", [D, D])
    lng_d = din("ln_g", [D])
    lnb_d = din("ln_b", [D])
    y_d = nc.dram_tensor("y", [T, D], F32, kind="ExternalOutput").ap()
    dbg_d = None
    if dbg:
        dbg_d = nc.dram_tensor("dbg", [128, 16384], F32, kind="ExternalOutput").ap()

    es = contextlib.ExitStack()
    with es:
        P = Prog(nc, es)
        if max_ops is not None:
            P.max_ops = max_ops
        sb, ps, op = P.sb, P.ps, P.op

        w_in_c = [sb("w_in_c%d" % j, [128, 8, min(512, NCOL - j * 512)], BF16) for j in range(8)]
        w_out_b = sb("w_out_b", [128, 8, D], BF16)
        wsT_b = sb("wsT_b", [128, 8, 128], BF16)
        wua_b = sb("wua_b", [128, 512], BF16)
        identb = sb("identb", [128, 128], BF16)
        identf = sb("identf", [128, 128])
        ones_f = sb("ones_f", [128, 128])
        tri = sb("tri", [128, 3, 128])
        maskNN = sb("maskNN", [128, 4, 128], BF16)
        maskS8 = sb("maskS8", [128, 8, 64], BF16)
        maskI8 = sb("maskI8", [128, 8, 64], BF16)
        I8 = sb("I8", [128, 8, 64], BF16)
        mu = sb("mu", [128, 2176])
        lnvg = sb("lnvg", [128, 512])
        lnvb = sb("lnvb", [128, 512])
        kk_ = sb("kk_", [128, 512])
        ka_ = sb("ka_", [128, 512])
        rk_ = sb("rk_", [128, 512])
        gng = sb("gng", [128, 512])
        gnb = sb("gnb", [128, 512])
        lng = sb("lng", [128, D])
        lnb = sb("lnb", [128, D])
        rows = sb("rows", [128, 512])
        small = sb("small", [128, 64])
        mhalf = sb("mhalf", [128, 8])
        XT = sb("XT", [128, 4, 128])
        hT = [sb("hT%d" % i, [128, 8, 130], BF16) for i in range(2)]
        hd = sb("hd", [128, 8, 128], BF16)
        XA = [sb("XA0", [128, D])] * 2
        U_sb = sb("U_sb", [128, 512], BF16)
        SG_a = sb("SG_a", [128, 512], BF16)
        SG_b = [sb("SG_b%d" % i, [128, 512], BF16) for i in range(3)]
        R_b = sb("R_b", [128, 512], BF16)
        V_g = sb("V_g", [128, 512])
        VN = sb("VN", [128, 512], BF16)
        PBt = sb("PBt", [128, 2176])
        SIGW = sb("SIGW", [128, 512])
        A_sig = sb("A_sig", [128, 512], BF16)
        NA = sb("NA", [128, 512], BF16)
        BB = sb("BB", [128, 512], BF16)
        KM = sb("KM", [128, 512], BF16)
        S1 = sb("S1", [128, 512])
        S2 = sb("S2", [128, 512])
        S3 = sb("S3", [128, 512])
        TA = sb("TA", [128, 512], BF16)
        TB = sb("TB", [128, 512], BF16)
        BH = [sb("BH%d" % i, [128, 512], BF16) for i in range(2)]
        KH = [sb("KH%d" % i, [128, 512], BF16) for i in range(2)]
        VB = [sb("VB%d" % i, [128, 512], BF16) for i in range(3)]
        WA_T = sb("WA_T", [128, 128], BF16)
        FM2 = [sb("FM2_%d" % i, [128, 4, 8, 64], BF16) for i in range(2)]
        X0 = sb("X0", [128, 8, 192], BF16)
        Xa = sb("Xa", [128, 8, 192], BF16)
        ABR2 = [sb("ABR%d" % i, [128, 8, 64], BF16) for i in range(2)]
        AAK2 = [sb("AAK%d" % i, [128, 8, 64], BF16) for i in range(2)]
        AKR2 = [sb("AKR%d" % i, [128, 8, 64], BF16) for i in range(2)]
        EB2 = [sb("EB%d" % i, [128, 8, 64], BF16) for i in range(2)]
        Tm2 = [sb("Tm%d" % i, [128, 8, 64], BF16) for i in range(2)]
        WTs = sb("WTs", [128, 512], BF16)
        onesb_t = WTs
        UTs = sb("UTs", [128, 512], BF16)
        ST = sb("ST", [128, 8, 64])
        Dg = sb("Dg", [128, 8, 64])
        STb = sb("STb", [128, 8, 64], BF16)
        GC = [sb("GC%d" % i, [128, 16]) for i in range(3)]
        YS = sb("YS", [128, 512])
        CATa = [sb("CATa%d" % i, [128, 512], BF16) for i in range(3)]
        CATb = sb("CATb", [128, 512], BF16)
        CATT = sb("CATT", [128, 8, 128], BF16)
        st = sb("st", [128, 64])
        stB = sb("stB", [128, 64])

        PB0 = ps("PB0", [128, 512])
        PB1 = ps("PB1", [128, 512])
        PM = ps("PM", [128, 512])
        PY = ps("PY", [128, 512])
        PX0 = ps("PX0", [128, 512])
        PX1 = ps("PX1", [128, 512])
        PP = ps("PP", [128, 512])
        PC = ps("PC", [128, 512])
        PBIG = [PB0, PB1]
        big_i = [0]

        def nextbig():
            b = PBIG[big_i[0] % 2]
            big_i[0] += 1
            return b

        def fsz(ap):
            n_ = 1
            for d_ in ap.shape[1:]:
                n_ *= int(d_)
            return n_

        def mm(out, lhsT, rhs, start=True, stop=True, tp=None):
            n_ = fsz(rhs)
            d_ = 55.0 if n_ <= 64 else 0.55 * n_ + 20
            if lhsT.dtype == F32:
                d_ = 4 * d_ + 100
            d_ *= PE_SCALE
            if tp is None:
                op("pe", lambda e: e.matmul(out, lhsT=lhsT, rhs=rhs, start=start, stop=stop),
                   outs=[out], ins=[lhsT, rhs], dur=d_)
            else:
                op("pe", lambda e: e.matmul(out, lhsT=lhsT, rhs=rhs, start=start, stop=stop, tile_position=tp),
                   outs=[out], ins=[lhsT, rhs], dur=d_)

        def tr(out, in_, ident, tp=None):
            if tp is None:
                op("pe", lambda e: e.transpose(out=out, in_=in_, identity=ident), outs=[out], ins=[in_, ident], dur=100.0)
            else:
                op("pe", lambda e: e.transpose(out=out, in_=in_, identity=ident, tile_position=tp), outs=[out], ins=[in_, ident], dur=55.0)

        def act(out, in_, func, scale=None, bias=None):
            kw = {}
            if scale is not None:
                kw["scale"] = scale
            if bias is not None:
                kw["bias"] = bias
            ins = [in_] + [a for a in (scale, bias) if hasattr(a, "tensor")]
            op("act", lambda e: e.activation(out=out, in_=in_, func=func, **kw), outs=[out], ins=ins,
               dur=240.0 + 0.7 * fsz(out) + (200.0 if len(ins) > 1 else 0.0))

        def tt(eng, out, in0, in1, o):
            d_ = (110.0 + 1.1 * fsz(out)) if eng == "dve" else (300.0 + 1.75 * fsz(out))
            if o == ALU.pow:
                d_ = 1400.0
            op(eng, lambda e: e.tensor_tensor(out=out, in0=in0, in1=in1, op=o), outs=[out], ins=[in0, in1], dur=d_)

        def ts(eng, out, in0, s1, s2, o0, o1=None):
            ins = [in0] + [a for a in (s1, s2) if hasattr(a, "tensor")]
            d_ = (110.0 + 1.1 * fsz(out)) if eng == "dve" else (300.0 + 1.75 * fsz(out))
            if o1 is None:
                op(eng, lambda e: e.tensor_scalar(out=out, in0=in0, scalar1=s1, scalar2=None, op0=o0),
                   outs=[out], ins=ins, dur=d_)
            else:
                op(eng, lambda e: e.tensor_scalar(out=out, in0=in0, scalar1=s1, scalar2=s2, op0=o0, op1=o1),
                   outs=[out], ins=ins, dur=d_)

        def stt(out, in0, s, in1, o0, o1):
            ins = [in0, in1] + ([s] if hasattr(s, "tensor") else [])
            op("dve", lambda e: e.scalar_tensor_tensor(out=out, in0=in0, scalar=s, in1=in1, op0=o0, op1=o1),
               outs=[out], ins=ins, dur=110.0 + 1.1 * fsz(out))

        def red(out, in_):
            op("dve", lambda e: e.tensor_reduce(out=out, in_=in_, axis=AX.X, op=ALU.add), outs=[out], ins=[in_],
               dur=110.0 + 1.1 * fsz(in_))

        def cp(eng, out, in_):
            if eng == "act":
                op("act", lambda e: e.activation(out=out, in_=in_, func=AF.Copy), outs=[out], ins=[in_],
                   dur=240.0 + 0.7 * fsz(out))
            else:
                op(eng, lambda e: e.tensor_copy(out=out, in_=in_), outs=[out], ins=[in_],
                   dur=(110.0 + 1.1 * fsz(out)) if eng == "dve" else (300.0 + 1.75 * fsz(out)))

        def dma(out, in_, sem):
            op("sp", lambda e: e.dma_start(out=out, in_=in_), outs=[out], ins=[in_], dma=sem,
               dur=2000.0 + 0.004 * 4 * fsz(out) * 128)

        def memset(eng, ap, v):
            op(eng, lambda e: e.memset(ap, v), outs=[ap], dur=300.0)

        def asel(out, in_, pattern, cmp, base, cm):
            op("pool", lambda e: e.affine_select(out=out, in_=in_, pattern=pattern, compare_op=cmp, fill=0.0,
                                                 base=base, channel_multiplier=cm), outs=[out], ins=[in_],
               dur=250.0 + 0.9 * fsz(out))

        dbg_off = [0]

        def dump(name, ap, ncols):
            if not dbg:
                return
            o = dbg_off[0]
            n = ap.partition_size() if callable(ap.partition_size) else ap.partition_size
            p0 = ap.start_partition() if callable(ap.start_partition) else ap.start_partition
            dbg[name] = (o, p0, n, ncols)
            dma(dbg_d[p0:p0 + n, o:o + ncols], ap, "dbg_" + name)
            dbg_off[0] += ncols

        memset("pool", ones_f[:], 1.0)
        memset("pool", WTs[:], 1.0)
        memset("pool", tri[:], 0.0)
        memset("pool", mhalf[:], -0.5)
        memset("pool", hT[1][:], 0.0)
        memset("pool", ST[:], 0.0)
        memset("pool", STb[:], 0.0)
        memset("pool", S3[:], NEGC)
        asel(identf[:], ones_f[:], [[-1, 128]], ALU.is_equal, 0, 1)
        cp("pool", identb[:], identf[:])
        for c in range(2):
            psl = slice(c * 64, (c + 1) * 64)
            asel(tri[psl, 0, c * 64:(c + 1) * 64], S3[psl, 0:64], [[1, 64]], ALU.is_ge, 0, -1)
            asel(tri[psl, 1, c * 64:(c + 1) * 64], S3[psl, 0:64], [[1, 64]], ALU.is_gt, 0, -1)
            asel(tri[psl, 2, c * 64:(c + 1) * 64], S3[psl, 0:64], [[-1, 64]], ALU.is_gt, 0, 1)
            asel(maskS8[psl, :, :], WTs[psl, :].rearrange("p (h t) -> p h t", h=8), [[0, 8], [1, 64]], ALU.is_gt, 0, -1)
            asel(maskI8[psl, :, :], WTs[psl, :].rearrange("p (h t) -> p h t", h=8), [[0, 8], [1, 64]], ALU.is_ge, 0, -1)
            asel(I8[psl, :, :], WTs[psl, :].rearrange("p (h t) -> p h t", h=8), [[0, 8], [-1, 64]], ALU.is_equal, 0, 1)
            asel(maskNN[psl, :, 0:64], WTs[psl, 0:256].rearrange("p (h t) -> p h t", h=4), [[0, 4], [-1, 64]], ALU.is_gt, 0, 1)
            asel(maskNN[psl, :, 64:128], WTs[psl, 0:256].rearrange("p (h t) -> p h t", h=4), [[0, 4], [1, 64]], ALU.is_gt, 0, -1)

        for (tile_, src, n) in [(mu, mu_d, 2176), (lnvg, lnvg_d, 512), (lnvb, lnvb_d, 512), (kk_, kk_d, 512),
                                (ka_, ka_d, 512), (rk_, rk_d, 512), (gng, gng_d, 512), (gnb, gnb_d, 512),
                                (lng, lng_d, D), (lnb, lnb_d, D)]:
            dma(tile_[:], src.partition_broadcast(128), "par_" + tile_.name)
        dma(small[:, 8:16], bsp_d[:, :], "par_bsp")
        dma(rows[0:1, :], bada_g_d[:, 0:512], "par_rows0")
        dma(rows[32:33, :], bada_g_d[:, 512:1024], "par_rows32")
        dma(small[:, 0:8], cT_d[:, :], "par_c")
        act(small[:, 24:32], small[:, 0:8], AF.Silu)

        for g in range(8):
            stg = XA[0]
            dma(stg[:, 0:128], wsT_d[g, :, :], "stgw")
            asel(stg[:, 128:256], stg[:, 0:128], [[1, 128]], ALU.is_ge, 0, -1)
            cp("pool", wsT_b[:, g, :], stg[:, 128:256])
        stg = XA[0]
        dma(stg[:, 0:512], wua_d[:, :], "stgw")
        cp("dve", wua_b[:], stg[:, 0:512])

        slots = [S1[:, :], S2[:, :], V_g[:, :], YS[:, :], SIGW[:, :], S3[:, :]]
        si = [0]
        ring = [list(range(6))]

        def stage(src_ap, w=512):
            r_ = ring[0]
            i = r_[si[0] % len(r_)]
            si[0] += 1
            v = slots[i][:, 0:w]
            dma(v, src_ap, "stg%d" % i)
            return v

        mod_banks = [PX0, PX1, PP, PC, PB0, PB1]
        dma(PBt[0:1, 0:2048], bada_row_d[:, 0:2048], "par_badarow")

        def mod_chunk(j):
            po = 32 if j == 5 else 0
            for kc in range(8):
                v = stage(wada_d[kc * 128:(kc + 1) * 128, j * 512:(j + 1) * 512])
                if po:
                    mm(mod_banks[j][po:po + 1, :], small[:, 24 + kc:25 + kc], v, start=(kc == 0), stop=(kc == 7), tp=(0, po))
                else:
                    mm(mod_banks[j][0:1, :], small[:, 24 + kc:25 + kc], v, start=(kc == 0), stop=(kc == 7))
        for j in range(4):
            mod_chunk(j)
            tt("dve", PBt[0:1, j * 512:(j + 1) * 512], mod_banks[j][0:1, :], PBt[0:1, j * 512:(j + 1) * 512], ALU.add)
        for j in range(16):
            mm(PM[:, j:j + 1], PBt[0:1, j * 128:(j + 1) * 128], ones_f[0:1, 0:1])
        cp("dve", small[:, 32:48], PM[:, 0:16])
        ts("dve", small[:, 40:48], small[:, 40:48], 1.0, None, ALU.add)
        ci = 0
        for j in range(8):
            c0 = j * 512
            w = min(512, NCOL - c0)
            for kc in range(8):
                v = stage(win_d[kc * 128:(kc + 1) * 128, c0:c0 + w], w)
                cp(["act", "dve", "pool"][ci % 3], w_in_c[j][:, kc, 0:w], v)
                ci += 1
        ring[0] = [3, 5]
        for j in (4, 5):
            mod_chunk(j)
        tt("dve", rows[0:1, 0:512], PB0[0:1, :], rows[0:1, 0:512], ALU.add)
        tt("dve", rows[32:33, 0:512], PB1[32:33, :], rows[32:33, 0:512], ALU.add)
        for n2 in range(2):
            bank = [PX0, PX1][n2]
            mm(bank[:, :], ones_f[32 * n2:32 * n2 + 1, 0:128], rows[32 * n2:32 * n2 + 1, 0:512])
            cp("act", XA[0][:, n2 * 512:(n2 + 1) * 512], bank[:, :])
        for nb in range(2):
            for kc in range(8):
                v = stage(wout_d[kc * 128:(kc + 1) * 128, nb * 512:(nb + 1) * 512])
                tt(["dve", "pool"][ci % 2], w_out_b[:, kc, nb * 512:(nb + 1) * 512], v,
                   XA[0][:, nb * 512:(nb + 1) * 512], ALU.mult)
                ci += 1
        dma(rows[0:1, 0:512], w0a0_d[:, 0:512], "par_rows0")
        dma(rows[64:65, 0:512], w0a0_d[:, 512:1024], "par_rows64")

        xT_v = xT_d.rearrange("(k p) t -> p k t", p=128)

        PMb = PM[:, :].bitcast(BF16)

        def front(n):
            steps = []
            par = n % 2
            h_ = hT[par]
            tok = slice(n * 128, (n + 1) * 128)
            FMp, VBp, BHp, KHp, SGp = FM2[par], VB[n % 3], BH[par], KH[par], SG_b[n % 3]
            GCp, CAp = GC[n % 3], CATa[n % 3]

            def s_load():
                for kc in range(8):
                    dma(XT[:, kc % 4, :], xT_v[:, kc, tok], "xT%d" % (kc % 4))
                    ts("dve", h_[:, kc, 2:130], XT[:, kc % 4, :], small[:, 40 + kc:41 + kc], small[:, 32 + kc:33 + kc],
                       ALU.mult, ALU.add)
                cp("dve", h_[:, :, 1:2], hT[1 - par][:, :, 129:130])
                tt("dve", hd[:], h_[:, :, 1:129], h_[:, :, 2:130], ALU.subtract)
            steps.append(s_load)

            def proj(j, src=None):
                c0 = j * 512
                w = min(512, NCOL - c0)
                bank = nextbig()
                for kc in range(8):
                    lh = h_[:, kc, 2:130] if src is None else src[:, kc, :]
                    mm(bank[:, 0:w], lh, w_in_c[j][:, kc, 0:w], start=(kc == 0), stop=(kc == 7))
                return bank, w

            def s_u():
                bank, w = proj(0)
                act(U_sb[:], bank[:, :], AF.Gelu_apprx_tanh)
            steps.append(s_u)

            def s_v():
                bank, w = proj(1)
                act(V_g[:], bank[:, :], AF.Gelu_apprx_tanh)
            steps.append(s_v)

            def s_ga():
                bank, w = proj(2)
                act(SG_a[:], bank[:, :], AF.Silu)
            steps.append(s_ga)

            def stats(stt_, src3, sq, sq3, eps):
                red(stt_[:, 0:8], src3)
                tt("pool", sq, src3.rearrange("p g d -> p (g d)") if False else sq_src[0], sq_src[0], ALU.mult)
                red(stt_[:, 8:16], sq3)
                ts("dve", stt_[:, 0:8], stt_[:, 0:8], 1.0 / 64, None, ALU.mult)
                tt("dve", stt_[:, 16:24], stt_[:, 0:8], stt_[:, 0:8], ALU.mult)
                ts("dve", stt_[:, 8:16], stt_[:, 8:16], 1.0 / 64, eps, ALU.mult, ALU.add)
                tt("dve", stt_[:, 8:16], stt_[:, 8:16], stt_[:, 16:24], ALU.subtract)
                tt("pool", stt_[:, 24:32], stt_[:, 8:16], mhalf[:], ALU.pow)
            sq_src = [None]

            def s_ln():
                Vg3 = V_g[:].rearrange("p (g d) -> p g d", g=8)
                S33 = S3[:].rearrange("p (g d) -> p g d", g=8)
                sq_src[0] = V_g[:]
                stats(st, Vg3, S3[:], S33, LN_EPS)
                mb = st[:, 0:8].unsqueeze(2).broadcast_to([128, 8, 64])
                rb = st[:, 24:32].unsqueeze(2).broadcast_to([128, 8, 64])
                tt("dve", S33, Vg3, mb, ALU.subtract)
                tt("dve", S33, S33, rb, ALU.mult)
                tt("pool", S3[:], S3[:], lnvg[:], ALU.mult)
                tt("pool", VN[:], S3[:], lnvb[:], ALU.add)
            steps.append(s_ln)

            def s_sp():
                tt("pool", U_sb[:], U_sb[:], SG_a[:], ALU.mult)
                for g in range(8):
                    mm(PM[:, g * 64:(g + 1) * 64], wsT_b[:, g, :], VN[:, g * 64:(g + 1) * 64], start=True, stop=True)
                tt("dve", S3[:].rearrange("p (g d) -> p g d", g=8), PM[:, :].rearrange("p (g d) -> p g d", g=8),
                   small[:, 8:16].unsqueeze(2).broadcast_to([128, 8, 64]), ALU.add)
                tt("dve", CAp[:], S3[:], U_sb[:], ALU.mult)
                if dbg and n == dbg.get("_tile", 0):
                    dump("out_a", S3[:], 512)
            steps.append(s_sp)

            def mk_b(j):
                def s_b():
                    c0 = j * 512
                    tmp = V_g if j % 2 == 0 else U_sb
                    bank_p, w = proj(3 + j)
                    cp("act", PBt[:, c0:c0 + w], bank_p[:, 0:w])
                    bank_d, w = proj(3 + j, src=hd)
                    cp("act", tmp[:, 0:w], bank_d[:, 0:w])
                    tt("dve", tmp[:, 0:w], tmp[:, 0:w], mu[:, c0:c0 + w], ALU.mult)
                    tt("dve", PBt[:, c0:c0 + w], PBt[:, c0:c0 + w], tmp[:, 0:w], ALU.add)
                return s_b
            for j in range(5):
                steps.append(mk_b(j))

            r_ = PBt[:, 0:512]
            k_ = PBt[:, 512:1024]
            v_ = PBt[:, 1024:1536]
            g_ = PBt[:, 1536:2048]

            def s_wa():
                if dbg and n == dbg.get("_tile", 0):
                    dump("pb", PBt[:, :], 2176)
                act(SGp[:], g_, AF.Silu)
                cp("act", VBp[:], v_)
                cp("act", R_b[:], r_)
                tr(PM[:, 0:128], PBt[:, 2048:2176], identf[:])
                act(WA_T[0:64, :], PM[0:64, 0:128], AF.Tanh)
                cp("act", WA_T[64:128, :], PM[64:128, 0:128])
                mm(PM[:, :], WA_T[0:64, :], wua_b[0:64, :], start=True, stop=False)
                mm(PM[:, :], ones_f[0:1, 0:128], rows[0:1, 0:512], start=False, stop=True)
                act(SIGW[:], PM[:, :], AF.Sigmoid)
                mm(PY[:, :], WA_T[64:128, :], wua_b[64:128, :], start=True, stop=False)
                mm(PY[:, :], ones_f[64:65, 0:128], rows[64:65, 0:512], start=False, stop=True)
                act(A_sig[:], PY[:, :], AF.Sigmoid)
            steps.append(s_wa)

            def s_kk():
                S13 = S1[:].rearrange("p (g d) -> p g d", g=8)
                S23 = S2[:].rearrange("p (g d) -> p g d", g=8)
                tt("dve", S1[:], k_, kk_[:], ALU.mult)
                tt("dve", S2[:], S1[:], S1[:], ALU.mult)
                red(st[:, 32:40], S23)
                ts("dve", st[:, 32:40], st[:, 32:40], 1e-12, None, ALU.add)
                tt("pool", st[:, 40:48], st[:, 32:40], mhalf[:], ALU.pow)
                tt("dve", NA[:].rearrange("p (g d) -> p g d", g=8), S13,
                   st[:, 40:48].unsqueeze(2).broadcast_to([128, 8, 64]), ALU.mult)
                tt("dve", BB[:], NA[:], A_sig[:], ALU.mult)
                stt(S2[:], A_sig[:], -1.0, ka_[:], ALU.add, ALU.mult)
                stt(KM[:], S2[:], 1.0, k_, ALU.add, ALU.mult)
                tt("pool", S2[:], R_b[:], KM[:], ALU.mult)
                tt("pool", S2[:], S2[:], rk_[:], ALU.mult)
                red(GCp[:, 8:16], S23)
            steps.append(s_kk)

            def fm_pass(pi):
                for i, kt in enumerate([TA, TB]):
                    for h in range(8):
                        for c in range(2):
                            cs = slice(c * 64, (c + 1) * 64)
                            o = (i * 8 + h) * 64
                            tr(PMb[cs, o:o + 64], kt[cs, h * 64:(h + 1) * 64], identb[cs, c * 64:(c + 1) * 64],
                               tp=(c * 64, c * 64))
                cp("act" if pi == 0 else "dve", FMp[:, 2 * pi:2 * pi + 2, :, :],
                   PMb.rearrange("p (a h t) -> p a h t", a=2, h=8))

            def s_cum():
                mm(PY[:, :], tri[:, 0, :], SIGW[:])
                act(S1[:], PY[:, :], AF.Exp)
                act(S2[:], PY[:, :], AF.Exp, scale=-1.0)
                tt("dve", TB[:], R_b[:], S1[:], ALU.mult)
                mm(PY[:, :], tri[:, 1, :], SIGW[:])
                act(S1[:], PY[:, :], AF.Exp)
                stt(TA[:], NA[:], -1.0, S1[:], ALU.mult, ALU.mult)
                fm_pass(0)
            steps.append(s_cum)

            def s_cum2():
                tt("dve", TA[:], BB[:], S2[:], ALU.mult)
                tt("dve", TB[:], KM[:], S2[:], ALU.mult)
                mm(PY[:, :], tri[:, 2, :], SIGW[:])
                act(S1[:], PY[:, :], AF.Exp)
                tt("dve", BHp[:], BB[:], S1[:], ALU.mult)
                tt("pool", KHp[:], KM[:], S1[:], ALU.mult)
                for h in range(8):
                    for c in range(2):
                        cs = slice(c * 64, (c + 1) * 64)
                        mm(PY[cs, h:h + 1], SIGW[cs, h * 64:(h + 1) * 64], ones_f[cs, 0:1], tp=(c * 64, c * 64))
                act(GCp[:, 0:8], PY[:, 0:8], AF.Exp, scale=NEGC)
                fm_pass(1)
            steps.append(s_cum2)
            return steps

        def inv(n):
            steps = []
            par = n % 2
            FMp = FM2[par]
            ABR, AAK, AKR, EB, Tm = ABR2[par], AAK2[par], AKR2[par], EB2[par], Tm2[par]

            def fm(kind, h, c):
                return FMp[c * 64:(c + 1) * 64, kind, h, :]

            def s_A():
                for h in range(8):
                    for c in range(2):
                        cs = slice(c * 64, (c + 1) * 64)
                        bank = PX0 if h < 4 else PX1
                        o = (h % 4) * 128
                        mm(bank[cs, o + 64:o + 128], fm(2, h, c), fm(0, h, c), tp=(c * 64, c * 64))
                        mm(bank[cs, o:o + 64], fm(0, h, c), fm(2, h, c), tp=(c * 64, c * 64))
                for h4, bank in enumerate([PX0, PX1]):
                    tt("dve", X0[:, h4 * 4:(h4 + 1) * 4, 0:128], bank[:, :].rearrange("p (h t) -> p h t", h=4),
                       maskNN[:], ALU.mult)
                tt("pool", EB[:], I8[:], X0[:, :, 0:64], ALU.subtract)
            steps.append(s_A)

            def s_A2():
                for (dst, ka, kb, msk) in [(ABR, 2, 1, maskI8), (AAK, 3, 0, maskS8), (AKR, 3, 1, maskI8)]:
                    for h in range(8):
                        for c in range(2):
                            cs = slice(c * 64, (c + 1) * 64)
                            mm(PP[cs, h * 64:(h + 1) * 64], fm(ka, h, c), fm(kb, h, c), tp=(c * 64, c * 64))
                    tt("dve", dst[:], PP[:, :].rearrange("p (h t) -> p h t", h=8), msk[:], ALU.mult)
            steps.append(s_A2)

            seq = [(X0, Xa), (Xa, X0), (X0, Xa), (Xa, X0), (X0, Xa), (Xa, None)]

            def mk_round(k):
                S_, D_ = seq[k]

                def s_r():
                    for h in range(8):
                        for c in range(2):
                            cs = slice(c * 64, (c + 1) * 64)
                            tp = (c * 64, c * 64)
                            bank = PX0 if h < 4 else PX1
                            o = (h % 4) * 128
                            if k == 0:
                                mm(bank[cs, o:o + 64], S_[cs, h, 0:64], S_[cs, h, 64:128], tp=tp)
                            elif k <= 3:
                                mm(bank[cs, o:o + 128], S_[cs, h, 0:64], S_[cs, h, 64:192], tp=tp)
                            else:
                                mm(bank[cs, o + 64:o + 128], S_[cs, h, 0:64], S_[cs, h, 128:192], tp=tp)
                            if k <= 4:
                                mm(PP[cs, h * 64:(h + 1) * 64], S_[cs, h, 64:128], S_[cs, h, 0:64], tp=tp)
                    for h4, bank in enumerate([PX0, PX1]):
                        hs_ = slice(h4 * 4, (h4 + 1) * 4)
                        bv = bank[:, :].rearrange("p (h t) -> p h t", h=4)
                        if k <= 3:
                            cp("act", D_[:, hs_, 64:128], bv[:, :, 0:64])
                        if k == 0:
                            tt("pool", D_[:, hs_, 128:192], S_[:, hs_, 64:128], I8[:, hs_, :], ALU.add)
                        elif k <= 4:
                            tt("dve", D_[:, hs_, 128:192], bv[:, :, 64:128], S_[:, hs_, 128:192], ALU.add)
                        else:
                            tt("dve", Tm[:, hs_, :], bv[:, :, 64:128], S_[:, hs_, 128:192], ALU.add)
                    if k <= 4:
                        cp("act", D_[:, :, 0:64], PP[:, :].rearrange("p (h t) -> p h t", h=8))
                return s_r
            for k in range(6):
                steps.append(mk_round(k))

            def s_E():
                for h in range(8):
                    for c in range(2):
                        cs = slice(c * 64, (c + 1) * 64)
                        mm(PP[cs, h * 64:(h + 1) * 64], EB[cs, h, :], Tm[cs, h, :], tp=(c * 64, c * 64))
                tt("dve", EB[:], I8[:], PP[:, :].rearrange("p (h t) -> p h t", h=8), ALU.subtract)
            steps.append(s_E)
            return steps

        def chain_out(n):
            steps = []
            par = n % 2
            xa = XA[par]
            tok = slice(n * 128, (n + 1) * 128)
            isdbg = dbg and n == dbg.get("_tile", 0)
            FMp, VBp, BHp, KHp, SGp = FM2[par], VB[n % 3], BH[par], KH[par], SG_b[n % 3]
            GCp, CAp = GC[n % 3], CATa[n % 3]
            ABR, AAK, AKR, EB, Tm = ABR2[par], AAK2[par], AKR2[par], EB2[par], Tm2[par]

            def fm(kind, h, c):
                return FMp[c * 64:(c + 1) * 64, kind, h, :]

            def s_pre():
                dma(xa[:], x_d[tok, :], "xa")
                for c in range(2):
                    cs = slice(c * 64, (c + 1) * 64)
                    asel(Dg[cs, :, :], GCp[cs, 0:8].unsqueeze(2).broadcast_to([64, 8, 64]), [[0, 8], [-1, 64]],
                         ALU.is_equal, 0, 1)
            steps.append(s_pre)

            def mk_chain(c):
                cs = slice(c * 64, (c + 1) * 64)
                os_ = slice((1 - c) * 64, (2 - c) * 64)
                tpc = (c * 64, c * 64)

                def s_w():
                    for h in range(8):
                        mm(PC[cs, h * 64:(h + 1) * 64], fm(0, h, c), STb[cs, h, :], start=True, stop=False, tp=tpc)
                        mm(PC[cs, h * 64:(h + 1) * 64], AAK[cs, h, :], VBp[cs, h * 64:(h + 1) * 64], start=False, stop=True, tp=tpc)
                    cp("act", WTs[cs, :], PC[cs, :])

                def s_u():
                    for h in range(8):
                        mm(PC[cs, h * 64:(h + 1) * 64], Tm[cs, h, :], WTs[cs, h * 64:(h + 1) * 64], tp=tpc)
                    cp("act", WTs[cs, :], PC[cs, :])
                    for h in range(8):
                        mm(PC[cs, h * 64:(h + 1) * 64], EB[cs, h, :], WTs[cs, h * 64:(h + 1) * 64], tp=tpc)
                    tt("dve", UTs[cs, :], PC[cs, :], WTs[cs, :], ALU.add)

                def s_s():
                    tps = (c * 64, (1 - c) * 64)
                    for h in range(8):
                        o = PC[os_, h * 64:(h + 1) * 64]
                        mm(o, Dg[cs, h, :], ST[cs, h, :], start=True, stop=False, tp=tps)
                        mm(o, BHp[cs, h * 64:(h + 1) * 64], UTs[cs, h * 64:(h + 1) * 64], start=False, stop=False, tp=tps)
                        mm(o, KHp[cs, h * 64:(h + 1) * 64], VBp[cs, h * 64:(h + 1) * 64], start=False, stop=True, tp=tps)
                    cp("dve", ST[os_, :, :], PC[os_, :].rearrange("p (h i) -> p h i", h=8))
                    cp("act", STb[os_, :, :], PC[os_, :].rearrange("p (h i) -> p h i", h=8))

                def s_y():
                    for h in range(8):
                        o = PY[cs, h * 64:(h + 1) * 64]
                        mm(o, fm(1, h, c), STb[cs, h, :], start=True, stop=False, tp=tpc)
                        mm(o, ABR[cs, h, :], UTs[cs, h * 64:(h + 1) * 64], start=False, stop=False, tp=tpc)
                        mm(o, AKR[cs, h, :], VBp[cs, h * 64:(h + 1) * 64], start=False, stop=True, tp=tpc)
                    cp("act", YS[cs, :], PY[cs, :])
                return [s_w, s_u, s_s, s_y]
            for c in range(2):
                steps.extend(mk_chain(c))

            def s_post():
                if isdbg:
                    dump("ys", YS[:, :], 512)
                Y3 = YS[:].rearrange("p (g d) -> p g d", g=8)
                S33 = S3[:].rearrange("p (g d) -> p g d", g=8)
                red(stB[:, 0:8], Y3)
                tt("pool", S3[:], YS[:], YS[:], ALU.mult)
                red(stB[:, 8:16], S33)
                ts("dve", stB[:, 0:8], stB[:, 0:8], 1.0 / 64, None, ALU.mult)
                tt("dve", stB[:, 16:24], stB[:, 0:8], stB[:, 0:8], ALU.mult)
                ts("dve", stB[:, 8:16], stB[:, 8:16], 1.0 / 64, GN_EPS, ALU.mult, ALU.add)
                tt("dve", stB[:, 8:16], stB[:, 8:16], stB[:, 16:24], ALU.subtract)
                tt("pool", stB[:, 24:32], stB[:, 8:16], mhalf[:], ALU.pow)
                mb = stB[:, 0:8].unsqueeze(2).broadcast_to([128, 8, 64])
                rb = stB[:, 24:32].unsqueeze(2).broadcast_to([128, 8, 64])
                bb = GCp[:, 8:16].unsqueeze(2).broadcast_to([128, 8, 64])
                tt("pool", S33, VBp[:].rearrange("p (g d) -> p g d", g=8), bb, ALU.mult)
                tt("dve", Y3, Y3, mb, ALU.subtract)
                tt("dve", Y3, Y3, rb, ALU.mult)
                tt("dve", YS[:], YS[:], gng[:], ALU.mult)
                tt("dve", YS[:], YS[:], gnb[:], ALU.add)
                tt("dve", YS[:], YS[:], S3[:], ALU.add)
                tt("dve", CATb[:], YS[:], SGp[:], ALU.mult)
                if isdbg:
                    dump("out_b", YS[:, :], 512)
            steps.append(s_post)

            def s_out():
                for kc in range(8):
                    src_ = CAp if kc < 4 else CATb
                    tr(PMb[:, kc * 128:(kc + 1) * 128], src_[:, (kc % 4) * 128:(kc % 4 + 1) * 128], identb[:])
                cp("act", CATT[:], PMb.rearrange("p (k t) -> p k t", k=8))
                for nb in range(2):
                    bank = nextbig()
                    for kc in range(8):
                        mm(bank[:, :], CATT[:, kc, :], w_out_b[:, kc, nb * 512:(nb + 1) * 512], start=(kc == 0), stop=(kc == 7))
                    stt(xa[:, nb * 512:(nb + 1) * 512], xa[:, nb * 512:(nb + 1) * 512], ALPHA, bank[:, :], ALU.mult, ALU.add)
                    op("dve", (lambda nb_: (lambda e: e.bn_stats(out=stB[:, 32 + 6 * nb_:38 + 6 * nb_], in_=xa[:, nb_ * 512:(nb_ + 1) * 512])))(nb),
                       outs=[stB[:, 32 + 6 * nb:38 + 6 * nb]], ins=[xa[:, nb * 512:(nb + 1) * 512]])
                op("dve", lambda e: e.bn_aggr(out=stB[:, 56:58], in_=stB[:, 32:44]), outs=[stB[:, 56:58]], ins=[stB[:, 32:44]])
                ts("dve", stB[:, 58:59], stB[:, 57:58], LN_EPS, None, ALU.add)
                tt("pool", stB[:, 59:60], stB[:, 58:59], mhalf[:, 0:1], ALU.pow)
                stt(stB[:, 60:61], stB[:, 56:57], -1.0, stB[:, 59:60], ALU.mult, ALU.mult)
                act(xa[:], xa[:], AF.Identity, scale=stB[:, 59:60], bias=stB[:, 60:61])
                tt("dve", xa[:, 0:512], xa[:, 0:512], lng[:, 0:512], ALU.mult)
                tt("pool", xa[:, 512:1024], xa[:, 512:1024], lng[:, 512:1024], ALU.mult)
                tt("dve", xa[:, 0:512], xa[:, 0:512], lnb[:, 0:512], ALU.add)
                tt("pool", xa[:, 512:1024], xa[:, 512:1024], lnb[:, 512:1024], ALU.add)
                dma(y_d[tok, :], xa[:], "out")
            steps.append(s_out)
            return steps

        for it in range(NT + 2):
            lists = []
            if it < NT:
                lists.append(front(it))
            if 0 <= it - 1 < NT:
                lists.append(inv(it - 1))
            if 0 <= it - 2 < NT:
                lists.append(chain_out(it - 2))
            idx = [0] * len(lists)
            while True:
                best, bf = None, None
                for li, L in enumerate(lists):
                    if idx[li] < len(L):
                        frac = idx[li] / len(L)
                        if bf is None or frac < bf:
                            best, bf = li, frac
                if best is None:
                    break
                lists[best][idx[best]]()
                idx[best] += 1

        out_nodes = [i for i, nd in enumerate(P.nodes) if nd.dma is not None and (nd.dma.startswith("out") or nd.dma.startswith("dbg"))]
        order = P.schedule()
        P.emit(order)
        last = {}
        for i in out_nodes:
            last[P.nodes[i].dma] = i
        for i in last.values():
            P.wait_node("sp", i)

        block = es.enter_context(nc.Block())

        @block.tensor
        def _(e):
            P.replay("pe", e)

        @block.scalar
        def _(e):
            P.replay("act", e)

        @block.vector
        def _(e):
            P.replay("dve", e)

        @block.gpsimd
        def _(e):
            P.replay("pool", e)

        @block.sync
        def _(e):
            P.replay("sp", e)
    return nc


def make_in_maps(inputs, n_cores=8):
    f = lambda a: np.ascontiguousarray(np.asarray(a, dtype=np.float32))
    x = f(inputs["x"])
    c = f(inputs["c"])
    b_ada = f(inputs["b_ada"])
    shared = {
        "w_ada": f(inputs["w_ada"]),
        "b_ada_row": f(b_ada.reshape(1, 3072)),
        "b_ada_g": f(b_ada[2048:3072].reshape(1, 1024)),
        "w_in": f(inputs["w_in"]),
        "mu_b": f(inputs["mu_b"]),
        "ln_v_g": f(inputs["ln_v_g"]).reshape(512),
        "ln_v_b": f(inputs["ln_v_b"]).reshape(512),
        "w_spT": f(np.transpose(f(inputs["w_spatial"]), (0, 2, 1))),
        "b_sp": f(f(inputs["b_spatial"]).T),
        "w0a0": f(np.concatenate([f(inputs["w0"]), f(inputs["a0"])]).reshape(1, 1024)),
        "wua": f(np.concatenate([f(inputs["w_up"]), f(inputs["a_up"])], axis=0)),
        "k_k": f(inputs["k_k"]),
        "k_a": f(inputs["k_a"]),
        "r_k": f(inputs["r_k"]).reshape(512),
        "gn_g": f(inputs["gn_g"]).reshape(512),
        "gn_b": f(inputs["gn_b"]).reshape(512),
        "w_out": f(inputs["w_out"]),
        "ln_g": f(inputs["ln_g"]),
        "ln_b": f(inputs["ln_b"]),
    }
    maps = []
    for b in range(n_cores):
        m = dict(shared)
        m["x"] = f(x[b])
        m["xT"] = f(x[b].T)
        m["cT"] = f(c[b].reshape(8, 128).T)
        maps.append(m)
    return maps


def kernel(**inputs):
    maps = make_in_maps(inputs, 8)
    nc = build(NT=32)
    res = run_bass_kernel_spmd(nc, maps, core_ids=list(range(8)))
    out = np.stack([np.asarray(r["y"], dtype=np.float32) for r in res.results], axis=0)
    return out
```

```python
import contextlib
import numpy as np
import concourse.bass as bass
import concourse.mybir as mybir
from concourse.bass_utils import run_bass_kernel_spmd

F32 = mybir.dt.float32
BF16 = mybir.dt.bfloat16
AF = mybir.ActivationFunctionType
ALU = mybir.AluOpType
AX = mybir.AxisListType

D = 1024
T = 4096
NCOL = 3712
NB0 = 1536
LN_EPS = 1e-5
GN_EPS = 64e-5
ALPHA = 2.0 ** 0.25
NEGC = -0.6065306597126334
SEM_CAP = 30000


class Buf:
    def __init__(self, t, name):
        self.t = t
        self.name = name
        self.w = [None, None]
        self.r = [[], []]
        self.psum = False

    def __getitem__(self, idx):
        return self.t[idx]


class Node:
    __slots__ = ("eng", "fn", "dma", "deps", "dur", "start", "end", "tok", "users", "ndep", "est")

    def __init__(self, eng, fn, dma, deps, dur):
        self.eng = eng
        self.fn = fn
        self.dma = dma
        self.deps = deps
        self.dur = dur
        self.start = None
        self.end = None
        self.tok = None
        self.users = []
        self.ndep = 0
        self.est = 0.0


import os
HOP_NS = float(os.environ.get("K_HOP", "900"))
SAME_NS = float(os.environ.get("K_SAME", "120"))
PE_SCALE = float(os.environ.get("K_PESCALE", "1.0"))


class Prog:
    def __init__(self, nc, es):
        self.nc = nc
        self.es = es
        self.names = ["pe", "act", "dve", "pool", "sp"]
        self.prog = {k: [] for k in self.names}
        self.nsem = 0
        self.dsems = {}
        self.bufs = {}
        self.nodes = []
        self.nops = 0

    def dsem(self, name):
        if name not in self.dsems:
            s = self.es.enter_context(self.nc.semaphore("d_" + name))
            self.dsems[name] = [s, "d_" + name, 0]
        return self.dsems[name]

    def sb(self, name, shape, dt=F32):
        t = self.es.enter_context(self.nc.sbuf_tensor(name, shape, dt))
        b = Buf(t, name)
        self.bufs[name] = b
        return b

    def ps(self, name, shape, dt=F32):
        t = self.es.enter_context(self.nc.psum_tensor(name, shape, dt))
        b = Buf(t, name)
        b.psum = True
        self.bufs[name] = b
        return b

    def _halves(self, ap):
        p0 = ap.start_partition() if callable(ap.start_partition) else ap.start_partition
        n = ap.partition_size() if callable(ap.partition_size) else ap.partition_size
        hs = []
        if p0 < 64:
            hs.append(0)
        if p0 + n > 64:
            hs.append(1)
        return hs

    def _regions(self, aps):
        out = []
        for ap in aps:
            nm = ap.tensor.name
            b = self.bufs.get(nm)
            if b is None:
                continue
            for h in self._halves(ap):
                out.append((b, h))
        return out

    def op(self, eng, fn, outs=(), ins=(), dma=None, dur=300.0):
        self.nops += 1
        if self.nops > getattr(self, "max_ops", 10 ** 9):
            return None
        rd = self._regions(ins)
        wr = self._regions(outs)
        deps = set()
        for (b, h) in rd:
            if b.w[h] is not None:
                deps.add(b.w[h])
            if b.psum:
                for t in b.r[h]:
                    if self.nodes[t].eng != eng:
                        deps.add(t)
        for (b, h) in wr:
            if b.w[h] is not None:
                deps.add(b.w[h])
            for t in b.r[h]:
                deps.add(t)
        nid = len(self.nodes)
        self.nodes.append(Node(eng, fn, dma, sorted(deps), float(dur)))
        for (b, h) in rd:
            b.r[h].append(nid)
        for (b, h) in wr:
            b.w[h] = nid
            b.r[h] = []
        return nid

    def schedule(self):
        nodes = self.nodes
        for i, nd in enumerate(nodes):
            nd.ndep = len(nd.deps)
            for d in nd.deps:
                nodes[d].users.append(i)
        free = {k: 0.0 for k in self.names}
        cands = {k: [] for k in self.names}
        order = {k: [] for k in self.names}
        for i, nd in enumerate(nodes):
            if nd.ndep == 0:
                cands[nd.eng].append(i)
        remaining = len(nodes)
        while remaining:
            best = None
            for k in self.names:
                cl = cands[k]
                if not cl:
                    continue
                f = free[k]
                pick = None
                for i in cl:
                    e = nodes[i].est
                    if e <= f:
                        if pick is None or pick[0] > f or i < pick[1]:
                            pick = (f, i) if (pick is None or pick[0] > f or i < pick[1]) else pick
                    else:
                        if pick is None or e < pick[0]:
                            pick = (e, i)
                if best is None or pick[0] < best[0] or (pick[0] == best[0] and pick[1] < best[1]):
                    best = (pick[0], pick[1], k)
            st, i, k = best
            nd = nodes[i]
            cands[k].remove(i)
            nd.start = st
            if nd.dma is not None:
                nd.end = st + nd.dur
                free[k] = st + 60.0
            else:
                nd.end = st + nd.dur
                free[k] = nd.end
            order[k].append(i)
            remaining -= 1
            for u in nd.users:
                un = nodes[u]
                lat = SAME_NS if (un.eng == k and nd.dma is None) else HOP_NS
                if un.eng == "pe" and k == "pe":
                    lat = 0.0
                t = nd.end + lat
                if t > un.est:
                    un.est = t
                un.ndep -= 1
                if un.ndep == 0:
                    cands[un.eng].append(u)
        self.makespan = max(nd.end for nd in nodes) if nodes else 0.0
        return order

    def emit(self, order):
        nodes = self.nodes
        sem = {}
        cnt = {}
        waited = {k: {} for k in self.names}

        def new_sem(k):
            self.nsem += 1
            nm = "s_%s_%d" % (k, self.nsem)
            s_ = self.es.enter_context(self.nc.semaphore(nm))
            sem[k] = (s_, nm)
            cnt[k] = 0
        for k in self.names:
            new_sem(k)
        allids = sorted(range(len(nodes)), key=lambda i: (nodes[i].start, i))
        pos = {k: 0 for k in self.names}
        for i in allids:
            nd = nodes[i]
            k = nd.eng
            assert order[k][pos[k]] == i
            pos[k] += 1
            for d in nd.deps:
                dn = nodes[d]
                if dn.eng == "pe" and k == "pe" and dn.dma is None:
                    continue
                semh, semn, val = dn.tok
                if waited[k].get(semn, 0) >= val:
                    continue
                waited[k][semn] = val
                self.prog[k].append(("wait", semh, val))
            if nd.dma is not None:
                ds = self.dsem(nd.dma)
                ds[2] += 16
                nd.tok = (ds[0], ds[1], ds[2])
                self.prog[k].append(("op", nd.fn, ds[0], 16))
            else:
                if cnt[k] >= SEM_CAP:
                    new_sem(k)
                cnt[k] += 1
                nd.tok = (sem[k][0], sem[k][1], cnt[k])
                self.prog[k].append(("op", nd.fn, sem[k][0], 1))
        self.waited = waited

    def wait_node(self, eng, nid):
        semh, semn, val = self.nodes[nid].tok
        if self.waited[eng].get(semn, 0) >= val:
            return
        self.waited[eng][semn] = val
        self.prog[eng].append(("wait", semh, val))

    def replay(self, eng, e):
        for item in self.prog[eng]:
            if item[0] == "wait":
                e.wait_ge(item[1], item[2])
            else:
                item[1](e).then_inc(item[2], item[3])


def build(NT=32, dbg=None, max_ops=None):
    nc = bass.Bass("TRN2", target_bir_lowering=False)
    dram = {}

    def din(name, shape):
        dram[name] = nc.dram_tensor(name, list(shape), F32, kind="ExternalInput").ap()
        return dram[name]

    xT_d = din("xT", [D, T])
    x_d = din("x", [T, D])
    cT_d = din("cT", [128, 8])
    wada_d = din("w_ada", [D, 3 * D])
    bada_row_d = din("b_ada_row", [1, 3 * D])
    bada_g_d = din("b_ada_g", [1, D])
    win_d = din("w_in", [D, NCOL])
    mu_d = din("mu_b", [2176])
    lnvg_d = din("ln_v_g", [512])
    lnvb_d = din("ln_v_b", [512])
    wsT_d = din("w_spT", [8, 128, 128])
    bsp_d = din("b_sp", [128, 8])
    w0a0_d = din("w0a0", [1, 1024])
    wua_d = din("wua", [128, 512])
    kk_d = din("k_k", [512])
    ka_d = din("k_a", [512])
    rk_d = din("r_k", [512])
    gng_d = din("gn_g", [512])
    gnb_d = din("gn_b", [512])
    wout_d = din("w_out", [D, D])
    lng_d = din("ln_g", [D])
    lnb_d = din("ln_b", [D])
    y_d = nc.dram_tensor("y", [T, D], F32, kind="ExternalOutput").ap()
    dbg_d = None
    if dbg:
        dbg_d = nc.dram_tensor("dbg", [128, 16384], F32, kind="ExternalOutput").ap()

    es = contextlib.ExitStack()
    with es:
        P = Prog(nc, es)
        if max_ops is not None:
            P.max_ops = max_ops
        sb, ps, op = P.sb, P.ps, P.op

        w_in_c = [sb("w_in_c%d" % j, [128, 8, min(512, NCOL - j * 512)], BF16) for j in range(8)]
        w_out_b = sb("w_out_b", [128, 8, D], BF16)
        wsT_b = sb("wsT_b", [128, 8, 128], BF16)
        wua_b = sb("wua_b", [128, 512], BF16)
        identb = sb("identb", [128, 128], BF16)
        identf = sb("identf", [128, 128])
        ones_f = sb("ones_f", [128, 128])
        tri = sb("tri", [128, 3, 128])
        maskNN = sb("maskNN", [128, 4, 128], BF16)
        maskS8 = sb("maskS8", [128, 8, 64], BF16)
        maskI8 = sb("maskI8", [128, 8, 64], BF16)
        I8 = sb("I8", [128, 8, 64], BF16)
        mu = sb("mu", [128, 2176])
        lnvg = sb("lnvg", [128, 512])
        lnvb = sb("lnvb", [128, 512])
        kk_ = sb("kk_", [128, 512])
        ka_ = sb("ka_", [128, 512])
        rk_ = sb("rk_", [128, 512])
        gng = sb("gng", [128, 512])
        gnb = sb("gnb", [128, 512])
        lng = sb("lng", [128, D])
        lnb = sb("lnb", [128, D])
        rows = sb("rows", [128, 512])
        small = sb("small", [128, 64])
        mhalf = sb("mhalf", [128, 8])
        XT = sb("XT", [128, 4, 128])
        hT = [sb("hT%d" % i, [128, 8, 130], BF16) for i in range(2)]
        hd = sb("hd", [128, 8, 128], BF16)
        XA = [sb("XA0", [128, D])] * 2
        U_sb = sb("U_sb", [128, 512], BF16)
        SG_a = sb("SG_a", [128, 512], BF16)
        SG_b = [sb("SG_b%d" % i, [128, 512], BF16) for i in range(3)]
        R_b = sb("R_b", [128, 512], BF16)
        V_g = sb("V_g", [128, 512])
        VN = sb("VN", [128, 512], BF16)
        PBt = sb("PBt", [128, 2176])
        SIGW = sb("SIGW", [128, 512])
        A_sig = sb("A_sig", [128, 512], BF16)
        NA = sb("NA", [128, 512], BF16)
        BB = sb("BB", [128, 512], BF16)
        KM = sb("KM", [128, 512], BF16)
        S1 = sb("S1", [128, 512])
        S2 = sb("S2", [128, 512])
        S3 = sb("S3", [128, 512])
        TA = sb("TA", [128, 512], BF16)
        TB = sb("TB", [128, 512], BF16)
        BH = [sb("BH%d" % i, [128, 512], BF16) for i in range(2)]
        KH = [sb("KH%d" % i, [128, 512], BF16) for i in range(2)]
        VB = [sb("VB%d" % i, [128, 512], BF16) for i in range(3)]
        WA_T = sb("WA_T", [128, 128], BF16)
        FM2 = [sb("FM2_%d" % i, [128, 4, 8, 64], BF16) for i in range(2)]
        X0 = sb("X0", [128, 8, 192], BF16)
        Xa = sb("Xa", [128, 8, 192], BF16)
        ABR2 = [sb("ABR%d" % i, [128, 8, 64], BF16) for i in range(2)]
        AAK2 = [sb("AAK%d" % i, [128, 8, 64], BF16) for i in range(2)]
        AKR2 = [sb("AKR%d" % i, [128, 8, 64], BF16) for i in range(2)]
        EB2 = [sb("EB%d" % i, [128, 8, 64], BF16) for i in range(2)]
        Tm2 = [sb("Tm%d" % i, [128, 8, 64], BF16) for i in range(2)]
        WTs = sb("WTs", [128, 512], BF16)
        onesb_t = WTs
        UTs = sb("UTs", [128, 512], BF16)
        ST = sb("ST", [128, 8, 64])
        Dg = sb("Dg", [128, 8, 64])
        STb = sb("STb", [128, 8, 64], BF16)
        GC = [sb("GC%d" % i, [128, 16]) for i in range(3)]
        YS = sb("YS", [128, 512])
        CATa = [sb("CATa%d" % i, [128, 512], BF16) for i in range(3)]
        CATb = sb("CATb", [128, 512], BF16)
        CATT = sb("CATT", [128, 8, 128], BF16)
        st = sb("st", [128, 64])
        stB = sb("stB", [128, 64])

        PB0 = ps("PB0", [128, 512])
        PB1 = ps("PB1", [128, 512])
        PM = ps("PM", [128, 512])
        PY = ps("PY", [128, 512])
        PX0 = ps("PX0", [128, 512])
        PX1 = ps("PX1", [128, 512])
        PP = ps("PP", [128, 512])
        PC = ps("PC", [128, 512])
        PBIG = [PB0, PB1]
        big_i = [0]

        def nextbig():
            b = PBIG[big_i[0] % 2]
            big_i[0] += 1
            return b

        def fsz(ap):
            n_ = 1
            for d_ in ap.shape[1:]:
                n_ *= int(d_)
            return n_

        def mm(out, lhsT, rhs, start=True, stop=True, tp=None):
            n_ = fsz(rhs)
            d_ = 55.0 if n_ <= 64 else 0.55 * n_ + 20
            if lhsT.dtype == F32:
                d_ = 4 * d_ + 100
            d_ *= PE_SCALE
            if tp is None:
                op("pe", lambda e: e.matmul(out, lhsT=lhsT, rhs=rhs, start=start, stop=stop),
                   outs=[out], ins=[lhsT, rhs], dur=d_)
            else:
                op("pe", lambda e: e.matmul(out, lhsT=lhsT, rhs=rhs, start=start, stop=stop, tile_position=tp),
                   outs=[out], ins=[lhsT, rhs], dur=d_)

        def tr(out, in_, ident, tp=None):
            if tp is None:
                op("pe", lambda e: e.transpose(out=out, in_=in_, identity=ident), outs=[out], ins=[in_, ident], dur=100.0)
            else:
                op("pe", lambda e: e.transpose(out=out, in_=in_, identity=ident, tile_position=tp), outs=[out], ins=[in_, ident], dur=55.0)

        def act(out, in_, func, scale=None, bias=None):
            kw = {}
            if scale is not None:
                kw["scale"] = scale
            if bias is not None:
                kw["bias"] = bias
            ins = [in_] + [a for a in (scale, bias) if hasattr(a, "tensor")]
            op("act", lambda e: e.activation(out=out, in_=in_, func=func, **kw), outs=[out], ins=ins,
               dur=240.0 + 0.7 * fsz(out) + (200.0 if len(ins) > 1 else 0.0))

        def tt(eng, out, in0, in1, o):
            d_ = (110.0 + 1.1 * fsz(out)) if eng == "dve" else (300.0 + 1.75 * fsz(out))
            if o == ALU.pow:
                d_ = 1400.0
            op(eng, lambda e: e.tensor_tensor(out=out, in0=in0, in1=in1, op=o), outs=[out], ins=[in0, in1], dur=d_)

        def ts(eng, out, in0, s1, s2, o0, o1=None):
            ins = [in0] + [a for a in (s1, s2) if hasattr(a, "tensor")]
            d_ = (110.0 + 1.1 * fsz(out)) if eng == "dve" else (300.0 + 1.75 * fsz(out))
            if o1 is None:
                op(eng, lambda e: e.tensor_scalar(out=out, in0=in0, scalar1=s1, scalar2=None, op0=o0),
                   outs=[out], ins=ins, dur=d_)
            else:
                op(eng, lambda e: e.tensor_scalar(out=out, in0=in0, scalar1=s1, scalar2=s2, op0=o0, op1=o1),
                   outs=[out], ins=ins, dur=d_)

        def stt(out, in0, s, in1, o0, o1):
            ins = [in0, in1] + ([s] if hasattr(s, "tensor") else [])
            op("dve", lambda e: e.scalar_tensor_tensor(out=out, in0=in0, scalar=s, in1=in1, op0=o0, op1=o1),
               outs=[out], ins=ins, dur=110.0 + 1.1 * fsz(out))

        def red(out, in_):
            op("dve", lambda e: e.tensor_reduce(out=out, in_=in_, axis=AX.X, op=ALU.add), outs=[out], ins=[in_],
               dur=110.0 + 1.1 * fsz(in_))

        def cp(eng, out, in_):
            if eng == "act":
                op("act", lambda e: e.activation(out=out, in_=in_, func=AF.Copy), outs=[out], ins=[in_],
                   dur=240.0 + 0.7 * fsz(out))
            else:
                op(eng, lambda e: e.tensor_copy(out=out, in_=in_), outs=[out], ins=[in_],
                   dur=(110.0 + 1.1 * fsz(out)) if eng == "dve" else (300.0 + 1.75 * fsz(out)))

        def dma(out, in_, sem):
            op("sp", lambda e: e.dma_start(out=out, in_=in_), outs=[out], ins=[in_], dma=sem,
               dur=2000.0 + 0.004 * 4 * fsz(out) * 128)

        def memset(eng, ap, v):
            op(eng, lambda e: e.memset(ap, v), outs=[ap], dur=300.0)

        def asel(out, in_, pattern, cmp, base, cm):
            op("pool", lambda e: e.affine_select(out=out, in_=in_, pattern=pattern, compare_op=cmp, fill=0.0,
                                                 base=base, channel_multiplier=cm), outs=[out], ins=[in_],
               dur=250.0 + 0.9 * fsz(out))

        dbg_off = [0]

        def dump(name, ap, ncols):
            if not dbg:
                return
            o = dbg_off[0]
            n = ap.partition_size() if callable(ap.partition_size) else ap.partition_size
            p0 = ap.start_partition() if callable(ap.start_partition) else ap.start_partition
            dbg[name] = (o, p0, n, ncols)
            dma(dbg_d[p0:p0 + n, o:o + ncols], ap, "dbg_" + name)
            dbg_off[0] += ncols

        memset("pool", ones_f[:], 1.0)
        memset("pool", WTs[:], 1.0)
        memset("pool", tri[:], 0.0)
        memset("pool", mhalf[:], -0.5)
        memset("pool", hT[1][:], 0.0)
        memset("pool", ST[:], 0.0)
        memset("pool", STb[:], 0.0)
        memset("pool", S3[:], NEGC)
        asel(identf[:], ones_f[:], [[-1, 128]], ALU.is_equal, 0, 1)
        cp("pool", identb[:], identf[:])
        for c in range(2):
            psl = slice(c * 64, (c + 1) * 64)
            asel(tri[psl, 0, c * 64:(c + 1) * 64], S3[psl, 0:64], [[1, 64]], ALU.is_ge, 0, -1)
            asel(tri[psl, 1, c * 64:(c + 1) * 64], S3[psl, 0:64], [[1, 64]], ALU.is_gt, 0, -1)
            asel(tri[psl, 2, c * 64:(c + 1) * 64], S3[psl, 0:64], [[-1, 64]], ALU.is_gt, 0, 1)
            asel(maskS8[psl, :, :], WTs[psl, :].rearrange("p (h t) -> p h t", h=8), [[0, 8], [1, 64]], ALU.is_gt, 0, -1)
            asel(maskI8[psl, :, :], WTs[psl, :].rearrange("p (h t) -> p h t", h=8), [[0, 8], [1, 64]], ALU.is_ge, 0, -1)
            asel(I8[psl, :, :], WTs[psl, :].rearrange("p (h t) -> p h t", h=8), [[0, 8], [-1, 64]], ALU.is_equal, 0, 1)
            asel(maskNN[psl, :, 0:64], WTs[psl, 0:256].rearrange("p (h t) -> p h t", h=4), [[0, 4], [-1, 64]], ALU.is_gt, 0, 1)
            asel(maskNN[psl, :, 64:128], WTs[psl, 0:256].rearrange("p (h t) -> p h t", h=4), [[0, 4], [1, 64]], ALU.is_gt, 0, -1)

        for (tile_, src, n) in [(mu, mu_d, 2176), (lnvg, lnvg_d, 512), (lnvb, lnvb_d, 512), (kk_, kk_d, 512),
                                (ka_, ka_d, 512), (rk_, rk_d, 512), (gng, gng_d, 512), (gnb, gnb_d, 512),
                                (lng, lng_d, D), (lnb, lnb_d, D)]:
            dma(tile_[:], src.partition_broadcast(128), "par_" + tile_.name)
        dma(small[:, 8:16], bsp_d[:, :], "par_bsp")
        dma(rows[0:1, :], bada_g_d[:, 0:512], "par_rows0")
        dma(rows[32:33, :], bada_g_d[:, 512:1024], "par_rows32")
        dma(small[:, 0:8], cT_d[:, :], "par_c")
        act(small[:, 24:32], small[:, 0:8], AF.Silu)

        for g in range(8):
            stg = XA[0]
            dma(stg[:, 0:128], wsT_d[g, :, :], "stgw")
            asel(stg[:, 128:256], stg[:, 0:128], [[1, 128]], ALU.is_ge, 0, -1)
            cp("pool", wsT_b[:, g, :], stg[:, 128:256])
        stg = XA[0]
        dma(stg[:, 0:512], wua_d[:, :], "stgw")
        cp("dve", wua_b[:], stg[:, 0:512])

        slots = [S1[:, :], S2[:, :], V_g[:, :], YS[:, :], SIGW[:, :], S3[:, :]]
        si = [0]
        ring = [list(range(6))]

        def stage(src_ap, w=512):
            r_ = ring[0]
            i = r_[si[0] % len(r_)]
            si[0] += 1
            v = slots[i][:, 0:w]
            dma(v, src_ap, "stg%d" % i)
            return v

        mod_banks = [PX0, PX1, PP, PC, PB0, PB1]
        dma(PBt[0:1, 0:2048], bada_row_d[:, 0:2048], "par_badarow")

        def mod_chunk(j):
            po = 32 if j == 5 else 0
            for kc in range(8):
                v = stage(wada_d[kc * 128:(kc + 1) * 128, j * 512:(j + 1) * 512])
                if po:
                    mm(mod_banks[j][po:po + 1, :], small[:, 24 + kc:25 + kc], v, start=(kc == 0), stop=(kc == 7), tp=(0, po))
                else:
                    mm(mod_banks[j][0:1, :], small[:, 24 + kc:25 + kc], v, start=(kc == 0), stop=(kc == 7))
        for j in range(4):
            mod_chunk(j)
            tt("dve", PBt[0:1, j * 512:(j + 1) * 512], mod_banks[j][0:1, :], PBt[0:1, j * 512:(j + 1) * 512], ALU.add)
        for j in range(16):
            mm(PM[:, j:j + 1], PBt[0:1, j * 128:(j + 1) * 128], ones_f[0:1, 0:1])
        cp("dve", small[:, 32:48], PM[:, 0:16])
        ts("dve", small[:, 40:48], small[:, 40:48], 1.0, None, ALU.add)
        ci = 0
        for j in range(8):
            c0 = j * 512
            w = min(512, NCOL - c0)
            for kc in range(8):
                v = stage(win_d[kc * 128:(kc + 1) * 128, c0:c0 + w], w)
                cp(["act", "dve", "pool"][ci % 3], w_in_c[j][:, kc, 0:w], v)
                ci += 1
        ring[0] = [3, 5]
        for j in (4, 5):
            mod_chunk(j)
        tt("dve", rows[0:1, 0:512], PB0[0:1, :], rows[0:1, 0:512], ALU.add)
        tt("dve", rows[32:33, 0:512], PB1[32:33, :], rows[32:33, 0:512], ALU.add)
        for n2 in range(2):
            bank = [PX0, PX1][n2]
            mm(bank[:, :], ones_f[32 * n2:32 * n2 + 1, 0:128], rows[32 * n2:32 * n2 + 1, 0:512])
            cp("act", XA[0][:, n2 * 512:(n2 + 1) * 512], bank[:, :])
        for nb in range(2):
            for kc in range(8):
                v = stage(wout_d[kc * 128:(kc + 1) * 128, nb * 512:(nb + 1) * 512])
                tt(["dve", "pool"][ci % 2], w_out_b[:, kc, nb * 512:(nb + 1) * 512], v,
                   XA[0][:, nb * 512:(nb + 1) * 512], ALU.mult)
                ci += 1
        dma(rows[0:1, 0:512], w0a0_d[:, 0:512], "par_rows0")
        dma(rows[64:65, 0:512], w0a0_d[:, 512:1024], "par_rows64")

        xT_v = xT_d.rearrange("(k p) t -> p k t", p=128)

        PMb = PM[:, :].bitcast(BF16)

        def front(n):
            steps = []
            par = n % 2
            h_ = hT[par]
            tok = slice(n * 128, (n + 1) * 128)
            FMp, VBp, BHp, KHp, SGp = FM2[par], VB[n % 3], BH[par], KH[par], SG_b[n % 3]
            GCp, CAp = GC[n % 3], CATa[n % 3]

            def s_load():
                for kc in range(8):
                    dma(XT[:, kc % 4, :], xT_v[:, kc, tok], "xT%d" % (kc % 4))
                    ts("dve", h_[:, kc, 2:130], XT[:, kc % 4, :], small[:, 40 + kc:41 + kc], small[:, 32 + kc:33 + kc],
                       ALU.mult, ALU.add)
                cp("dve", h_[:, :, 1:2], hT[1 - par][:, :, 129:130])
                tt("dve", hd[:], h_[:, :, 1:129], h_[:, :, 2:130], ALU.subtract)
            steps.append(s_load)

            def proj(j, src=None):
                c0 = j * 512
                w = min(512, NCOL - c0)
                bank = nextbig()
                for kc in range(8):
                    lh = h_[:, kc, 2:130] if src is None else src[:, kc, :]
                    mm(bank[:, 0:w], lh, w_in_c[j][:, kc, 0:w], start=(kc == 0), stop=(kc == 7))
                return bank, w

            def s_u():
                bank, w = proj(0)
                act(U_sb[:], bank[:, :], AF.Gelu_apprx_tanh)
            steps.append(s_u)

            def s_v():
                bank, w = proj(1)
                act(V_g[:], bank[:, :], AF.Gelu_apprx_tanh)
            steps.append(s_v)

            def s_ga():
                bank, w = proj(2)
                act(SG_a[:], bank[:, :], AF.Silu)
            steps.append(s_ga)

            def stats(stt_, src3, sq, sq3, eps):
                red(stt_[:, 0:8], src3)
                tt("pool", sq, src3.rearrange("p g d -> p (g d)") if False else sq_src[0], sq_src[0], ALU.mult)
                red(stt_[:, 8:16], sq3)
                ts("dve", stt_[:, 0:8], stt_[:, 0:8], 1.0 / 64, None, ALU.mult)
                tt("dve", stt_[:, 16:24], stt_[:, 0:8], stt_[:, 0:8], ALU.mult)
                ts("dve", stt_[:, 8:16], stt_[:, 8:16], 1.0 / 64, eps, ALU.mult, ALU.add)
                tt("dve", stt_[:, 8:16], stt_[:, 8:16], stt_[:, 16:24], ALU.subtract)
                tt("pool", stt_[:, 24:32], stt_[:, 8:16], mhalf[:], ALU.pow)
            sq_src = [None]

            def s_ln():
                Vg3 = V_g[:].rearrange("p (g d) -> p g d", g=8)
                S13 = S1[:].rearrange("p (g d) -> p g d", g=8)
                sq_src[0] = V_g[:]
                stats(st, Vg3, S1[:], S13, LN_EPS)
                mb = st[:, 0:8].unsqueeze(2).broadcast_to([128, 8, 64])
                rb = st[:, 24:32].unsqueeze(2).broadcast_to([128, 8, 64])
                tt("dve", S13, Vg3, mb, ALU.subtract)
                tt("dve", S13, S13, rb, ALU.mult)
                tt("pool", S1[:], S1[:], lnvg[:], ALU.mult)
                tt("pool", VN[:], S1[:], lnvb[:], ALU.add)
            steps.append(s_ln)

            def s_sp():
                tt("pool", U_sb[:], U_sb[:], SG_a[:], ALU.mult)
                for g in range(8):
                    mm(PM[:, g * 64:(g + 1) * 64], wsT_b[:, g, :], VN[:, g * 64:(g + 1) * 64], start=True, stop=True)
                tt("dve", S1[:].rearrange("p (g d) -> p g d", g=8), PM[:, :].rearrange("p (g d) -> p g d", g=8),
                   small[:, 8:16].unsqueeze(2).broadcast_to([128, 8, 64]), ALU.add)
                tt("dve", CAp[:], S1[:], U_sb[:], ALU.mult)
                if dbg and n == dbg.get("_tile", 0):
                    dump("out_a", S1[:], 512)
            steps.append(s_sp)

            def mk_b(j):
                def s_b():
                    c0 = j * 512
                    bank_d, w = proj(3 + j, src=hd)
                    tt("dve", PBt[:, c0:c0 + w], bank_d[:, 0:w], mu[:, c0:c0 + w], ALU.mult)
                    bank_p, w = proj(3 + j)
                    tt("dve", PBt[:, c0:c0 + w], bank_p[:, 0:w], PBt[:, c0:c0 + w], ALU.add)
                return s_b
            for j in range(5):
                steps.append(mk_b(j))

            r_ = PBt[:, 0:512]
            k_ = PBt[:, 512:1024]
            v_ = PBt[:, 1024:1536]
            g_ = PBt[:, 1536:2048]

            def s_wa():
                if dbg and n == dbg.get("_tile", 0):
                    dump("pb", PBt[:, :], 2176)
                act(SGp[:], g_, AF.Silu)
                cp("act", VBp[:], v_)
                cp("act", R_b[:], r_)
                tr(PM[:, 0:128], PBt[:, 2048:2176], identf[:])
                act(WA_T[0:64, :], PM[0:64, 0:128], AF.Tanh)
                cp("act", WA_T[64:128, :], PM[64:128, 0:128])
                mm(PM[:, :], WA_T[0:64, :], wua_b[0:64, :], start=True, stop=False)
                mm(PM[:, :], ones_f[0:1, 0:128], rows[0:1, 0:512], start=False, stop=True)
                act(SIGW[:], PM[:, :], AF.Sigmoid)
                mm(PY[:, :], WA_T[64:128, :], wua_b[64:128, :], start=True, stop=False)
                mm(PY[:, :], ones_f[64:65, 0:128], rows[64:65, 0:512], start=False, stop=True)
                act(A_sig[:], PY[:, :], AF.Sigmoid)
            steps.append(s_wa)

            def s_kk():
                S13 = S1[:].rearrange("p (g d) -> p g d", g=8)
                S23 = S2[:].rearrange("p (g d) -> p g d", g=8)
                tt("dve", S1[:], k_, kk_[:], ALU.mult)
                tt("dve", S2[:], S1[:], S1[:], ALU.mult)
                red(st[:, 32:40], S23)
                ts("dve", st[:, 32:40], st[:, 32:40], 1e-12, None, ALU.add)
                tt("pool", st[:, 40:48], st[:, 32:40], mhalf[:], ALU.pow)
                tt("dve", NA[:].rearrange("p (g d) -> p g d", g=8), S13,
                   st[:, 40:48].unsqueeze(2).broadcast_to([128, 8, 64]), ALU.mult)
                tt("dve", BB[:], NA[:], A_sig[:], ALU.mult)
                stt(S2[:], A_sig[:], -1.0, ka_[:], ALU.add, ALU.mult)
                stt(KM[:], S2[:], 1.0, k_, ALU.add, ALU.mult)
                tt("pool", S2[:], R_b[:], KM[:], ALU.mult)
                tt("pool", S2[:], S2[:], rk_[:], ALU.mult)
                red(GCp[:, 8:16], S23)
            steps.append(s_kk)

            def fm_pass(pi):
                for i, kt in enumerate([TA, TB]):
                    for h in range(8):
                        for c in range(2):
                            cs = slice(c * 64, (c + 1) * 64)
                            o = (i * 8 + h) * 64
                            tr(PMb[cs, o:o + 64], kt[cs, h * 64:(h + 1) * 64], identb[cs, c * 64:(c + 1) * 64],
                               tp=(c * 64, c * 64))
                cp("act" if pi == 0 else "dve", FMp[:, 2 * pi:2 * pi + 2, :, :],
                   PMb.rearrange("p (a h t) -> p a h t", a=2, h=8))

            def s_cum():
                mm(PY[:, :], tri[:, 0, :], SIGW[:])
                act(S1[:], PY[:, :], AF.Exp)
                act(S2[:], PY[:, :], AF.Exp, scale=-1.0)
                tt("dve", TB[:], R_b[:], S1[:], ALU.mult)
                mm(PY[:, :], tri[:, 1, :], SIGW[:])
                act(S1[:], PY[:, :], AF.Exp)
                stt(TA[:], NA[:], -1.0, S1[:], ALU.mult, ALU.mult)
                fm_pass(0)
            steps.append(s_cum)

            def s_cum2():
                tt("dve", TA[:], BB[:], S2[:], ALU.mult)
                tt("dve", TB[:], KM[:], S2[:], ALU.mult)
                mm(PY[:, :], tri[:, 2, :], SIGW[:])
                act(S1[:], PY[:, :], AF.Exp)
                tt("dve", BHp[:], BB[:], S1[:], ALU.mult)
                tt("pool", KHp[:], KM[:], S1[:], ALU.mult)
                for h in range(8):
                    for c in range(2):
                        cs = slice(c * 64, (c + 1) * 64)
                        mm(PY[cs, h:h + 1], SIGW[cs, h * 64:(h + 1) * 64], ones_f[cs, 0:1], tp=(c * 64, c * 64))
                act(GCp[:, 0:8], PY[:, 0:8], AF.Exp, scale=NEGC)
                fm_pass(1)
            steps.append(s_cum2)
            return steps

        def inv(n):
            steps = []
            par = n % 2
            FMp = FM2[par]
            ABR, AAK, AKR, EB, Tm = ABR2[par], AAK2[par], AKR2[par], EB2[par], Tm2[par]

            def fm(kind, h, c):
                return FMp[c * 64:(c + 1) * 64, kind, h, :]

            def s_A():
                for h in range(8):
                    for c in range(2):
                        cs = slice(c * 64, (c + 1) * 64)
                        bank = PX0 if h < 4 else PX1
                        o = (h % 4) * 128
                        mm(bank[cs, o + 64:o + 128], fm(2, h, c), fm(0, h, c), tp=(c * 64, c * 64))
                        mm(bank[cs, o:o + 64], fm(0, h, c), fm(2, h, c), tp=(c * 64, c * 64))
                for h4, bank in enumerate([PX0, PX1]):
                    tt("dve", X0[:, h4 * 4:(h4 + 1) * 4, 0:128], bank[:, :].rearrange("p (h t) -> p h t", h=4),
                       maskNN[:], ALU.mult)
                tt("pool", EB[:], I8[:], X0[:, :, 0:64], ALU.subtract)
            steps.append(s_A)

            def s_A2():
                for (dst, ka, kb, msk) in [(ABR, 2, 1, maskI8), (AAK, 3, 0, maskS8), (AKR, 3, 1, maskI8)]:
                    for h in range(8):
                        for c in range(2):
                            cs = slice(c * 64, (c + 1) * 64)
                            mm(PP[cs, h * 64:(h + 1) * 64], fm(ka, h, c), fm(kb, h, c), tp=(c * 64, c * 64))
                    tt("dve", dst[:], PP[:, :].rearrange("p (h t) -> p h t", h=8), msk[:], ALU.mult)
            steps.append(s_A2)

            seq = [(X0, Xa), (Xa, X0), (X0, Xa), (Xa, X0), (X0, Xa), (Xa, None)]

            def mk_round(k):
                S_, D_ = seq[k]

                def s_r():
                    for h in range(8):
                        for c in range(2):
                            cs = slice(c * 64, (c + 1) * 64)
                            tp = (c * 64, c * 64)
                            bank = PX0 if h < 4 else PX1
                            o = (h % 4) * 128
                            if k == 0:
                                mm(bank[cs, o:o + 64], S_[cs, h, 0:64], S_[cs, h, 64:128], tp=tp)
                            elif k <= 3:
                                mm(bank[cs, o:o + 128], S_[cs, h, 0:64], S_[cs, h, 64:192], tp=tp)
                            else:
                                mm(bank[cs, o + 64:o + 128], S_[cs, h, 0:64], S_[cs, h, 128:192], tp=tp)
                            if k <= 4:
                                mm(PP[cs, h * 64:(h + 1) * 64], S_[cs, h, 64:128], S_[cs, h, 0:64], tp=tp)
                    for h4, bank in enumerate([PX0, PX1]):
                        hs_ = slice(h4 * 4, (h4 + 1) * 4)
                        bv = bank[:, :].rearrange("p (h t) -> p h t", h=4)
                        if k <= 3:
                            cp("act", D_[:, hs_, 64:128], bv[:, :, 0:64])
                        if k == 0:
                            tt("pool", D_[:, hs_, 128:192], S_[:, hs_, 64:128], I8[:, hs_, :], ALU.add)
                        elif k <= 4:
                            tt("dve", D_[:, hs_, 128:192], bv[:, :, 64:128], S_[:, hs_, 128:192], ALU.add)
                        else:
                            tt("dve", Tm[:, hs_, :], bv[:, :, 64:128], S_[:, hs_, 128:192], ALU.add)
                    if k <= 4:
                        cp("act", D_[:, :, 0:64], PP[:, :].rearrange("p (h t) -> p h t", h=8))
                return s_r
            for k in range(6):
                steps.append(mk_round(k))

            def s_E():
                for h in range(8):
                    for c in range(2):
                        cs = slice(c * 64, (c + 1) * 64)
                        mm(PP[cs, h * 64:(h + 1) * 64], EB[cs, h, :], Tm[cs, h, :], tp=(c * 64, c * 64))
                tt("dve", EB[:], I8[:], PP[:, :].rearrange("p (h t) -> p h t", h=8), ALU.subtract)
            steps.append(s_E)
            return steps

        def chain_out(n):
            steps = []
            par = n % 2
            xa = XA[par]
            tok = slice(n * 128, (n + 1) * 128)
            isdbg = dbg and n == dbg.get("_tile", 0)
            FMp, VBp, BHp, KHp, SGp = FM2[par], VB[n % 3], BH[par], KH[par], SG_b[n % 3]
            GCp, CAp = GC[n % 3], CATa[n % 3]
            ABR, AAK, AKR, EB, Tm = ABR2[par], AAK2[par], AKR2[par], EB2[par], Tm2[par]

            def fm(kind, h, c):
                return FMp[c * 64:(c + 1) * 64, kind, h, :]

            def s_pre():
                dma(xa[:], x_d[tok, :], "xa")
                for c in range(2):
                    cs = slice(c * 64, (c + 1) * 64)
                    asel(Dg[cs, :, :], GCp[cs, 0:8].unsqueeze(2).broadcast_to([64, 8, 64]), [[0, 8], [-1, 64]],
                         ALU.is_equal, 0, 1)
            steps.append(s_pre)

            def mk_chain(c):
                cs = slice(c * 64, (c + 1) * 64)
                os_ = slice((1 - c) * 64, (2 - c) * 64)
                tpc = (c * 64, c * 64)

                def s_w():
                    for h in range(8):
                        mm(PC[cs, h * 64:(h + 1) * 64], fm(0, h, c), STb[cs, h, :], start=True, stop=False, tp=tpc)
                        mm(PC[cs, h * 64:(h + 1) * 64], AAK[cs, h, :], VBp[cs, h * 64:(h + 1) * 64], start=False, stop=True, tp=tpc)
                    cp("act", WTs[cs, :], PC[cs, :])

                def s_u():
                    for h in range(8):
                        mm(PC[cs, h * 64:(h + 1) * 64], Tm[cs, h, :], WTs[cs, h * 64:(h + 1) * 64], tp=tpc)
                    cp("act", WTs[cs, :], PC[cs, :])
                    for h in range(8):
                        mm(PC[cs, h * 64:(h + 1) * 64], EB[cs, h, :], WTs[cs, h * 64:(h + 1) * 64], tp=tpc)
                    tt("dve", UTs[cs, :], PC[cs, :], WTs[cs, :], ALU.add)

                def s_s():
                    tps = (c * 64, (1 - c) * 64)
                    for h in range(8):
                        o = PC[os_, h * 64:(h + 1) * 64]
                        mm(o, Dg[cs, h, :], ST[cs, h, :], start=True, stop=False, tp=tps)
                        mm(o, BHp[cs, h * 64:(h + 1) * 64], UTs[cs, h * 64:(h + 1) * 64], start=False, stop=False, tp=tps)
                        mm(o, KHp[cs, h * 64:(h + 1) * 64], VBp[cs, h * 64:(h + 1) * 64], start=False, stop=True, tp=tps)
                    cp("dve", ST[os_, :, :], PC[os_, :].rearrange("p (h i) -> p h i", h=8))
                    cp("act", STb[os_, :, :], PC[os_, :].rearrange("p (h i) -> p h i", h=8))

                def s_y():
                    for h in range(8):
                        o = PY[cs, h * 64:(h + 1) * 64]
                        mm(o, fm(1, h, c), STb[cs, h, :], start=True, stop=False, tp=tpc)
                        mm(o, ABR[cs, h, :], UTs[cs, h * 64:(h + 1) * 64], start=False, stop=False, tp=tpc)
                        mm(o, AKR[cs, h, :], VBp[cs, h * 64:(h + 1) * 64], start=False, stop=True, tp=tpc)
                    cp("act", YS[cs, :], PY[cs, :])
                return [s_w, s_u, s_s, s_y]
            for c in range(2):
                steps.extend(mk_chain(c))

            def s_post():
                if isdbg:
                    dump("ys", YS[:, :], 512)
                Y3 = YS[:].rearrange("p (g d) -> p g d", g=8)
                S33 = S3[:].rearrange("p (g d) -> p g d", g=8)
                red(stB[:, 0:8], Y3)
                tt("pool", S3[:], YS[:], YS[:], ALU.mult)
                red(stB[:, 8:16], S33)
                ts("dve", stB[:, 0:8], stB[:, 0:8], 1.0 / 64, None, ALU.mult)
                tt("dve", stB[:, 16:24], stB[:, 0:8], stB[:, 0:8], ALU.mult)
                ts("dve", stB[:, 8:16], stB[:, 8:16], 1.0 / 64, GN_EPS, ALU.mult, ALU.add)
                tt("dve", stB[:, 8:16], stB[:, 8:16], stB[:, 16:24], ALU.subtract)
                tt("pool", stB[:, 24:32], stB[:, 8:16], mhalf[:], ALU.pow)
                mb = stB[:, 0:8].unsqueeze(2).broadcast_to([128, 8, 64])
                rb = stB[:, 24:32].unsqueeze(2).broadcast_to([128, 8, 64])
                bb = GCp[:, 8:16].unsqueeze(2).broadcast_to([128, 8, 64])
                tt("pool", S33, VBp[:].rearrange("p (g d) -> p g d", g=8), bb, ALU.mult)
                tt("dve", Y3, Y3, mb, ALU.subtract)
                tt("dve", Y3, Y3, rb, ALU.mult)
                tt("dve", YS[:], YS[:], gng[:], ALU.mult)
                tt("dve", YS[:], YS[:], gnb[:], ALU.add)
                tt("dve", YS[:], YS[:], S3[:], ALU.add)
                tt("dve", CATb[:], YS[:], SGp[:], ALU.mult)
                if isdbg:
                    dump("out_b", YS[:, :], 512)
            steps.append(s_post)

            def s_out():
                for kc in range(8):
                    src_ = CAp if kc < 4 else CATb
                    tr(PMb[:, kc * 128:(kc + 1) * 128], src_[:, (kc % 4) * 128:(kc % 4 + 1) * 128], identb[:])
                cp("act", CATT[:], PMb.rearrange("p (k t) -> p k t", k=8))
                for nb in range(2):
                    bank = nextbig()
                    for kc in range(8):
                        mm(bank[:, :], CATT[:, kc, :], w_out_b[:, kc, nb * 512:(nb + 1) * 512], start=(kc == 0), stop=(kc == 7))
                    stt(xa[:, nb * 512:(nb + 1) * 512], xa[:, nb * 512:(nb + 1) * 512], ALPHA, bank[:, :], ALU.mult, ALU.add)
                    op("dve", (lambda nb_: (lambda e: e.bn_stats(out=stB[:, 32 + 6 * nb_:38 + 6 * nb_], in_=xa[:, nb_ * 512:(nb_ + 1) * 512])))(nb),
                       outs=[stB[:, 32 + 6 * nb:38 + 6 * nb]], ins=[xa[:, nb * 512:(nb + 1) * 512]])
                op("dve", lambda e: e.bn_aggr(out=stB[:, 56:58], in_=stB[:, 32:44]), outs=[stB[:, 56:58]], ins=[stB[:, 32:44]])
                ts("dve", stB[:, 58:59], stB[:, 57:58], LN_EPS, None, ALU.add)
                tt("pool", stB[:, 59:60], stB[:, 58:59], mhalf[:, 0:1], ALU.pow)
                stt(stB[:, 60:61], stB[:, 56:57], -1.0, stB[:, 59:60], ALU.mult, ALU.mult)
                act(xa[:], xa[:], AF.Identity, scale=stB[:, 59:60], bias=stB[:, 60:61])
                tt("dve", xa[:, 0:512], xa[:, 0:512], lng[:, 0:512], ALU.mult)
                tt("pool", xa[:, 512:1024], xa[:, 512:1024], lng[:, 512:1024], ALU.mult)
                tt("dve", xa[:, 0:512], xa[:, 0:512], lnb[:, 0:512], ALU.add)
                tt("pool", xa[:, 512:1024], xa[:, 512:1024], lnb[:, 512:1024], ALU.add)
                dma(y_d[tok, :], xa[:], "out")
            steps.append(s_out)
            return steps

        for it in range(NT + 2):
            lists = []
            if it < NT:
                lists.append(front(it))
            if 0 <= it - 1 < NT:
                lists.append(inv(it - 1))
            if 0 <= it - 2 < NT:
                lists.append(chain_out(it - 2))
            idx = [0] * len(lists)
            while True:
                best, bf = None, None
                for li, L in enumerate(lists):
                    if idx[li] < len(L):
                        frac = idx[li] / len(L)
                        if bf is None or frac < bf:
                            best, bf = li, frac
                if best is None:
                    break
                lists[best][idx[best]]()
                idx[best] += 1

        out_nodes = [i for i, nd in enumerate(P.nodes) if nd.dma is not None and (nd.dma.startswith("out") or nd.dma.startswith("dbg"))]
        order = P.schedule()
        P.emit(order)
        last = {}
        for i in out_nodes:
            last[P.nodes[i].dma] = i
        for i in last.values():
            P.wait_node("sp", i)

        block = es.enter_context(nc.Block())

        @block.tensor
        def _(e):
            P.replay("pe", e)

        @block.scalar
        def _(e):
            P.replay("act", e)

        @block.vector
        def _(e):
            P.replay("dve", e)

        @block.gpsimd
        def _(e):
            P.replay("pool", e)

        @block.sync
        def _(e):
            P.replay("sp", e)
    return nc


def make_in_maps(inputs, n_cores=8):
    f = lambda a: np.ascontiguousarray(np.asarray(a, dtype=np.float32))
    x = f(inputs["x"])
    c = f(inputs["c"])
    b_ada = f(inputs["b_ada"])
    shared = {
        "w_ada": f(inputs["w_ada"]),
        "b_ada_row": f(b_ada.reshape(1, 3072)),
        "b_ada_g": f(b_ada[2048:3072].reshape(1, 1024)),
        "w_in": f(inputs["w_in"]),
        "mu_b": f(inputs["mu_b"]),
        "ln_v_g": f(inputs["ln_v_g"]).reshape(512),
        "ln_v_b": f(inputs["ln_v_b"]).reshape(512),
        "w_spT": f(np.transpose(f(inputs["w_spatial"]), (0, 2, 1))),
        "b_sp": f(f(inputs["b_spatial"]).T),
        "w0a0": f(np.concatenate([f(inputs["w0"]), f(inputs["a0"])]).reshape(1, 1024)),
        "wua": f(np.concatenate([f(inputs["w_up"]), f(inputs["a_up"])], axis=0)),
        "k_k": f(inputs["k_k"]),
        "k_a": f(inputs["k_a"]),
        "r_k": f(inputs["r_k"]).reshape(512),
        "gn_g": f(inputs["gn_g"]).reshape(512),
        "gn_b": f(inputs["gn_b"]).reshape(512),
        "w_out": f(inputs["w_out"]),
        "ln_g": f(inputs["ln_g"]),
        "ln_b": f(inputs["ln_b"]),
    }
    maps = []
    for b in range(n_cores):
        m = dict(shared)
        m["x"] = f(x[b])
        m["xT"] = f(x[b].T)
        m["cT"] = f(c[b].reshape(8, 128).T)
        maps.append(m)
    return maps


def kernel(**inputs):
    maps = make_in_maps(inputs, 8)
    nc = build(NT=32)
    res = run_bass_kernel_spmd(nc, maps, core_ids=list(range(8)))
    out = np.stack([np.asarray(r["y"], dtype=np.float32) for r in res.results], axis=0)
    return out
```

```python
import contextlib
import numpy as np
import concourse.bass as bass
import concourse.mybir as mybir
from concourse.bass_utils import run_bass_kernel_spmd

F32 = mybir.dt.float32
BF16 = mybir.dt.bfloat16
AF = mybir.ActivationFunctionType
ALU = mybir.AluOpType
AX = mybir.AxisListType

D = 1024
T = 4096
NCOL = 3712
NB0 = 1536
LN_EPS = 1e-5
GN_EPS = 64e-5
ALPHA = 2.0 ** 0.25
NEGC = -0.6065306597126334
SEM_CAP = 30000


class Buf:
    def __init__(self, t, name):
        self.t = t
        self.name = name
        self.w = [None, None]
        self.r = [[], []]
        self.psum = False

    def __getitem__(self, idx):
        return self.t[idx]


class Node:
    __slots__ = ("eng", "fn", "dma", "deps", "dur", "start", "end", "tok", "users", "ndep", "est")

    def __init__(self, eng, fn, dma, deps, dur):
        self.eng = eng
        self.fn = fn
        self.dma = dma
        self.deps = deps
        self.dur = dur
        self.start = None
        self.end = None
        self.tok = None
        self.users = []
        self.ndep = 0
        self.est = 0.0


import os
HOP_NS = float(os.environ.get("K_HOP", "900"))
SAME_NS = float(os.environ.get("K_SAME", "120"))
PE_SCALE = float(os.environ.get("K_PESCALE", "1.0"))


class Prog:
    def __init__(self, nc, es):
        self.nc = nc
        self.es = es
        self.names = ["pe", "act", "dve", "pool", "sp"]
        self.prog = {k: [] for k in self.names}
        self.nsem = 0
        self.dsems = {}
        self.bufs = {}
        self.nodes = []
        self.nops = 0

    def dsem(self, name):
        if name not in self.dsems:
            s = self.es.enter_context(self.nc.semaphore("d_" + name))
            self.dsems[name] = [s, "d_" + name, 0]
        return self.dsems[name]

    def sb(self, name, shape, dt=F32):
        t = self.es.enter_context(self.nc.sbuf_tensor(name, shape, dt))
        b = Buf(t, name)
        self.bufs[name] = b
        return b

    def ps(self, name, shape, dt=F32):
        t = self.es.enter_context(self.nc.psum_tensor(name, shape, dt))
        b = Buf(t, name)
        b.psum = True
        self.bufs[name] = b
        return b

    def _halves(self, ap):
        p0 = ap.start_partition() if callable(ap.start_partition) else ap.start_partition
        n = ap.partition_size() if callable(ap.partition_size) else ap.partition_size
        hs = []
        if p0 < 64:
            hs.append(0)
        if p0 + n > 64:
            hs.append(1)
        return hs

    def _regions(self, aps):
        out = []
        for ap in aps:
            nm = ap.tensor.name
            b = self.bufs.get(nm)
            if b is None:
                continue
            for h in self._halves(ap):
                out.append((b, h))
        return out

    def op(self, eng, fn, outs=(), ins=(), dma=None, dur=300.0):
        self.nops += 1
        if self.nops > getattr(self, "max_ops", 10 ** 9):
            return None
        rd = self._regions(ins)
        wr = self._regions(outs)
        deps = set()
        for (b, h) in rd:
            if b.w[h] is not None:
                deps.add(b.w[h])
            if b.psum:
                for t in b.r[h]:
                    if self.nodes[t].eng != eng:
                        deps.add(t)
        for (b, h) in wr:
            if b.w[h] is not None:
                deps.add(b.w[h])
            for t in b.r[h]:
                deps.add(t)
        nid = len(self.nodes)
        self.nodes.append(Node(eng, fn, dma, sorted(deps), float(dur)))
        for (b, h) in rd:
            b.r[h].append(nid)
        for (b, h) in wr:
            b.w[h] = nid
            b.r[h] = []
        return nid

    def schedule(self):
        nodes = self.nodes
        for i, nd in enumerate(nodes):
            nd.ndep = len(nd.deps)
            for d in nd.deps:
                nodes[d].users.append(i)
        free = {k: 0.0 for k in self.names}
        cands = {k: [] for k in self.names}
        order = {k: [] for k in self.names}
        for i, nd in enumerate(nodes):
            if nd.ndep == 0:
                cands[nd.eng].append(i)
        remaining = len(nodes)
        while remaining:
            best = None
            for k in self.names:
                cl = cands[k]
                if not cl:
                    continue
                f = free[k]
                pick = None
                for i in cl:
                    e = nodes[i].est
                    if e <= f:
                        if pick is None or pick[0] > f or i < pick[1]:
                            pick = (f, i) if (pick is None or pick[0] > f or i < pick[1]) else pick
                    else:
                        if pick is None or e < pick[0]:
                            pick = (e, i)
                if best is None or pick[0] < best[0] or (pick[0] == best[0] and pick[1] < best[1]):
                    best = (pick[0], pick[1], k)
            st, i, k = best
            nd = nodes[i]
            cands[k].remove(i)
            nd.start = st
            if nd.dma is not None:
                nd.end = st + nd.dur
                free[k] = st + 60.0
            else:
                nd.end = st + nd.dur
                free[k] = nd.end
            order[k].append(i)
            remaining -= 1
            for u in nd.users:
                un = nodes[u]
                lat = SAME_NS if (un.eng == k and nd.dma is None) else HOP_NS
                if un.eng == "pe" and k == "pe":
                    lat = 0.0
                t = nd.end + lat
                if t > un.est:
                    un.est = t
                un.ndep -= 1
                if un.ndep == 0:
                    cands[un.eng].append(u)
        self.makespan = max(nd.end for nd in nodes) if nodes else 0.0
        return order

    def emit(self, order):
        nodes = self.nodes
        sem = {}
        cnt = {}
        waited = {k: {} for k in self.names}

        def new_sem(k):
            self.nsem += 1
            nm = "s_%s_%d" % (k, self.nsem)
            s_ = self.es.enter_context(self.nc.semaphore(nm))
            sem[k] = (s_, nm)
            cnt[k] = 0
        for k in self.names:
            new_sem(k)
        allids = sorted(range(len(nodes)), key=lambda i: (nodes[i].start, i))
        pos = {k: 0 for k in self.names}
        for i in allids:
            nd = nodes[i]
            k = nd.eng
            assert order[k][pos[k]] == i
            pos[k] += 1
            for d in nd.deps:
                dn = nodes[d]
                if dn.eng == "pe" and k == "pe" and dn.dma is None:
                    continue
                semh, semn, val = dn.tok
                if waited[k].get(semn, 0) >= val:
                    continue
                waited[k][semn] = val
                self.prog[k].append(("wait", semh, val))
            if nd.dma is not None:
                ds = self.dsem(nd.dma)
                ds[2] += 16
                nd.tok = (ds[0], ds[1], ds[2])
                self.prog[k].append(("op", nd.fn, ds[0], 16))
            else:
                if cnt[k] >= SEM_CAP:
                    new_sem(k)
                cnt[k] += 1
                nd.tok = (sem[k][0], sem[k][1], cnt[k])
                self.prog[k].append(("op", nd.fn, sem[k][0], 1))
        self.waited = waited

    def wait_node(self, eng, nid):
        semh, semn, val = self.nodes[nid].tok
        if self.waited[eng].get(semn, 0) >= val:
            return
        self.waited[eng][semn] = val
        self.prog[eng].append(("wait", semh, val))

    def replay(self, eng, e):
        for item in self.prog[eng]:
            if item[0] == "wait":
                e.wait_ge(item[1], item[2])
            else:
                item[1](e).then_inc(item[2], item[3])


def build(NT=32, dbg=None, max_ops=None):
    nc = bass.Bass("TRN2", target_bir_lowering=False)
    dram = {}

    def din(name, shape):
        dram[name] = nc.dram_tensor(name, list(shape), F32, kind="ExternalInput").ap()
        return dram[name]

    xT_d = din("xT", [D, T])
    x_d = din("x", [T, D])
    cT_d = din("cT", [128, 8])
    wada_d = din("w_ada", [D, 3 * D])
    bada_row_d = din("b_ada_row", [1, 3 * D])
    bada_g_d = din("b_ada_g", [1, D])
    win_d = din("w_in", [D, NCOL])
    mu_d = din("mu_b", [2176])
    lnvg_d = din("ln_v_g", [512])
    lnvb_d = din("ln_v_b", [512])
    wsT_d = din("w_spT", [8, 128, 128])
    bsp_d = din("b_sp", [128, 8])
    w0a0_d = din("w0a0", [1, 1024])
    wua_d = din("wua", [128, 512])
    kk_d = din("k_k", [512])
    ka_d = din("k_a", [512])
    rk_d = din("r_k", [512])
    gng_d = din("gn_g", [512])
    gnb_d = din("gn_b", [512])
    wout_d = din("w_out", [D, D])
    lng_d = din("ln_g", [D])
    lnb_d = din("ln_b", [D])
    y_d = nc.dram_tensor("y", [T, D], F32, kind="ExternalOutput").ap()
    dbg_d = None
    if dbg:
        dbg_d = nc.dram_tensor("dbg", [128, 16384], F32, kind="ExternalOutput").ap()

    es = contextlib.ExitStack()
    with es:
        P = Prog(nc, es)
        if max_ops is not None:
            P.max_ops = max_ops
        sb, ps, op = P.sb, P.ps, P.op

        w_in_c = [sb("w_in_c%d" % j, [128, 8, min(512, NCOL - j * 512)], BF16) for j in range(8)]
        w_out_b = sb("w_out_b", [128, 8, D], BF16)
        wsT_b = sb("wsT_b", [128, 8, 128], BF16)
        wua_b = sb("wua_b", [128, 512], BF16)
        identb = sb("identb", [128, 128], BF16)
        identf = sb("identf", [128, 128])
        ones_f = sb("ones_f", [128, 128])
        tri = sb("tri", [128, 3, 128])
        maskNN = sb("maskNN", [128, 4, 128], BF16)
        maskS8 = sb("maskS8", [128, 8, 64], BF16)
        maskI8 = sb("maskI8", [128, 8, 64], BF16)
        I8 = sb("I8", [128, 8, 64], BF16)
        mu = sb("mu", [128, 2176])
        lnvg = sb("lnvg", [128, 512])
        lnvb = sb("lnvb", [128, 512])
        kk_ = sb("kk_", [128, 512])
        ka_ = sb("ka_", [128, 512])
        rk_ = sb("rk_", [128, 512])
        gng = sb("gng", [128, 512])
        gnb = sb("gnb", [128, 512])
        lng = sb("lng", [128, D])
        lnb = sb("lnb", [128, D])
        rows = sb("rows", [128, 512])
        small = sb("small", [128, 64])
        mhalf = sb("mhalf", [128, 8])
        XT = sb("XT", [128, 4, 128])
        hT = [sb("hT%d" % i, [128, 8, 130], BF16) for i in range(2)]
        hd = sb("hd", [128, 8, 128], BF16)
        XA = [sb("XA0", [128, D])] * 2
        U_sb = sb("U_sb", [128, 512], BF16)
        SG_a = sb("SG_a", [128, 512], BF16)
        SG_b = [sb("SG_b%d" % i, [128, 512], BF16) for i in range(3)]
        R_b = sb("R_b", [128, 512], BF16)
        V_g = sb("V_g", [128, 512])
        VN = sb("VN", [128, 512], BF16)
        PBt = sb("PBt", [128, 2176])
        SIGW = sb("SIGW", [128, 512])
        A_sig = sb("A_sig", [128, 512], BF16)
        NA = sb("NA", [128, 512], BF16)
        BB = sb("BB", [128, 512], BF16)
        KM = sb("KM", [128, 512], BF16)
        S1 = sb("S1", [128, 512])
        S2 = sb("S2", [128, 512])
        S3 = sb("S3", [128, 512])
        TA = sb("TA", [128, 512], BF16)
        TB = sb("TB", [128, 512], BF16)
        BH = [sb("BH%d" % i, [128, 512], BF16) for i in range(2)]
        KH = [sb("KH%d" % i, [128, 512], BF16) for i in range(2)]
        VB = [sb("VB%d" % i, [128, 512], BF16) for i in range(3)]
        WA_T = sb("WA_T", [128, 128], BF16)
        FM2 = [sb("FM2_%d" % i, [128, 4, 8, 64], BF16) for i in range(2)]
        X0 = sb("X0", [128, 8, 192], BF16)
        Xa = sb("Xa", [128, 8, 192], BF16)
        ABR2 = [sb("ABR%d" % i, [128, 8, 64], BF16) for i in range(2)]
        AAK2 = [sb("AAK%d" % i, [128, 8, 64], BF16) for i in range(2)]
        AKR2 = [sb("AKR%d" % i, [128, 8, 64], BF16) for i in range(2)]
        EB2 = [sb("EB%d" % i, [128, 8, 64], BF16) for i in range(2)]
        Tm2 = [sb("Tm%d" % i, [128, 8, 64], BF16) for i in range(2)]
        WTs = sb("WTs", [128, 512], BF16)
        onesb_t = WTs
        UTs = sb("UTs", [128, 512], BF16)
        ST = sb("ST", [128, 8, 64])
        Dg = sb("Dg", [128, 8, 64])
        STb = sb("STb", [128, 8, 64], BF16)
        GC = [sb("GC%d" % i, [128, 16]) for i in range(3)]
        YS = sb("YS", [128, 512])
        CATa = [sb("CATa%d" % i, [128, 512], BF16) for i in range(3)]
        CATb = sb("CATb", [128, 512], BF16)
        CATT = sb("CATT", [128, 8, 128], BF16)
        st = sb("st", [128, 64])
        stB = sb("stB", [128, 64])

        PB0 = ps("PB0", [128, 512])
        PB1 = ps("PB1", [128, 512])
        PM = ps("PM", [128, 512])
        PY = ps("PY", [128, 512])
        PX0 = ps("PX0", [128, 512])
        PX1 = ps("PX1", [128, 512])
        PP = ps("PP", [128, 512])
        PC = ps("PC", [128, 512])
        PBIG = [PB0, PB1]
        big_i = [0]

        def nextbig():
            b = PBIG[big_i[0] % 2]
            big_i[0] += 1
            return b

        def fsz(ap):
            n_ = 1
            for d_ in ap.shape[1:]:
                n_ *= int(d_)
            return n_

        def mm(out, lhsT, rhs, start=True, stop=True, tp=None):
            n_ = fsz(rhs)
            d_ = 55.0 if n_ <= 64 else 0.55 * n_ + 20
            if lhsT.dtype == F32:
                d_ = 4 * d_ + 100
            d_ *= PE_SCALE
            if tp is None:
                op("pe", lambda e: e.matmul(out, lhsT=lhsT, rhs=rhs, start=start, stop=stop),
                   outs=[out], ins=[lhsT, rhs], dur=d_)
            else:
                op("pe", lambda e: e.matmul(out, lhsT=lhsT, rhs=rhs, start=start, stop=stop, tile_position=tp),
                   outs=[out], ins=[lhsT, rhs], dur=d_)

        def tr(out, in_, ident, tp=None):
            if tp is None:
                op("pe", lambda e: e.transpose(out=out, in_=in_, identity=ident), outs=[out], ins=[in_, ident], dur=100.0)
            else:
                op("pe", lambda e: e.transpose(out=out, in_=in_, identity=ident, tile_position=tp), outs=[out], ins=[in_, ident], dur=55.0)

        def act(out, in_, func, scale=None, bias=None):
            kw = {}
            if scale is not None:
                kw["scale"] = scale
            if bias is not None:
                kw["bias"] = bias
            ins = [in_] + [a for a in (scale, bias) if hasattr(a, "tensor")]
            op("act", lambda e: e.activation(out=out, in_=in_, func=func, **kw), outs=[out], ins=ins,
               dur=240.0 + 0.7 * fsz(out) + (200.0 if len(ins) > 1 else 0.0))

        def tt(eng, out, in0, in1, o):
            d_ = (110.0 + 1.1 * fsz(out)) if eng == "dve" else (300.0 + 1.75 * fsz(out))
            if o == ALU.pow:
                d_ = 1400.0
            op(eng, lambda e: e.tensor_tensor(out=out, in0=in0, in1=in1, op=o), outs=[out], ins=[in0, in1], dur=d_)

        def ts(eng, out, in0, s1, s2, o0, o1=None):
            ins = [in0] + [a for a in (s1, s2) if hasattr(a, "tensor")]
            d_ = (110.0 + 1.1 * fsz(out)) if eng == "dve" else (300.0 + 1.75 * fsz(out))
            if o1 is None:
                op(eng, lambda e: e.tensor_scalar(out=out, in0=in0, scalar1=s1, scalar2=None, op0=o0),
                   outs=[out], ins=ins, dur=d_)
            else:
                op(eng, lambda e: e.tensor_scalar(out=out, in0=in0, scalar1=s1, scalar2=s2, op0=o0, op1=o1),
                   outs=[out], ins=ins, dur=d_)

        def stt(out, in0, s, in1, o0, o1):
            ins = [in0, in1] + ([s] if hasattr(s, "tensor") else [])
            op("dve", lambda e: e.scalar_tensor_tensor(out=out, in0=in0, scalar=s, in1=in1, op0=o0, op1=o1),
               outs=[out], ins=ins, dur=110.0 + 1.1 * fsz(out))

        def red(out, in_):
            op("dve", lambda e: e.tensor_reduce(out=out, in_=in_, axis=AX.X, op=ALU.add), outs=[out], ins=[in_],
               dur=110.0 + 1.1 * fsz(in_))

        def cp(eng, out, in_):
            if eng == "act":
                op("act", lambda e: e.activation(out=out, in_=in_, func=AF.Copy), outs=[out], ins=[in_],
                   dur=240.0 + 0.7 * fsz(out))
            else:
                op(eng, lambda e: e.tensor_copy(out=out, in_=in_), outs=[out], ins=[in_],
                   dur=(110.0 + 1.1 * fsz(out)) if eng == "dve" else (300.0 + 1.75 * fsz(out)))

        def dma(out, in_, sem):
            op("sp", lambda e: e.dma_start(out=out, in_=in_), outs=[out], ins=[in_], dma=sem,
               dur=2000.0 + 0.004 * 4 * fsz(out) * 128)

        def memset(eng, ap, v):
            op(eng, lambda e: e.memset(ap, v), outs=[ap], dur=300.0)

        def asel(out, in_, pattern, cmp, base, cm):
            op("pool", lambda e: e.affine_select(out=out, in_=in_, pattern=pattern, compare_op=cmp, fill=0.0,
                                                 base=base, channel_multiplier=cm), outs=[out], ins=[in_],
               dur=250.0 + 0.9 * fsz(out))

        dbg_off = [0]

        def dump(name, ap, ncols):
            if not dbg:
                return
            o = dbg_off[0]
            n = ap.partition_size() if callable(ap.partition_size) else ap.partition_size
            p0 = ap.start_partition() if callable(ap.start_partition) else ap.start_partition
            dbg[name] = (o, p0, n, ncols)
            dma(dbg_d[p0:p0 + n, o:o + ncols], ap, "dbg_" + name)
            dbg_off[0] += ncols

        memset("pool", ones_f[:], 1.0)
        memset("pool", WTs[:], 1.0)
        memset("pool", tri[:], 0.0)
        memset("pool", mhalf[:], -0.5)
        memset("pool", hT[1][:], 0.0)
        memset("pool", ST[:], 0.0)
        memset("pool", STb[:], 0.0)
        memset("pool", S3[:], NEGC)
        asel(identf[:], ones_f[:], [[-1, 128]], ALU.is_equal, 0, 1)
        cp("pool", identb[:], identf[:])
        for c in range(2):
            psl = slice(c * 64, (c + 1) * 64)
            asel(tri[psl, 0, c * 64:(c + 1) * 64], S3[psl, 0:64], [[1, 64]], ALU.is_ge, 0, -1)
            asel(tri[psl, 1, c * 64:(c + 1) * 64], S3[psl, 0:64], [[1, 64]], ALU.is_gt, 0, -1)
            asel(tri[psl, 2, c * 64:(c + 1) * 64], S3[psl, 0:64], [[-1, 64]], ALU.is_gt, 0, 1)
            asel(maskS8[psl, :, :], WTs[psl, :].rearrange("p (h t) -> p h t", h=8), [[0, 8], [1, 64]], ALU.is_gt, 0, -1)
            asel(maskI8[psl, :, :], WTs[psl, :].rearrange("p (h t) -> p h t", h=8), [[0, 8], [1, 64]], ALU.is_ge, 0, -1)
            asel(I8[psl, :, :], WTs[psl, :].rearrange("p (h t) -> p h t", h=8), [[0, 8], [-1, 64]], ALU.is_equal, 0, 1)
            asel(maskNN[psl, :, 0:64], WTs[psl, 0:256].rearrange("p (h t) -> p h t", h=4), [[0, 4], [-1, 64]], ALU.is_gt, 0, 1)
            asel(maskNN[psl, :, 64:128], WTs[psl, 0:256].rearrange("p (h t) -> p h t", h=4), [[0, 4], [1, 64]], ALU.is_gt, 0, -1)

        for (tile_, src, n) in [(mu, mu_d, 2176), (lnvg, lnvg_d, 512), (lnvb, lnvb_d, 512), (kk_, kk_d, 512),
                                (ka_, ka_d, 512), (rk_, rk_d, 512), (gng, gng_d, 512), (gnb, gnb_d, 512),
                                (lng, lng_d, D), (lnb, lnb_d, D)]:
            dma(tile_[:], src.partition_broadcast(128), "par_" + tile_.name)
        dma(small[:, 8:16], bsp_d[:, :], "par_bsp")
        dma(rows[0:1, :], bada_g_d[:, 0:512], "par_rows0")
        dma(rows[32:33, :], bada_g_d[:, 512:1024], "par_rows32")
        dma(small[:, 0:8], cT_d[:, :], "par_c")
        act(small[:, 24:32], small[:, 0:8], AF.Silu)

        for g in range(8):
            stg = XA[0]
            dma(stg[:, 0:128], wsT_d[g, :, :], "stgw")
            asel(stg[:, 128:256], stg[:, 0:128], [[1, 128]], ALU.is_ge, 0, -1)
            cp("pool", wsT_b[:, g, :], stg[:, 128:256])
        stg = XA[0]
        dma(stg[:, 0:512], wua_d[:, :], "stgw")
        cp("dve", wua_b[:], stg[:, 0:512])

        slots = [S1[:, :], S2[:, :], V_g[:, :], YS[:, :], SIGW[:, :], S3[:, :]]
        si = [0]
        ring = [list(range(6))]

        def stage(src_ap, w=512):
            r_ = ring[0]
            i = r_[si[0] % len(r_)]
            si[0] += 1
            v = slots[i][:, 0:w]
            dma(v, src_ap, "stg%d" % i)
            return v

        mod_banks = [PX0, PX1, PP, PC, PB0, PB1]
        dma(PBt[0:1, 0:2048], bada_row_d[:, 0:2048], "par_badarow")

        def mod_chunk(j):
            po = 32 if j == 5 else 0
            for kc in range(8):
                v = stage(wada_d[kc * 128:(kc + 1) * 128, j * 512:(j + 1) * 512])
                if po:
                    mm(mod_banks[j][po:po + 1, :], small[:, 24 + kc:25 + kc], v, start=(kc == 0), stop=(kc == 7), tp=(0, po))
                else:
                    mm(mod_banks[j][0:1, :], small[:, 24 + kc:25 + kc], v, start=(kc == 0), stop=(kc == 7))
        for j in range(4):
            mod_chunk(j)
            tt("dve", PBt[0:1, j * 512:(j + 1) * 512], mod_banks[j][0:1, :], PBt[0:1, j * 512:(j + 1) * 512], ALU.add)
        for j in range(16):
            mm(PM[:, j:j + 1], PBt[0:1, j * 128:(j + 1) * 128], ones_f[0:1, 0:1])
        cp("dve", small[:, 32:48], PM[:, 0:16])
        ts("dve", small[:, 40:48], small[:, 40:48], 1.0, None, ALU.add)
        ci = 0
        for j in range(8):
            c0 = j * 512
            w = min(512, NCOL - c0)
            for kc in range(8):
                v = stage(win_d[kc * 128:(kc + 1) * 128, c0:c0 + w], w)
                cp(["act", "dve", "pool"][ci % 3], w_in_c[j][:, kc, 0:w], v)
                ci += 1
        ring[0] = [3, 5]
        for j in (4, 5):
            mod_chunk(j)
        tt("dve", rows[0:1, 0:512], PB0[0:1, :], rows[0:1, 0:512], ALU.add)
        tt("dve", rows[32:33, 0:512], PB1[32:33, :], rows[32:33, 0:512], ALU.add)
        for n2 in range(2):
            bank = [PX0, PX1][n2]
            mm(bank[:, :], ones_f[32 * n2:32 * n2 + 1, 0:128], rows[32 * n2:32 * n2 + 1, 0:512])
            cp("act", XA[0][:, n2 * 512:(n2 + 1) * 512], bank[:, :])
        for nb in range(2):
            for kc in range(8):
                v = stage(wout_d[kc * 128:(kc + 1) * 128, nb * 512:(nb + 1) * 512])
                tt(["dve", "pool"][ci % 2], w_out_b[:, kc, nb * 512:(nb + 1) * 512], v,
                   XA[0][:, nb * 512:(nb + 1) * 512], ALU.mult)
                ci += 1
        dma(rows[0:1, 0:512], w0a0_d[:, 0:512], "par_rows0")
        dma(rows[64:65, 0:512], w0a0_d[:, 512:1024], "par_rows64")

        xT_v = xT_d.rearrange("(k p) t -> p k t", p=128)

        PMb = PM[:, :].bitcast(BF16)

        def front(n):
            steps = []
            par = n % 2
            h_ = hT[par]
            tok = slice(n * 128, (n + 1) * 128)
            FMp, VBp, BHp, KHp, SGp = FM2[par], VB[n % 3], BH[par], KH[par], SG_b[n % 3]
            GCp, CAp = GC[n % 3], CATa[n % 3]

            def s_load():
                for kc in range(8):
                    dma(XT[:, kc % 4, :], xT_v[:, kc, tok], "xT%d" % (kc % 4))
                    ts("dve", h_[:, kc, 2:130], XT[:, kc % 4, :], small[:, 40 + kc:41 + kc], small[:, 32 + kc:33 + kc],
                       ALU.mult, ALU.add)
                cp("dve", h_[:, :, 1:2], hT[1 - par][:, :, 129:130])
                tt("dve", hd[:], h_[:, :, 1:129], h_[:, :, 2:130], ALU.subtract)
            steps.append(s_load)

            def proj(j, src=None):
                c0 = j * 512
                w = min(512, NCOL - c0)
                bank = nextbig()
                for kc in range(8):
                    lh = h_[:, kc, 2:130] if src is None else src[:, kc, :]
                    mm(bank[:, 0:w], lh, w_in_c[j][:, kc, 0:w], start=(kc == 0), stop=(kc == 7))
                return bank, w

            def s_u():
                bank, w = proj(0)
                act(U_sb[:], bank[:, :], AF.Gelu_apprx_tanh)
            steps.append(s_u)

            def s_v():
                bank, w = proj(1)
                act(V_g[:], bank[:, :], AF.Gelu_apprx_tanh)
            steps.append(s_v)

            def s_ga():
                bank, w = proj(2)
                act(SG_a[:], bank[:, :], AF.Silu)
            steps.append(s_ga)

            def stats(stt_, src3, sq, sq3, eps):
                red(stt_[:, 0:8], src3)
                tt("pool", sq, src3.rearrange("p g d -> p (g d)") if False else sq_src[0], sq_src[0], ALU.mult)
                red(stt_[:, 8:16], sq3)
                ts("dve", stt_[:, 0:8], stt_[:, 0:8], 1.0 / 64, None, ALU.mult)
                tt("dve", stt_[:, 16:24], stt_[:, 0:8], stt_[:, 0:8], ALU.mult)
                ts("dve", stt_[:, 8:16], stt_[:, 8:16], 1.0 / 64, eps, ALU.mult, ALU.add)
                tt("dve", stt_[:, 8:16], stt_[:, 8:16], stt_[:, 16:24], ALU.subtract)
                tt("pool", stt_[:, 24:32], stt_[:, 8:16], mhalf[:], ALU.pow)
            sq_src = [None]

            def s_ln():
                Vg3 = V_g[:].rearrange("p (g d) -> p g d", g=8)
                S13 = S1[:].rearrange("p (g d) -> p g d", g=8)
                sq_src[0] = V_g[:]
                stats(st, Vg3, S1[:], S13, LN_EPS)
                mb = st[:, 0:8].unsqueeze(2).broadcast_to([128, 8, 64])
                rb = st[:, 24:32].unsqueeze(2).broadcast_to([128, 8, 64])
                tt("dve", S13, Vg3, mb, ALU.subtract)
                tt("dve", S13, S13, rb, ALU.mult)
                tt("pool", S1[:], S1[:], lnvg[:], ALU.mult)
                tt("pool", VN[:], S1[:], lnvb[:], ALU.add)
            steps.append(s_ln)

            def s_sp():
                tt("pool", U_sb[:], U_sb[:], SG_a[:], ALU.mult)
                for g in range(8):
                    mm(PM[:, g * 64:(g + 1) * 64], wsT_b[:, g, :], VN[:, g * 64:(g + 1) * 64], start=True, stop=True)
                tt("dve", S1[:].rearrange("p (g d) -> p g d", g=8), PM[:, :].rearrange("p (g d) -> p g d", g=8),
                   small[:, 8:16].unsqueeze(2).broadcast_to([128, 8, 64]), ALU.add)
                tt("dve", CAp[:], S1[:], U_sb[:], ALU.mult)
                if dbg and n == dbg.get("_tile", 0):
                    dump("out_a", S1[:], 512)
            steps.append(s_sp)

            def mk_b(j):
                def s_b():
                    c0 = j * 512
                    bank_d, w = proj(3 + j, src=hd)
                    tt("dve", PBt[:, c0:c0 + w], bank_d[:, 0:w], mu[:, c0:c0 + w], ALU.mult)
                    bank_p, w = proj(3 + j)
                    tt("dve", PBt[:, c0:c0 + w], bank_p[:, 0:w], PBt[:, c0:c0 + w], ALU.add)
                return s_b
            for j in range(5):
                steps.append(mk_b(j))

            r_ = PBt[:, 0:512]
            k_ = PBt[:, 512:1024]
            v_ = PBt[:, 1024:1536]
            g_ = PBt[:, 1536:2048]

            def s_wa():
                if dbg and n == dbg.get("_tile", 0):
                    dump("pb", PBt[:, :], 2176)
                act(SGp[:], g_, AF.Silu)
                cp("act", VBp[:], v_)
                cp("act", R_b[:], r_)
                tr(PM[:, 0:128], PBt[:, 2048:2176], identf[:])
                act(WA_T[0:64, :], PM[0:64, 0:128], AF.Tanh)
                cp("act", WA_T[64:128, :], PM[64:128, 0:128])
                mm(PM[:, :], WA_T[0:64, :], wua_b[0:64, :], start=True, stop=False)
                mm(PM[:, :], ones_f[0:1, 0:128], rows[0:1, 0:512], start=False, stop=True)
                act(SIGW[:], PM[:, :], AF.Sigmoid)
                mm(PY[:, :], WA_T[64:128, :], wua_b[64:128, :], start=True, stop=False)
                mm(PY[:, :], ones_f[64:65, 0:128], rows[64:65, 0:512], start=False, stop=True)
                act(A_sig[:], PY[:, :], AF.Sigmoid)
            steps.append(s_wa)

            def s_kk():
                S13 = S1[:].rearrange("p (g d) -> p g d", g=8)
                S23 = S2[:].rearrange("p (g d) -> p g d", g=8)
                tt("dve", S1[:], k_, kk_[:], ALU.mult)
                tt("dve", S2[:], S1[:], S1[:], ALU.mult)
                red(st[:, 32:40], S23)
                ts("dve", st[:, 32:40], st[:, 32:40], 1e-12, None, ALU.add)
                tt("pool", st[:, 40:48], st[:, 32:40], mhalf[:], ALU.pow)
                tt("dve", NA[:].rearrange("p (g d) -> p g d", g=8), S13,
                   st[:, 40:48].unsqueeze(2).broadcast_to([128, 8, 64]), ALU.mult)
                tt("dve", BB[:], NA[:], A_sig[:], ALU.mult)
                stt(S2[:], A_sig[:], -1.0, ka_[:], ALU.add, ALU.mult)
                stt(KM[:], S2[:], 1.0, k_, ALU.add, ALU.mult)
                tt("pool", S2[:], R_b[:], KM[:], ALU.mult)
                tt("pool", S2[:], S2[:], rk_[:], ALU.mult)
                red(GCp[:, 8:16], S23)
            steps.append(s_kk)

            def fm_pass(pi):
                for i, kt in enumerate([TA, TB]):
                    for h in range(8):
                        for c in range(2):
                            cs = slice(c * 64, (c + 1) * 64)
                            o = (i * 8 + h) * 64
                            tr(PMb[cs, o:o + 64], kt[cs, h * 64:(h + 1) * 64], identb[cs, c * 64:(c + 1) * 64],
                               tp=(c * 64, c * 64))
                cp("act", FMp[:, 2 * pi:2 * pi + 2, :, :],
                   PMb.rearrange("p (a h t) -> p a h t", a=2, h=8))

            def s_cum():
                mm(PY[:, :], tri[:, 0, :], SIGW[:])
                act(S1[:], PY[:, :], AF.Exp)
                act(S2[:], PY[:, :], AF.Exp, scale=-1.0)
                tt("dve", TB[:], R_b[:], S1[:], ALU.mult)
                mm(PY[:, :], tri[:, 1, :], SIGW[:])
                act(S1[:], PY[:, :], AF.Exp)
                stt(TA[:], NA[:], -1.0, S1[:], ALU.mult, ALU.mult)
                fm_pass(0)
            steps.append(s_cum)

            def s_cum2():
                tt("dve", TA[:], BB[:], S2[:], ALU.mult)
                tt("dve", TB[:], KM[:], S2[:], ALU.mult)
                mm(PY[:, :], tri[:, 2, :], SIGW[:])
                act(S1[:], PY[:, :], AF.Exp)
                tt("dve", BHp[:], BB[:], S1[:], ALU.mult)
                tt("pool", KHp[:], KM[:], S1[:], ALU.mult)
                for h in range(8):
                    for c in range(2):
                        cs = slice(c * 64, (c + 1) * 64)
                        mm(PY[cs, h:h + 1], SIGW[cs, h * 64:(h + 1) * 64], ones_f[cs, 0:1], tp=(c * 64, c * 64))
                act(GCp[:, 0:8], PY[:, 0:8], AF.Exp, scale=NEGC)
                fm_pass(1)
            steps.append(s_cum2)
            return steps

        def inv(n):
            steps = []
            par = n % 2
            FMp = FM2[par]
            ABR, AAK, AKR, EB, Tm = ABR2[par], AAK2[par], AKR2[par], EB2[par], Tm2[par]

            def fm(kind, h, c):
                return FMp[c * 64:(c + 1) * 64, kind, h, :]

            def s_A():
                for h in range(8):
                    for c in range(2):
                        cs = slice(c * 64, (c + 1) * 64)
                        bank = PX0 if h < 4 else PX1
                        o = (h % 4) * 128
                        mm(bank[cs, o + 64:o + 128], fm(2, h, c), fm(0, h, c), tp=(c * 64, c * 64))
                        mm(bank[cs, o:o + 64], fm(0, h, c), fm(2, h, c), tp=(c * 64, c * 64))
                for h4, bank in enumerate([PX0, PX1]):
                    tt("dve", X0[:, h4 * 4:(h4 + 1) * 4, 0:128], bank[:, :].rearrange("p (h t) -> p h t", h=4),
                       maskNN[:], ALU.mult)
                tt("pool", EB[:], I8[:], X0[:, :, 0:64], ALU.subtract)
            steps.append(s_A)

            def s_A2():
                for (dst, ka, kb, msk) in [(ABR, 2, 1, maskI8), (AAK, 3, 0, maskS8), (AKR, 3, 1, maskI8)]:
                    for h in range(8):
                        for c in range(2):
                            cs = slice(c * 64, (c + 1) * 64)
                            mm(PP[cs, h * 64:(h + 1) * 64], fm(ka, h, c), fm(kb, h, c), tp=(c * 64, c * 64))
                    tt("dve", dst[:], PP[:, :].rearrange("p (h t) -> p h t", h=8), msk[:], ALU.mult)
            steps.append(s_A2)

            seq = [(X0, Xa), (Xa, X0), (X0, Xa), (Xa, X0), (X0, Xa), (Xa, None)]

            def mk_round(k):
                S_, D_ = seq[k]

                def s_r():
                    for h in range(8):
                        for c in range(2):
                            cs = slice(c * 64, (c + 1) * 64)
                            tp = (c * 64, c * 64)
                            bank = PX0 if h < 4 else PX1
                            o = (h % 4) * 128
                            if k == 0:
                                mm(bank[cs, o:o + 64], S_[cs, h, 0:64], S_[cs, h, 64:128], tp=tp)
                            elif k <= 3:
                                mm(bank[cs, o:o + 128], S_[cs, h, 0:64], S_[cs, h, 64:192], tp=tp)
                            else:
                                mm(bank[cs, o + 64:o + 128], S_[cs, h, 0:64], S_[cs, h, 128:192], tp=tp)
                            if k <= 4:
                                mm(PP[cs, h * 64:(h + 1) * 64], S_[cs, h, 64:128], S_[cs, h, 0:64], tp=tp)
                    for h4, bank in enumerate([PX0, PX1]):
                        hs_ = slice(h4 * 4, (h4 + 1) * 4)
                        bv = bank[:, :].rearrange("p (h t) -> p h t", h=4)
                        if k <= 3:
                            cp("act", D_[:, hs_, 64:128], bv[:, :, 0:64])
                        if k == 0:
                            tt("pool", D_[:, hs_, 128:192], S_[:, hs_, 64:128], I8[:, hs_, :], ALU.add)
                        elif k <= 4:
                            tt("dve", D_[:, hs_, 128:192], bv[:, :, 64:128], S_[:, hs_, 128:192], ALU.add)
                        else:
                            tt("dve", Tm[:, hs_, :], bv[:, :, 64:128], S_[:, hs_, 128:192], ALU.add)
                    if k <= 4:
                        cp("act", D_[:, :, 0:64], PP[:, :].rearrange("p (h t) -> p h t", h=8))
                return s_r
            for k in range(6):
                steps.append(mk_round(k))

            def s_E():
                for h in range(8):
                    for c in range(2):
                        cs = slice(c * 64, (c + 1) * 64)
                        mm(PP[cs, h * 64:(h + 1) * 64], EB[cs, h, :], Tm[cs, h, :], tp=(c * 64, c * 64))
                tt("dve", EB[:], I8[:], PP[:, :].rearrange("p (h t) -> p h t", h=8), ALU.subtract)
            steps.append(s_E)
            return steps

        def chain_out(n):
            steps = []
            par = n % 2
            xa = XA[par]
            tok = slice(n * 128, (n + 1) * 128)
            isdbg = dbg and n == dbg.get("_tile", 0)
            FMp, VBp, BHp, KHp, SGp = FM2[par], VB[n % 3], BH[par], KH[par], SG_b[n % 3]
            GCp, CAp = GC[n % 3], CATa[n % 3]
            ABR, AAK, AKR, EB, Tm = ABR2[par], AAK2[par], AKR2[par], EB2[par], Tm2[par]

            def fm(kind, h, c):
                return FMp[c * 64:(c + 1) * 64, kind, h, :]

            def s_pre():
                dma(xa[:], x_d[tok, :], "xa")
                for c in range(2):
                    cs = slice(c * 64, (c + 1) * 64)
                    asel(Dg[cs, :, :], GCp[cs, 0:8].unsqueeze(2).broadcast_to([64, 8, 64]), [[0, 8], [-1, 64]],
                         ALU.is_equal, 0, 1)
            steps.append(s_pre)

            def mk_chain(c):
                cs = slice(c * 64, (c + 1) * 64)
                os_ = slice((1 - c) * 64, (2 - c) * 64)
                tpc = (c * 64, c * 64)

                def s_w():
                    for h in range(8):
                        mm(PC[cs, h * 64:(h + 1) * 64], fm(0, h, c), STb[cs, h, :], start=True, stop=False, tp=tpc)
                        mm(PC[cs, h * 64:(h + 1) * 64], AAK[cs, h, :], VBp[cs, h * 64:(h + 1) * 64], start=False, stop=True, tp=tpc)
                    cp("act", WTs[cs, :], PC[cs, :])

                def s_u():
                    for h in range(8):
                        mm(PC[cs, h * 64:(h + 1) * 64], Tm[cs, h, :], WTs[cs, h * 64:(h + 1) * 64], tp=tpc)
                    cp("act", WTs[cs, :], PC[cs, :])
                    for h in range(8):
                        mm(PC[cs, h * 64:(h + 1) * 64], EB[cs, h, :], WTs[cs, h * 64:(h + 1) * 64], tp=tpc)
                    tt("dve", UTs[cs, :], PC[cs, :], WTs[cs, :], ALU.add)

                def s_s():
                    tps = (c * 64, (1 - c) * 64)
                    for h in range(8):
                        o = PC[os_, h * 64:(h + 1) * 64]
                        mm(o, Dg[cs, h, :], ST[cs, h, :], start=True, stop=False, tp=tps)
                        mm(o, BHp[cs, h * 64:(h + 1) * 64], UTs[cs, h * 64:(h + 1) * 64], start=False, stop=False, tp=tps)
                        mm(o, KHp[cs, h * 64:(h + 1) * 64], VBp[cs, h * 64:(h + 1) * 64], start=False, stop=True, tp=tps)
                    cp("dve", ST[os_, :, :], PC[os_, :].rearrange("p (h i) -> p h i", h=8))
                    cp("act", STb[os_, :, :], PC[os_, :].rearrange("p (h i) -> p h i", h=8))

                def s_y():
                    for h in range(8):
                        o = PY[cs, h * 64:(h + 1) * 64]
                        mm(o, fm(1, h, c), STb[cs, h, :], start=True, stop=False, tp=tpc)
                        mm(o, ABR[cs, h, :], UTs[cs, h * 64:(h + 1) * 64], start=False, stop=False, tp=tpc)
                        mm(o, AKR[cs, h, :], VBp[cs, h * 64:(h + 1) * 64], start=False, stop=True, tp=tpc)
                    cp("act", YS[cs, :], PY[cs, :])
                return [s_w, s_u, s_s, s_y]
            for c in range(2):
                steps.extend(mk_chain(c))

            def s_post():
                if isdbg:
                    dump("ys", YS[:, :], 512)
                Y3 = YS[:].rearrange("p (g d) -> p g d", g=8)
                S33 = S3[:].rearrange("p (g d) -> p g d", g=8)
                red(stB[:, 0:8], Y3)
                tt("pool", S3[:], YS[:], YS[:], ALU.mult)
                red(stB[:, 8:16], S33)
                ts("dve", stB[:, 0:8], stB[:, 0:8], 1.0 / 64, None, ALU.mult)
                tt("dve", stB[:, 16:24], stB[:, 0:8], stB[:, 0:8], ALU.mult)
                ts("dve", stB[:, 8:16], stB[:, 8:16], 1.0 / 64, GN_EPS, ALU.mult, ALU.add)
                tt("dve", stB[:, 8:16], stB[:, 8:16], stB[:, 16:24], ALU.subtract)
                tt("pool", stB[:, 24:32], stB[:, 8:16], mhalf[:], ALU.pow)
                mb = stB[:, 0:8].unsqueeze(2).broadcast_to([128, 8, 64])
                rb = stB[:, 24:32].unsqueeze(2).broadcast_to([128, 8, 64])
                bb = GCp[:, 8:16].unsqueeze(2).broadcast_to([128, 8, 64])
                tt("pool", S33, VBp[:].rearrange("p (g d) -> p g d", g=8), bb, ALU.mult)
                tt("dve", Y3, Y3, mb, ALU.subtract)
                tt("dve", Y3, Y3, rb, ALU.mult)
                tt("dve", YS[:], YS[:], gng[:], ALU.mult)
                tt("dve", YS[:], YS[:], gnb[:], ALU.add)
                tt("dve", YS[:], YS[:], S3[:], ALU.add)
                tt("dve", CATb[:], YS[:], SGp[:], ALU.mult)
                if isdbg:
                    dump("out_b", YS[:, :], 512)
            steps.append(s_post)

            def s_out():
                for kc in range(8):
                    src_ = CAp if kc < 4 else CATb
                    tr(PMb[:, kc * 128:(kc + 1) * 128], src_[:, (kc % 4) * 128:(kc % 4 + 1) * 128], identb[:])
                cp("act", CATT[:], PMb.rearrange("p (k t) -> p k t", k=8))
                for nb in range(2):
                    bank = nextbig()
                    for kc in range(8):
                        mm(bank[:, :], CATT[:, kc, :], w_out_b[:, kc, nb * 512:(nb + 1) * 512], start=(kc == 0), stop=(kc == 7))
                    stt(xa[:, nb * 512:(nb + 1) * 512], xa[:, nb * 512:(nb + 1) * 512], ALPHA, bank[:, :], ALU.mult, ALU.add)
                    op("dve", (lambda nb_: (lambda e: e.bn_stats(out=stB[:, 32 + 6 * nb_:38 + 6 * nb_], in_=xa[:, nb_ * 512:(nb_ + 1) * 512])))(nb),
                       outs=[stB[:, 32 + 6 * nb:38 + 6 * nb]], ins=[xa[:, nb * 512:(nb + 1) * 512]])
                op("dve", lambda e: e.bn_aggr(out=stB[:, 56:58], in_=stB[:, 32:44]), outs=[stB[:, 56:58]], ins=[stB[:, 32:44]])
                ts("dve", stB[:, 58:59], stB[:, 57:58], LN_EPS, None, ALU.add)
                tt("pool", stB[:, 59:60], stB[:, 58:59], mhalf[:, 0:1], ALU.pow)
                stt(stB[:, 60:61], stB[:, 56:57], -1.0, stB[:, 59:60], ALU.mult, ALU.mult)
                act(xa[:], xa[:], AF.Identity, scale=stB[:, 59:60], bias=stB[:, 60:61])
                tt("dve", xa[:, 0:512], xa[:, 0:512], lng[:, 0:512], ALU.mult)
                tt("pool", xa[:, 512:1024], xa[:, 512:1024], lng[:, 512:1024], ALU.mult)
                tt("dve", xa[:, 0:512], xa[:, 0:512], lnb[:, 0:512], ALU.add)
                tt("pool", xa[:, 512:1024], xa[:, 512:1024], lnb[:, 512:1024], ALU.add)
                dma(y_d[tok, :], xa[:], "out")
            steps.append(s_out)
            return steps

        for it in range(NT + 2):
            lists = []
            if it < NT:
                lists.append(front(it))
            if 0 <= it - 1 < NT:
                lists.append(inv(it - 1))
            if 0 <= it - 2 < NT:
                lists.append(chain_out(it - 2))
            idx = [0] * len(lists)
            while True:
                best, bf = None, None
                for li, L in enumerate(lists):
                    if idx[li] < len(L):
                        frac = idx[li] / len(L)
                        if bf is None or frac < bf:
                            best, bf = li, frac
                if best is None:
                    break
                lists[best][idx[best]]()
                idx[best] += 1

        out_nodes = [i for i, nd in enumerate(P.nodes) if nd.dma is not None and (nd.dma.startswith("out") or nd.dma.startswith("dbg"))]
        order = P.schedule()
        P.emit(order)
        last = {}
        for i in out_nodes:
            last[P.nodes[i].dma] = i
        for i in last.values():
            P.wait_node("sp", i)

        block = es.enter_context(nc.Block())

        @block.tensor
        def _(e):
            P.replay("pe", e)

        @block.scalar
        def _(e):
            P.replay("act", e)

        @block.vector
        def _(e):
            P.replay("dve", e)

        @block.gpsimd
        def _(e):
            P.replay("pool", e)

        @block.sync
        def _(e):
            P.replay("sp", e)
    return nc


def make_in_maps(inputs, n_cores=8):
    f = lambda a: np.ascontiguousarray(np.asarray(a, dtype=np.float32))
    x = f(inputs["x"])
    c = f(inputs["c"])
    b_ada = f(inputs["b_ada"])
    shared = {
        "w_ada": f(inputs["w_ada"]),
        "b_ada_row": f(b_ada.reshape(1, 3072)),
        "b_ada_g": f(b_ada[2048:3072].reshape(1, 1024)),
        "w_in": f(inputs["w_in"]),
        "mu_b": f(inputs["mu_b"]),
        "ln_v_g": f(inputs["ln_v_g"]).reshape(512),
        "ln_v_b": f(inputs["ln_v_b"]).reshape(512),
        "w_spT": f(np.transpose(f(inputs["w_spatial"]), (0, 2, 1))),
        "b_sp": f(f(inputs["b_spatial"]).T),
        "w0a0": f(np.concatenate([f(inputs["w0"]), f(inputs["a0"])]).reshape(1, 1024)),
        "wua": f(np.concatenate([f(inputs["w_up"]), f(inputs["a_up"])], axis=0)),
        "k_k": f(inputs["k_k"]),
        "k_a": f(inputs["k_a"]),
        "r_k": f(inputs["r_k"]).reshape(512),
        "gn_g": f(inputs["gn_g"]).reshape(512),
        "gn_b": f(inputs["gn_b"]).reshape(512),
        "w_out": f(inputs["w_out"]),
        "ln_g": f(inputs["ln_g"]),
        "ln_b": f(inputs["ln_b"]),
    }
    maps = []
    for b in range(n_cores):
        m = dict(shared)
        m["x"] = f(x[b])
        m["xT"] = f(x[b].T)
        m["cT"] = f(c[b].reshape(8, 128).T)
        maps.append(m)
    return maps


def kernel(**inputs):
    maps = make_in_maps(inputs, 8)
    nc = build(NT=32)
    res = run_bass_kernel_spmd(nc, maps, core_ids=list(range(8)))
    out = np.stack([np.asarray(r["y"], dtype=np.float32) for r in res.results], axis=0)
    return out
```

```python
import contextlib
import numpy as np
import concourse.bass as bass
import concourse.mybir as mybir
from concourse.bass_utils import run_bass_kernel_spmd

F32 = mybir.dt.float32
BF16 = mybir.dt.bfloat16
AF = mybir.ActivationFunctionType
ALU = mybir.AluOpType
AX = mybir.AxisListType

D = 1024
T = 4096
NCOL = 3712
NB0 = 1536
LN_EPS = 1e-5
GN_EPS = 64e-5
ALPHA = 2.0 ** 0.25
NEGC = -0.6065306597126334
SEM_CAP = 30000


class Buf:
    def __init__(self, t, name):
        self.t = t
        self.name = name
        self.w = [None, None]
        self.r = [[], []]
        self.psum = False

    def __getitem__(self, idx):
        return self.t[idx]


class Node:
    __slots__ = ("eng", "fn", "dma", "deps", "dur", "start", "end", "tok", "users", "ndep", "est")

    def __init__(self, eng, fn, dma, deps, dur):
        self.eng = eng
        self.fn = fn
        self.dma = dma
        self.deps = deps
        self.dur = dur
        self.start = None
        self.end = None
        self.tok = None
        self.users = []
        self.ndep = 0
        self.est = 0.0


import os
HOP_NS = float(os.environ.get("K_HOP", "900"))
SAME_NS = float(os.environ.get("K_SAME", "120"))
PE_SCALE = float(os.environ.get("K_PESCALE", "1.0"))


class Prog:
    def __init__(self, nc, es):
        self.nc = nc
        self.es = es
        self.names = ["pe", "act", "dve", "pool", "sp"]
        self.prog = {k: [] for k in self.names}
        self.nsem = 0
        self.dsems = {}
        self.bufs = {}
        self.nodes = []
        self.nops = 0

    def dsem(self, name):
        if name not in self.dsems:
            s = self.es.enter_context(self.nc.semaphore("d_" + name))
            self.dsems[name] = [s, "d_" + name, 0]
        return self.dsems[name]

    def sb(self, name, shape, dt=F32):
        t = self.es.enter_context(self.nc.sbuf_tensor(name, shape, dt))
        b = Buf(t, name)
        self.bufs[name] = b
        return b

    def ps(self, name, shape, dt=F32):
        t = self.es.enter_context(self.nc.psum_tensor(name, shape, dt))
        b = Buf(t, name)
        b.psum = True
        self.bufs[name] = b
        return b

    def _halves(self, ap):
        p0 = ap.start_partition() if callable(ap.start_partition) else ap.start_partition
        n = ap.partition_size() if callable(ap.partition_size) else ap.partition_size
        hs = []
        if p0 < 64:
            hs.append(0)
        if p0 + n > 64:
            hs.append(1)
        return hs

    def _regions(self, aps):
        out = []
        for ap in aps:
            nm = ap.tensor.name
            b = self.bufs.get(nm)
            if b is None:
                continue
            for h in self._halves(ap):
                out.append((b, h))
        return out

    def op(self, eng, fn, outs=(), ins=(), dma=None, dur=300.0):
        self.nops += 1
        if self.nops > getattr(self, "max_ops", 10 ** 9):
            return None
        rd = self._regions(ins)
        wr = self._regions(outs)
        deps = set()
        for (b, h) in rd:
            if b.w[h] is not None:
                deps.add(b.w[h])
            if b.psum:
                for t in b.r[h]:
                    if self.nodes[t].eng != eng:
                        deps.add(t)
        for (b, h) in wr:
            if b.w[h] is not None:
                deps.add(b.w[h])
            for t in b.r[h]:
                deps.add(t)
        nid = len(self.nodes)
        self.nodes.append(Node(eng, fn, dma, sorted(deps), float(dur)))
        for (b, h) in rd:
            b.r[h].append(nid)
        for (b, h) in wr:
            b.w[h] = nid
            b.r[h] = []
        return nid

    def schedule(self):
        nodes = self.nodes
        for i, nd in enumerate(nodes):
            nd.ndep = len(nd.deps)
            for d in nd.deps:
                nodes[d].users.append(i)
        free = {k: 0.0 for k in self.names}
        cands = {k: [] for k in self.names}
        order = {k: [] for k in self.names}
        for i, nd in enumerate(nodes):
            if nd.ndep == 0:
                cands[nd.eng].append(i)
        remaining = len(nodes)
        while remaining:
            best = None
            for k in self.names:
                cl = cands[k]
                if not cl:
                    continue
                f = free[k]
                pick = None
                for i in cl:
                    e = nodes[i].est
                    if e <= f:
                        if pick is None or pick[0] > f or i < pick[1]:
                            pick = (f, i) if (pick is None or pick[0] > f or i < pick[1]) else pick
                    else:
                        if pick is None or e < pick[0]:
                            pick = (e, i)
                if best is None or pick[0] < best[0] or (pick[0] == best[0] and pick[1] < best[1]):
                    best = (pick[0], pick[1], k)
            st, i, k = best
            nd = nodes[i]
            cands[k].remove(i)
            nd.start = st
            if nd.dma is not None:
                nd.end = st + nd.dur
                free[k] = st + 60.0
            else:
                nd.end = st + nd.dur
                free[k] = nd.end
            order[k].append(i)
            remaining -= 1
            for u in nd.users:
                un = nodes[u]
                lat = SAME_NS if (un.eng == k and nd.dma is None) else HOP_NS
                if un.eng == "pe" and k == "pe":
                    lat = 0.0
                t = nd.end + lat
                if t > un.est:
                    un.est = t
                un.ndep -= 1
                if un.ndep == 0:
                    cands[un.eng].append(u)
        self.makespan = max(nd.end for nd in nodes) if nodes else 0.0
        return order

    def emit(self, order):
        nodes = self.nodes
        sem = {}
        cnt = {}
        waited = {k: {} for k in self.names}

        def new_sem(k):
            self.nsem += 1
            nm = "s_%s_%d" % (k, self.nsem)
            s_ = self.es.enter_context(self.nc.semaphore(nm))
            sem[k] = (s_, nm)
            cnt[k] = 0
        for k in self.names:
            new_sem(k)
        allids = sorted(range(len(nodes)), key=lambda i: (nodes[i].start, i))
        pos = {k: 0 for k in self.names}
        for i in allids:
            nd = nodes[i]
            k = nd.eng
            assert order[k][pos[k]] == i
            pos[k] += 1
            for d in nd.deps:
                dn = nodes[d]
                if dn.eng == "pe" and k == "pe" and dn.dma is None:
                    continue
                semh, semn, val = dn.tok
                if waited[k].get(semn, 0) >= val:
                    continue
                waited[k][semn] = val
                self.prog[k].append(("wait", semh, val))
            if nd.dma is not None:
                ds = self.dsem(nd.dma)
                ds[2] += 16
                nd.tok = (ds[0], ds[1], ds[2])
                self.prog[k].append(("op", nd.fn, ds[0], 16))
            else:
                if cnt[k] >= SEM_CAP:
                    new_sem(k)
                cnt[k] += 1
                nd.tok = (sem[k][0], sem[k][1], cnt[k])
                self.prog[k].append(("op", nd.fn, sem[k][0], 1))
        self.waited = waited

    def wait_node(self, eng, nid):
        semh, semn, val = self.nodes[nid].tok
        if self.waited[eng].get(semn, 0) >= val:
            return
        self.waited[eng][semn] = val
        self.prog[eng].append(("wait", semh, val))

    def replay(self, eng, e):
        for item in self.prog[eng]:
            if item[0] == "wait":
                e.wait_ge(item[1], item[2])
            else:
                item[1](e).then_inc(item[2], item[3])


def build(NT=32, dbg=None, max_ops=None):
    nc = bass.Bass("TRN2", target_bir_lowering=False)
    dram = {}

    def din(name, shape):
        dram[name] = nc.dram_tensor(name, list(shape), F32, kind="ExternalInput").ap()
        return dram[name]

    xT_d = din("xT", [D, T])
    x_d = din("x", [T, D])
    cT_d = din("cT", [128, 8])
    wada_d = din("w_ada", [D, 3 * D])
    bada_row_d = din("b_ada_row", [1, 3 * D])
    bada_g_d = din("b_ada_g", [1, D])
    win_d = din("w_in", [D, NCOL])
    mu_d = din("mu_b", [2176])
    lnvg_d = din("ln_v_g", [512])
    lnvb_d = din("ln_v_b", [512])
    wsT_d = din("w_spT", [8, 128, 128])
    bsp_d = din("b_sp", [128, 8])
    w0a0_d = din("w0a0", [1, 1024])
    wua_d = din("wua", [128, 512])
    kk_d = din("k_k", [512])
    ka_d = din("k_a", [512])
    rk_d = din("r_k", [512])
    gng_d = din("gn_g", [512])
    gnb_d = din("gn_b", [512])
    wout_d = din("w_out", [D, D])
    lng_d = din("ln_g", [D])
    lnb_d = din("ln_b", [D])
    y_d = nc.dram_tensor("y", [T, D], F32, kind="ExternalOutput").ap()
    dbg_d = None
    if dbg:
        dbg_d = nc.dram_tensor("dbg", [128, 16384], F32, kind="ExternalOutput").ap()

    es = contextlib.ExitStack()
    with es:
        P = Prog(nc, es)
        if max_ops is not None:
            P.max_ops = max_ops
        sb, ps, op = P.sb, P.ps, P.op

        w_in_c = [sb("w_in_c%d" % j, [128, 8, min(512, NCOL - j * 512)], BF16) for j in range(8)]
        w_out_b = sb("w_out_b", [128, 8, D], BF16)
        wsT_b = sb("wsT_b", [128, 8, 128], BF16)
        wua_b = sb("wua_b", [128, 512], BF16)
        identb = sb("identb", [128, 128], BF16)
        identf = sb("identf", [128, 128])
        ones_f = sb("ones_f", [128, 128])
        tri = sb("tri", [128, 3, 128])
        maskNN = sb("maskNN", [128, 4, 128], BF16)
        maskS8 = sb("maskS8", [128, 8, 64], BF16)
        maskI8 = sb("maskI8", [128, 8, 64], BF16)
        I8 = sb("I8", [128, 8, 64], BF16)
        mu = sb("mu", [128, 2176])
        lnvg = sb("lnvg", [128, 512])
        lnvb = sb("lnvb", [128, 512])
        kk_ = sb("kk_", [128, 512])
        ka_ = sb("ka_", [128, 512])
        rk_ = sb("rk_", [128, 512])
        gng = sb("gng", [128, 512])
        gnb = sb("gnb", [128, 512])
        lng = sb("lng", [128, D])
        lnb = sb("lnb", [128, D])
        rows = sb("rows", [128, 512])
        small = sb("small", [128, 64])
        mhalf = sb("mhalf", [128, 8])
        XT = sb("XT", [128, 4, 128])
        hT = [sb("hT%d" % i, [128, 8, 130], BF16) for i in range(2)]
        hd = sb("hd", [128, 8, 128], BF16)
        XA = [sb("XA0", [128, D])] * 2
        U_sb = sb("U_sb", [128, 512], BF16)
        SG_a = sb("SG_a", [128, 512], BF16)
        SG_b = [sb("SG_b%d" % i, [128, 512], BF16) for i in range(3)]
        R_b = sb("R_b", [128, 512], BF16)
        V_g = sb("V_g", [128, 512])
        VN = sb("VN", [128, 512], BF16)
        PBt = sb("PBt", [128, 2176])
        SIGW = sb("SIGW", [128, 512])
        A_sig = sb("A_sig", [128, 512], BF16)
        NA = sb("NA", [128, 512], BF16)
        BB = sb("BB", [128, 512], BF16)
        KM = sb("KM", [128, 512], BF16)
        S1 = sb("S1", [128, 512])
        S2 = sb("S2", [128, 512])
        S3 = sb("S3", [128, 512])
        TA = sb("TA", [128, 512], BF16)
        TB = sb("TB", [128, 512], BF16)
        BH = [sb("BH%d" % i, [128, 512], BF16) for i in range(2)]
        KH = [sb("KH%d" % i, [128, 512], BF16) for i in range(2)]
        VB = [sb("VB%d" % i, [128, 512], BF16) for i in range(3)]
        WA_T = sb("WA_T", [128, 128], BF16)
        FM2 = [sb("FM2_%d" % i, [128, 4, 8, 64], BF16) for i in range(2)]
        X0 = sb("X0", [128, 8, 192], BF16)
        Xa = sb("Xa", [128, 8, 192], BF16)
        ABR2 = [sb("ABR%d" % i, [128, 8, 64], BF16) for i in range(2)]
        AAK2 = [sb("AAK%d" % i, [128, 8, 64], BF16) for i in range(2)]
        AKR2 = [sb("AKR%d" % i, [128, 8, 64], BF16) for i in range(2)]
        EB2 = [sb("EB%d" % i, [128, 8, 64], BF16) for i in range(2)]
        Tm2 = [sb("Tm%d" % i, [128, 8, 64], BF16) for i in range(2)]
        WTs = sb("WTs", [128, 512], BF16)
        onesb_t = WTs
        UTs = sb("UTs", [128, 512], BF16)
        ST = sb("ST", [128, 8, 64])
        Dg = sb("Dg", [128, 8, 64])
        STb = sb("STb", [128, 8, 64], BF16)
        GC = [sb("GC%d" % i, [128, 16]) for i in range(3)]
        YS = sb("YS", [128, 512])
        CATa = [sb("CATa%d" % i, [128, 512], BF16) for i in range(3)]
        CATb = sb("CATb", [128, 512], BF16)
        CATT = sb("CATT", [128, 8, 128], BF16)
        st = sb("st", [128, 64])
        stB = sb("stB", [128, 64])

        PB0 = ps("PB0", [128, 512])
        PB1 = ps("PB1", [128, 512])
        PM = ps("PM", [128, 512])
        PY = ps("PY", [128, 512])
        PX0 = ps("PX0", [128, 512])
        PX1 = ps("PX1", [128, 512])
        PP = ps("PP", [128, 512])
        PC = ps("PC", [128, 512])
        PBIG = [PB0, PB1]
        big_i = [0]

        def nextbig():
            b = PBIG[big_i[0] % 2]
            big_i[0] += 1
            return b

        def fsz(ap):
            n_ = 1
            for d_ in ap.shape[1:]:
                n_ *= int(d_)
            return n_

        def mm(out, lhsT, rhs, start=True, stop=True, tp=None):
            n_ = fsz(rhs)
            d_ = 55.0 if n_ <= 64 else 0.55 * n_ + 20
            if lhsT.dtype == F32:
                d_ = 4 * d_ + 100
            d_ *= PE_SCALE
            if tp is None:
                op("pe", lambda e: e.matmul(out, lhsT=lhsT, rhs=rhs, start=start, stop=stop),
                   outs=[out], ins=[lhsT, rhs], dur=d_)
            else:
                op("pe", lambda e: e.matmul(out, lhsT=lhsT, rhs=rhs, start=start, stop=stop, tile_position=tp),
                   outs=[out], ins=[lhsT, rhs], dur=d_)

        def tr(out, in_, ident, tp=None):
            if tp is None:
                op("pe", lambda e: e.transpose(out=out, in_=in_, identity=ident), outs=[out], ins=[in_, ident], dur=100.0)
            else:
                op("pe", lambda e: e.transpose(out=out, in_=in_, identity=ident, tile_position=tp), outs=[out], ins=[in_, ident], dur=55.0)

        def act(out, in_, func, scale=None, bias=None):
            kw = {}
            if scale is not None:
                kw["scale"] = scale
            if bias is not None:
                kw["bias"] = bias
            ins = [in_] + [a for a in (scale, bias) if hasattr(a, "tensor")]
            op("act", lambda e: e.activation(out=out, in_=in_, func=func, **kw), outs=[out], ins=ins,
               dur=240.0 + 0.7 * fsz(out) + (200.0 if len(ins) > 1 else 0.0))

        def tt(eng, out, in0, in1, o):
            d_ = (110.0 + 1.1 * fsz(out)) if eng == "dve" else (300.0 + 1.75 * fsz(out))
            if o == ALU.pow:
                d_ = 1400.0
            op(eng, lambda e: e.tensor_tensor(out=out, in0=in0, in1=in1, op=o), outs=[out], ins=[in0, in1], dur=d_)

        def ts(eng, out, in0, s1, s2, o0, o1=None):
            ins = [in0] + [a for a in (s1, s2) if hasattr(a, "tensor")]
            d_ = (110.0 + 1.1 * fsz(out)) if eng == "dve" else (300.0 + 1.75 * fsz(out))
            if o1 is None:
                op(eng, lambda e: e.tensor_scalar(out=out, in0=in0, scalar1=s1, scalar2=None, op0=o0),
                   outs=[out], ins=ins, dur=d_)
            else:
                op(eng, lambda e: e.tensor_scalar(out=out, in0=in0, scalar1=s1, scalar2=s2, op0=o0, op1=o1),
                   outs=[out], ins=ins, dur=d_)

        def stt(out, in0, s, in1, o0, o1):
            ins = [in0, in1] + ([s] if hasattr(s, "tensor") else [])
            op("dve", lambda e: e.scalar_tensor_tensor(out=out, in0=in0, scalar=s, in1=in1, op0=o0, op1=o1),
               outs=[out], ins=ins, dur=110.0 + 1.1 * fsz(out))

        def red(out, in_):
            op("dve", lambda e: e.tensor_reduce(out=out, in_=in_, axis=AX.X, op=ALU.add), outs=[out], ins=[in_],
               dur=110.0 + 1.1 * fsz(in_))

        def cp(eng, out, in_):
            if eng == "act":
                op("act", lambda e: e.activation(out=out, in_=in_, func=AF.Copy), outs=[out], ins=[in_],
                   dur=240.0 + 0.7 * fsz(out))
            else:
                op(eng, lambda e: e.tensor_copy(out=out, in_=in_), outs=[out], ins=[in_],
                   dur=(110.0 + 1.1 * fsz(out)) if eng == "dve" else (300.0 + 1.75 * fsz(out)))

        def dma(out, in_, sem):
            op("sp", lambda e: e.dma_start(out=out, in_=in_), outs=[out], ins=[in_], dma=sem,
               dur=2000.0 + 0.004 * 4 * fsz(out) * 128)

        def memset(eng, ap, v):
            op(eng, lambda e: e.memset(ap, v), outs=[ap], dur=300.0)

        def asel(out, in_, pattern, cmp, base, cm):
            op("pool", lambda e: e.affine_select(out=out, in_=in_, pattern=pattern, compare_op=cmp, fill=0.0,
                                                 base=base, channel_multiplier=cm), outs=[out], ins=[in_],
               dur=250.0 + 0.9 * fsz(out))

        dbg_off = [0]

        def dump(name, ap, ncols):
            if not dbg:
                return
            o = dbg_off[0]
            n = ap.partition_size() if callable(ap.partition_size) else ap.partition_size
            p0 = ap.start_partition() if callable(ap.start_partition) else ap.start_partition
            dbg[name] = (o, p0, n, ncols)
            dma(dbg_d[p0:p0 + n, o:o + ncols], ap, "dbg_" + name)
            dbg_off[0] += ncols

        memset("pool", ones_f[:], 1.0)
        memset("pool", WTs[:], 1.0)
        memset("pool", tri[:], 0.0)
        memset("pool", mhalf[:], -0.5)
        memset("pool", hT[1][:], 0.0)
        memset("pool", ST[:], 0.0)
        memset("pool", STb[:], 0.0)
        memset("pool", S3[:], NEGC)
        asel(identf[:], ones_f[:], [[-1, 128]], ALU.is_equal, 0, 1)
        cp("pool", identb[:], identf[:])
        for c in range(2):
            psl = slice(c * 64, (c + 1) * 64)
            asel(tri[psl, 0, c * 64:(c + 1) * 64], S3[psl, 0:64], [[1, 64]], ALU.is_ge, 0, -1)
            asel(tri[psl, 1, c * 64:(c + 1) * 64], S3[psl, 0:64], [[1, 64]], ALU.is_gt, 0, -1)
            asel(tri[psl, 2, c * 64:(c + 1) * 64], S3[psl, 0:64], [[-1, 64]], ALU.is_gt, 0, 1)
            asel(maskS8[psl, :, :], WTs[psl, :].rearrange("p (h t) -> p h t", h=8), [[0, 8], [1, 64]], ALU.is_gt, 0, -1)
            asel(maskI8[psl, :, :], WTs[psl, :].rearrange("p (h t) -> p h t", h=8), [[0, 8], [1, 64]], ALU.is_ge, 0, -1)
            asel(I8[psl, :, :], WTs[psl, :].rearrange("p (h t) -> p h t", h=8), [[0, 8], [-1, 64]], ALU.is_equal, 0, 1)
            asel(maskNN[psl, :, 0:64], WTs[psl, 0:256].rearrange("p (h t) -> p h t", h=4), [[0, 4], [-1, 64]], ALU.is_gt, 0, 1)
            asel(maskNN[psl, :, 64:128], WTs[psl, 0:256].rearrange("p (h t) -> p h t", h=4), [[0, 4], [1, 64]], ALU.is_gt, 0, -1)

        for (tile_, src, n) in [(mu, mu_d, 2176), (lnvg, lnvg_d, 512), (lnvb, lnvb_d, 512), (kk_, kk_d, 512),
                                (ka_, ka_d, 512), (rk_, rk_d, 512), (gng, gng_d, 512), (gnb, gnb_d, 512),
                                (lng, lng_d, D), (lnb, lnb_d, D)]:
            dma(tile_[:], src.partition_broadcast(128), "par_" + tile_.name)
        dma(small[:, 8:16], bsp_d[:, :], "par_bsp")
        dma(rows[0:1, :], bada_g_d[:, 0:512], "par_rows0")
        dma(rows[32:33, :], bada_g_d[:, 512:1024], "par_rows32")
        dma(small[:, 0:8], cT_d[:, :], "par_c")
        act(small[:, 24:32], small[:, 0:8], AF.Silu)

        for g in range(8):
            stg = XA[0]
            dma(stg[:, 0:128], wsT_d[g, :, :], "stgw")
            asel(stg[:, 128:256], stg[:, 0:128], [[1, 128]], ALU.is_ge, 0, -1)
            cp("pool", wsT_b[:, g, :], stg[:, 128:256])
        stg = XA[0]
        dma(stg[:, 0:512], wua_d[:, :], "stgw")
        cp("dve", wua_b[:], stg[:, 0:512])

        slots = [S1[:, :], S2[:, :], V_g[:, :], YS[:, :], SIGW[:, :], S3[:, :]]
        si = [0]
        ring = [list(range(6))]

        def stage(src_ap, w=512):
            r_ = ring[0]
            i = r_[si[0] % len(r_)]
            si[0] += 1
            v = slots[i][:, 0:w]
            dma(v, src_ap, "stg%d" % i)
            return v

        mod_banks = [PX0, PX1, PP, PC, PB0, PB1]
        dma(PBt[0:1, 0:2048], bada_row_d[:, 0:2048], "par_badarow")

        def mod_chunk(j):
            po = 32 if j == 5 else 0
            for kc in range(8):
                v = stage(wada_d[kc * 128:(kc + 1) * 128, j * 512:(j + 1) * 512])
                if po:
                    mm(mod_banks[j][po:po + 1, :], small[:, 24 + kc:25 + kc], v, start=(kc == 0), stop=(kc == 7), tp=(0, po))
                else:
                    mm(mod_banks[j][0:1, :], small[:, 24 + kc:25 + kc], v, start=(kc == 0), stop=(kc == 7))
        for j in range(4):
            mod_chunk(j)
            tt("dve", PBt[0:1, j * 512:(j + 1) * 512], mod_banks[j][0:1, :], PBt[0:1, j * 512:(j + 1) * 512], ALU.add)
        for j in range(16):
            mm(PM[:, j:j + 1], PBt[0:1, j * 128:(j + 1) * 128], ones_f[0:1, 0:1])
        cp("dve", small[:, 32:48], PM[:, 0:16])
        ts("dve", small[:, 40:48], small[:, 40:48], 1.0, None, ALU.add)
        ci = 0
        for j in range(8):
            c0 = j * 512
            w = min(512, NCOL - c0)
            for kc in range(8):
                v = stage(win_d[kc * 128:(kc + 1) * 128, c0:c0 + w], w)
                cp(["act", "dve", "pool"][ci % 3], w_in_c[j][:, kc, 0:w], v)
                ci += 1
        ring[0] = [3, 5]
        for j in (4, 5):
            mod_chunk(j)
        tt("dve", rows[0:1, 0:512], PB0[0:1, :], rows[0:1, 0:512], ALU.add)
        tt("dve", rows[32:33, 0:512], PB1[32:33, :], rows[32:33, 0:512], ALU.add)
        for n2 in range(2):
            bank = [PX0, PX1][n2]
            mm(bank[:, :], ones_f[32 * n2:32 * n2 + 1, 0:128], rows[32 * n2:32 * n2 + 1, 0:512])
            cp("act", XA[0][:, n2 * 512:(n2 + 1) * 512], bank[:, :])
        for nb in range(2):
            for kc in range(8):
                v = stage(wout_d[kc * 128:(kc + 1) * 128, nb * 512:(nb + 1) * 512])
                tt(["dve", "pool"][ci % 2], w_out_b[:, kc, nb * 512:(nb + 1) * 512], v,
                   XA[0][:, nb * 512:(nb + 1) * 512], ALU.mult)
                ci += 1
        dma(rows[0:1, 0:512], w0a0_d[:, 0:512], "par_rows0")
        dma(rows[64:65, 0:512], w0a0_d[:, 512:1024], "par_rows64")

        xT_v = xT_d.rearrange("(k p) t -> p k t", p=128)

        PMb = PM[:, :].bitcast(BF16)

        def front(n):
            steps = []
            par = n % 2
            h_ = hT[par]
            tok = slice(n * 128, (n + 1) * 128)
            FMp, VBp, BHp, KHp, SGp = FM2[par], VB[n % 3], BH[par], KH[par], SG_b[n % 3]
            GCp, CAp = GC[n % 3], CATa[n % 3]

            def s_load():
                for kc in range(8):
                    dma(XT[:, kc % 4, :], xT_v[:, kc, tok], "xT%d" % (kc % 4))
                    ts("dve", h_[:, kc, 2:130], XT[:, kc % 4, :], small[:, 40 + kc:41 + kc], small[:, 32 + kc:33 + kc],
                       ALU.mult, ALU.add)
                cp("dve", h_[:, :, 1:2], hT[1 - par][:, :, 129:130])
                tt("dve", hd[:], h_[:, :, 1:129], h_[:, :, 2:130], ALU.subtract)
            steps.append(s_load)

            def proj(j, src=None):
                c0 = j * 512
                w = min(512, NCOL - c0)
                bank = nextbig()
                for kc in range(8):
                    lh = h_[:, kc, 2:130] if src is None else src[:, kc, :]
                    mm(bank[:, 0:w], lh, w_in_c[j][:, kc, 0:w], start=(kc == 0), stop=(kc == 7))
                return bank, w

            def s_u():
                bank, w = proj(0)
                act(U_sb[:], bank[:, :], AF.Gelu_apprx_tanh)
            steps.append(s_u)

            def s_v():
                bank, w = proj(1)
                act(V_g[:], bank[:, :], AF.Gelu_apprx_tanh)
            steps.append(s_v)

            def s_ga():
                bank, w = proj(2)
                act(SG_a[:], bank[:, :], AF.Silu)
            steps.append(s_ga)

            def stats(stt_, src3, sq, sq3, eps):
                red(stt_[:, 0:8], src3)
                tt("pool", sq, src3.rearrange("p g d -> p (g d)") if False else sq_src[0], sq_src[0], ALU.mult)
                red(stt_[:, 8:16], sq3)
                ts("dve", stt_[:, 0:8], stt_[:, 0:8], 1.0 / 64, None, ALU.mult)
                tt("dve", stt_[:, 16:24], stt_[:, 0:8], stt_[:, 0:8], ALU.mult)
                ts("dve", stt_[:, 8:16], stt_[:, 8:16], 1.0 / 64, eps, ALU.mult, ALU.add)
                tt("dve", stt_[:, 8:16], stt_[:, 8:16], stt_[:, 16:24], ALU.subtract)
                tt("pool", stt_[:, 24:32], stt_[:, 8:16], mhalf[:], ALU.pow)
            sq_src = [None]

            def s_ln():
                Vg3 = V_g[:].rearrange("p (g d) -> p g d", g=8)
                S13 = S1[:].rearrange("p (g d) -> p g d", g=8)
                sq_src[0] = V_g[:]
                stats(st, Vg3, S1[:], S13, LN_EPS)
                mb = st[:, 0:8].unsqueeze(2).broadcast_to([128, 8, 64])
                rb = st[:, 24:32].unsqueeze(2).broadcast_to([128, 8, 64])
                tt("dve", S13, Vg3, mb, ALU.subtract)
                tt("dve", S13, S13, rb, ALU.mult)
                tt("pool", S1[:], S1[:], lnvg[:], ALU.mult)
                tt("pool", VN[:], S1[:], lnvb[:], ALU.add)
            steps.append(s_ln)

            def s_sp():
                tt("pool", U_sb[:], U_sb[:], SG_a[:], ALU.mult)
                for g in range(8):
                    mm(PM[:, g * 64:(g + 1) * 64], wsT_b[:, g, :], VN[:, g * 64:(g + 1) * 64], start=True, stop=True)
                tt("dve", S1[:].rearrange("p (g d) -> p g d", g=8), PM[:, :].rearrange("p (g d) -> p g d", g=8),
                   small[:, 8:16].unsqueeze(2).broadcast_to([128, 8, 64]), ALU.add)
                tt("dve", CAp[:], S1[:], U_sb[:], ALU.mult)
                if dbg and n == dbg.get("_tile", 0):
                    dump("out_a", S1[:], 512)
            steps.append(s_sp)

            def mk_b(j):
                def s_b():
                    c0 = j * 512
                    bank_d, w = proj(3 + j, src=hd)
                    tt("dve", PBt[:, c0:c0 + w], bank_d[:, 0:w], mu[:, c0:c0 + w], ALU.mult)
                    bank_p, w = proj(3 + j)
                    tt("dve", PBt[:, c0:c0 + w], bank_p[:, 0:w], PBt[:, c0:c0 + w], ALU.add)
                return s_b
            for j in range(5):
                steps.append(mk_b(j))

            r_ = PBt[:, 0:512]
            k_ = PBt[:, 512:1024]
            v_ = PBt[:, 1024:1536]
            g_ = PBt[:, 1536:2048]

            def s_copies():
                if dbg and n == dbg.get("_tile", 0):
                    dump("pb", PBt[:, :], 2176)
                act(SGp[:], g_, AF.Silu)
                cp("act", VBp[:], v_)
                cp("act", R_b[:], r_)
            steps.append(s_copies)

            def s_wa():
                tr(PM[:, 0:128], PBt[:, 2048:2176], identf[:])
                act(WA_T[0:64, :], PM[0:64, 0:128], AF.Tanh)
                cp("act", WA_T[64:128, :], PM[64:128, 0:128])
                mm(PM[:, :], WA_T[0:64, :], wua_b[0:64, :], start=True, stop=False)
                mm(PM[:, :], ones_f[0:1, 0:128], rows[0:1, 0:512], start=False, stop=True)
                act(SIGW[:], PM[:, :], AF.Sigmoid)
                mm(PY[:, :], WA_T[64:128, :], wua_b[64:128, :], start=True, stop=False)
                mm(PY[:, :], ones_f[64:65, 0:128], rows[64:65, 0:512], start=False, stop=True)
                act(A_sig[:], PY[:, :], AF.Sigmoid)
            steps.append(s_wa)

            def s_kk():
                S13 = S1[:].rearrange("p (g d) -> p g d", g=8)
                S23 = S2[:].rearrange("p (g d) -> p g d", g=8)
                tt("dve", S1[:], k_, kk_[:], ALU.mult)
                tt("dve", S2[:], S1[:], S1[:], ALU.mult)
                red(st[:, 32:40], S23)
                ts("dve", st[:, 32:40], st[:, 32:40], 1e-12, None, ALU.add)
                tt("pool", st[:, 40:48], st[:, 32:40], mhalf[:], ALU.pow)
                tt("dve", NA[:].rearrange("p (g d) -> p g d", g=8), S13,
                   st[:, 40:48].unsqueeze(2).broadcast_to([128, 8, 64]), ALU.mult)
                tt("dve", BB[:], NA[:], A_sig[:], ALU.mult)
                stt(S2[:], A_sig[:], -1.0, ka_[:], ALU.add, ALU.mult)
                stt(KM[:], S2[:], 1.0, k_, ALU.add, ALU.mult)
                tt("pool", S2[:], R_b[:], KM[:], ALU.mult)
                tt("pool", S2[:], S2[:], rk_[:], ALU.mult)
                red(GCp[:, 8:16], S23)
            steps.append(s_kk)

            def fm_pass(pi):
                for i, kt in enumerate([TA, TB]):
                    for h in range(8):
                        for c in range(2):
                            cs = slice(c * 64, (c + 1) * 64)
                            o = (i * 8 + h) * 64
                            tr(PMb[cs, o:o + 64], kt[cs, h * 64:(h + 1) * 64], identb[cs, c * 64:(c + 1) * 64],
                               tp=(c * 64, c * 64))
                cp("act", FMp[:, 2 * pi:2 * pi + 2, :, :],
                   PMb.rearrange("p (a h t) -> p a h t", a=2, h=8))

            def s_cum():
                mm(PY[:, :], tri[:, 0, :], SIGW[:])
                act(S1[:], PY[:, :], AF.Exp)
                act(S2[:], PY[:, :], AF.Exp, scale=-1.0)
                tt("dve", TB[:], R_b[:], S1[:], ALU.mult)
                mm(PY[:, :], tri[:, 1, :], SIGW[:])
                act(S1[:], PY[:, :], AF.Exp)
                stt(TA[:], NA[:], -1.0, S1[:], ALU.mult, ALU.mult)
                fm_pass(0)
            steps.append(s_cum)

            def s_cum2():
                tt("dve", TA[:], BB[:], S2[:], ALU.mult)
                tt("dve", TB[:], KM[:], S2[:], ALU.mult)
                mm(PY[:, :], tri[:, 2, :], SIGW[:])
                act(S1[:], PY[:, :], AF.Exp)
                tt("dve", BHp[:], BB[:], S1[:], ALU.mult)
                tt("pool", KHp[:], KM[:], S1[:], ALU.mult)
                for h in range(8):
                    for c in range(2):
                        cs = slice(c * 64, (c + 1) * 64)
                        mm(PY[cs, h:h + 1], SIGW[cs, h * 64:(h + 1) * 64], ones_f[cs, 0:1], tp=(c * 64, c * 64))
                act(GCp[:, 0:8], PY[:, 0:8], AF.Exp, scale=NEGC)
                fm_pass(1)
            steps.append(s_cum2)
            byname = {f.__name__: f for f in steps}
            bsteps = [f for f in steps if f.__name__ == "s_b"]
            order_ = [byname["s_load"], bsteps[4], byname["s_wa"], bsteps[0], bsteps[1], bsteps[2], bsteps[3],
                      byname["s_copies"], byname["s_kk"], byname["s_u"], byname["s_v"], byname["s_ga"],
                      byname["s_ln"], byname["s_sp"], byname["s_cum"], byname["s_cum2"]]
            assert len(order_) == len(steps)
            return order_

        def inv(n):
            steps = []
            par = n % 2
            FMp = FM2[par]
            ABR, AAK, AKR, EB, Tm = ABR2[par], AAK2[par], AKR2[par], EB2[par], Tm2[par]

            def fm(kind, h, c):
                return FMp[c * 64:(c + 1) * 64, kind, h, :]

            def s_A():
                for h in range(8):
                    for c in range(2):
                        cs = slice(c * 64, (c + 1) * 64)
                        bank = PX0 if h < 4 else PX1
                        o = (h % 4) * 128
                        mm(bank[cs, o + 64:o + 128], fm(2, h, c), fm(0, h, c), tp=(c * 64, c * 64))
                        mm(bank[cs, o:o + 64], fm(0, h, c), fm(2, h, c), tp=(c * 64, c * 64))
                for h4, bank in enumerate([PX0, PX1]):
                    tt("dve", X0[:, h4 * 4:(h4 + 1) * 4, 0:128], bank[:, :].rearrange("p (h t) -> p h t", h=4),
                       maskNN[:], ALU.mult)
                tt("pool", EB[:], I8[:], X0[:, :, 0:64], ALU.subtract)
            steps.append(s_A)

            def s_A2():
                for (dst, ka, kb, msk) in [(ABR, 2, 1, maskI8), (AAK, 3, 0, maskS8), (AKR, 3, 1, maskI8)]:
                    for h in range(8):
                        for c in range(2):
                            cs = slice(c * 64, (c + 1) * 64)
                            mm(PP[cs, h * 64:(h + 1) * 64], fm(ka, h, c), fm(kb, h, c), tp=(c * 64, c * 64))
                    tt("dve", dst[:], PP[:, :].rearrange("p (h t) -> p h t", h=8), msk[:], ALU.mult)
            steps.append(s_A2)

            seq = [(X0, Xa), (Xa, X0), (X0, Xa), (Xa, X0), (X0, Xa), (Xa, None)]

            def mk_round(k):
                S_, D_ = seq[k]

                def s_r():
                    for h in range(8):
                        for c in range(2):
                            cs = slice(c * 64, (c + 1) * 64)
                            tp = (c * 64, c * 64)
                            bank = PX0 if h < 4 else PX1
                            o = (h % 4) * 128
                            if k == 0:
                                mm(bank[cs, o:o + 64], S_[cs, h, 0:64], S_[cs, h, 64:128], tp=tp)
                            elif k <= 3:
                                mm(bank[cs, o:o + 128], S_[cs, h, 0:64], S_[cs, h, 64:192], tp=tp)
                            else:
                                mm(bank[cs, o + 64:o + 128], S_[cs, h, 0:64], S_[cs, h, 128:192], tp=tp)
                            if k <= 4:
                                mm(PP[cs, h * 64:(h + 1) * 64], S_[cs, h, 64:128], S_[cs, h, 0:64], tp=tp)
                    for h4, bank in enumerate([PX0, PX1]):
                        hs_ = slice(h4 * 4, (h4 + 1) * 4)
                        bv = bank[:, :].rearrange("p (h t) -> p h t", h=4)
                        if k <= 3:
                            cp("act", D_[:, hs_, 64:128], bv[:, :, 0:64])
                        if k == 0:
                            tt("pool", D_[:, hs_, 128:192], S_[:, hs_, 64:128], I8[:, hs_, :], ALU.add)
                        elif k <= 4:
                            tt("dve", D_[:, hs_, 128:192], bv[:, :, 64:128], S_[:, hs_, 128:192], ALU.add)
                        else:
                            tt("dve", Tm[:, hs_, :], bv[:, :, 64:128], S_[:, hs_, 128:192], ALU.add)
                    if k <= 4:
                        cp("act", D_[:, :, 0:64], PP[:, :].rearrange("p (h t) -> p h t", h=8))
                return s_r
            for k in range(6):
                steps.append(mk_round(k))

            def s_E():
                for h in range(8):
                    for c in range(2):
                        cs = slice(c * 64, (c + 1) * 64)
                        mm(PP[cs, h * 64:(h + 1) * 64], EB[cs, h, :], Tm[cs, h, :], tp=(c * 64, c * 64))
                tt("dve", EB[:], I8[:], PP[:, :].rearrange("p (h t) -> p h t", h=8), ALU.subtract)
            steps.append(s_E)
            return steps

        def chain_out(n):
            steps = []
            par = n % 2
            xa = XA[par]
            tok = slice(n * 128, (n + 1) * 128)
            isdbg = dbg and n == dbg.get("_tile", 0)
            FMp, VBp, BHp, KHp, SGp = FM2[par], VB[n % 3], BH[par], KH[par], SG_b[n % 3]
            GCp, CAp = GC[n % 3], CATa[n % 3]
            ABR, AAK, AKR, EB, Tm = ABR2[par], AAK2[par], AKR2[par], EB2[par], Tm2[par]

            def fm(kind, h, c):
                return FMp[c * 64:(c + 1) * 64, kind, h, :]

            def s_pre():
                dma(xa[:], x_d[tok, :], "xa")
                for c in range(2):
                    cs = slice(c * 64, (c + 1) * 64)
                    asel(Dg[cs, :, :], GCp[cs, 0:8].unsqueeze(2).broadcast_to([64, 8, 64]), [[0, 8], [-1, 64]],
                         ALU.is_equal, 0, 1)
            steps.append(s_pre)

            def mk_chain(c):
                cs = slice(c * 64, (c + 1) * 64)
                os_ = slice((1 - c) * 64, (2 - c) * 64)
                tpc = (c * 64, c * 64)

                def s_w():
                    for h in range(8):
                        mm(PC[cs, h * 64:(h + 1) * 64], fm(0, h, c), STb[cs, h, :], start=True, stop=False, tp=tpc)
                        mm(PC[cs, h * 64:(h + 1) * 64], AAK[cs, h, :], VBp[cs, h * 64:(h + 1) * 64], start=False, stop=True, tp=tpc)
                    cp("act", WTs[cs, :], PC[cs, :])

                def s_u():
                    for h in range(8):
                        mm(PC[cs, h * 64:(h + 1) * 64], Tm[cs, h, :], WTs[cs, h * 64:(h + 1) * 64], tp=tpc)
                    cp("act", WTs[cs, :], PC[cs, :])
                    for h in range(8):
                        mm(PC[cs, h * 64:(h + 1) * 64], EB[cs, h, :], WTs[cs, h * 64:(h + 1) * 64], tp=tpc)
                    tt("dve", UTs[cs, :], PC[cs, :], WTs[cs, :], ALU.add)

                def s_s():
                    tps = (c * 64, (1 - c) * 64)
                    for h in range(8):
                        o = PC[os_, h * 64:(h + 1) * 64]
                        mm(o, Dg[cs, h, :], ST[cs, h, :], start=True, stop=False, tp=tps)
                        mm(o, BHp[cs, h * 64:(h + 1) * 64], UTs[cs, h * 64:(h + 1) * 64], start=False, stop=False, tp=tps)
                        mm(o, KHp[cs, h * 64:(h + 1) * 64], VBp[cs, h * 64:(h + 1) * 64], start=False, stop=True, tp=tps)
                    cp("dve", ST[os_, :, :], PC[os_, :].rearrange("p (h i) -> p h i", h=8))
                    cp("act", STb[os_, :, :], PC[os_, :].rearrange("p (h i) -> p h i", h=8))

                def s_y():
                    for h in range(8):
                        o = PY[cs, h * 64:(h + 1) * 64]
                        mm(o, fm(1, h, c), STb[cs, h, :], start=True, stop=False, tp=tpc)
                        mm(o, ABR[cs, h, :], UTs[cs, h * 64:(h + 1) * 64], start=False, stop=False, tp=tpc)
                        mm(o, AKR[cs, h, :], VBp[cs, h * 64:(h + 1) * 64], start=False, stop=True, tp=tpc)
                    cp("act", YS[cs, :], PY[cs, :])
                return [s_w, s_u, s_s, s_y]
            for c in range(2):
                steps.extend(mk_chain(c))

            def s_post():
                if isdbg:
                    dump("ys", YS[:, :], 512)
                Y3 = YS[:].rearrange("p (g d) -> p g d", g=8)
                S33 = S3[:].rearrange("p (g d) -> p g d", g=8)
                red(stB[:, 0:8], Y3)
                tt("pool", S3[:], YS[:], YS[:], ALU.mult)
                red(stB[:, 8:16], S33)
                ts("dve", stB[:, 0:8], stB[:, 0:8], 1.0 / 64, None, ALU.mult)
                tt("dve", stB[:, 16:24], stB[:, 0:8], stB[:, 0:8], ALU.mult)
                ts("dve", stB[:, 8:16], stB[:, 8:16], 1.0 / 64, GN_EPS, ALU.mult, ALU.add)
                tt("dve", stB[:, 8:16], stB[:, 8:16], stB[:, 16:24], ALU.subtract)
                tt("pool", stB[:, 24:32], stB[:, 8:16], mhalf[:], ALU.pow)
                mb = stB[:, 0:8].unsqueeze(2).broadcast_to([128, 8, 64])
                rb = stB[:, 24:32].unsqueeze(2).broadcast_to([128, 8, 64])
                bb = GCp[:, 8:16].unsqueeze(2).broadcast_to([128, 8, 64])
                tt("pool", S33, VBp[:].rearrange("p (g d) -> p g d", g=8), bb, ALU.mult)
                tt("dve", Y3, Y3, mb, ALU.subtract)
                tt("dve", Y3, Y3, rb, ALU.mult)
                tt("dve", YS[:], YS[:], gng[:], ALU.mult)
                tt("dve", YS[:], YS[:], gnb[:], ALU.add)
                tt("dve", YS[:], YS[:], S3[:], ALU.add)
                tt("dve", CATb[:], YS[:], SGp[:], ALU.mult)
                if isdbg:
                    dump("out_b", YS[:, :], 512)
            steps.append(s_post)

            def s_out():
                for kc in range(8):
                    src_ = CAp if kc < 4 else CATb
                    tr(PMb[:, kc * 128:(kc + 1) * 128], src_[:, (kc % 4) * 128:(kc % 4 + 1) * 128], identb[:])
                cp("act", CATT[:], PMb.rearrange("p (k t) -> p k t", k=8))
                for nb in range(2):
                    bank = nextbig()
                    for kc in range(8):
                        mm(bank[:, :], CATT[:, kc, :], w_out_b[:, kc, nb * 512:(nb + 1) * 512], start=(kc == 0), stop=(kc == 7))
                    stt(xa[:, nb * 512:(nb + 1) * 512], xa[:, nb * 512:(nb + 1) * 512], ALPHA, bank[:, :], ALU.mult, ALU.add)
                    op("dve", (lambda nb_: (lambda e: e.bn_stats(out=stB[:, 32 + 6 * nb_:38 + 6 * nb_], in_=xa[:, nb_ * 512:(nb_ + 1) * 512])))(nb),
                       outs=[stB[:, 32 + 6 * nb:38 + 6 * nb]], ins=[xa[:, nb * 512:(nb + 1) * 512]])
                op("dve", lambda e: e.bn_aggr(out=stB[:, 56:58], in_=stB[:, 32:44]), outs=[stB[:, 56:58]], ins=[stB[:, 32:44]])
                ts("dve", stB[:, 58:59], stB[:, 57:58], LN_EPS, None, ALU.add)
                tt("pool", stB[:, 59:60], stB[:, 58:59], mhalf[:, 0:1], ALU.pow)
                stt(stB[:, 60:61], stB[:, 56:57], -1.0, stB[:, 59:60], ALU.mult, ALU.mult)
                act(xa[:], xa[:], AF.Identity, scale=stB[:, 59:60], bias=stB[:, 60:61])
                tt("dve", xa[:, 0:512], xa[:, 0:512], lng[:, 0:512], ALU.mult)
                tt("pool", xa[:, 512:1024], xa[:, 512:1024], lng[:, 512:1024], ALU.mult)
                tt("dve", xa[:, 0:512], xa[:, 0:512], lnb[:, 0:512], ALU.add)
                tt("pool", xa[:, 512:1024], xa[:, 512:1024], lnb[:, 512:1024], ALU.add)
                dma(y_d[tok, :], xa[:], "out")
            steps.append(s_out)
            return steps

        for it in range(NT + 2):
            lists = []
            if it < NT:
                lists.append(front(it))
            if 0 <= it - 1 < NT:
                lists.append(inv(it - 1))
            if 0 <= it - 2 < NT:
                lists.append(chain_out(it - 2))
            idx = [0] * len(lists)
            while True:
                best, bf = None, None
                for li, L in enumerate(lists):
                    if idx[li] < len(L):
                        frac = idx[li] / len(L)
                        if bf is None or frac < bf:
                            best, bf = li, frac
                if best is None:
                    break
                if lists[best][idx[best]].__name__ == "s_cum" and len(lists) >= 2 and lists[-1] is not lists[best]:
                    co = lists[-1]
                    names_ = [f.__name__ for f in co]
                    if "s_y" in names_:
                        last_y = max(i_ for i_, nm_ in enumerate(names_) if nm_ == "s_y")
                        while idx[-1] <= last_y:
                            co[idx[-1]]()
                            idx[-1] += 1
                lists[best][idx[best]]()
                idx[best] += 1

        out_nodes = [i for i, nd in enumerate(P.nodes) if nd.dma is not None and (nd.dma.startswith("out") or nd.dma.startswith("dbg"))]
        order = P.schedule()
        P.emit(order)
        last = {}
        for i in out_nodes:
            last[P.nodes[i].dma] = i
        for i in last.values():
            P.wait_node("sp", i)

        block = es.enter_context(nc.Block())

        @block.tensor
        def _(e):
            P.replay("pe", e)

        @block.scalar
        def _(e):
            P.replay("act", e)

        @block.vector
        def _(e):
            P.replay("dve", e)

        @block.gpsimd
        def _(e):
            P.replay("pool", e)

        @block.sync
        def _(e):
            P.replay("sp", e)
    return nc


def make_in_maps(inputs, n_cores=8):
    f = lambda a: np.ascontiguousarray(np.asarray(a, dtype=np.float32))
    x = f(inputs["x"])
    c = f(inputs["c"])
    b_ada = f(inputs["b_ada"])
    shared = {
        "w_ada": f(inputs["w_ada"]),
        "b_ada_row": f(b_ada.reshape(1, 3072)),
        "b_ada_g": f(b_ada[2048:3072].reshape(1, 1024)),
        "w_in": f(inputs["w_in"]),
        "mu_b": f(inputs["mu_b"]),
        "ln_v_g": f(inputs["ln_v_g"]).reshape(512),
        "ln_v_b": f(inputs["ln_v_b"]).reshape(512),
        "w_spT": f(np.transpose(f(inputs["w_spatial"]), (0, 2, 1))),
        "b_sp": f(f(inputs["b_spatial"]).T),
        "w0a0": f(np.concatenate([f(inputs["w0"]), f(inputs["a0"])]).reshape(1, 1024)),
        "wua": f(np.concatenate([f(inputs["w_up"]), f(inputs["a_up"])], axis=0)),
        "k_k": f(inputs["k_k"]),
        "k_a": f(inputs["k_a"]),
        "r_k": f(inputs["r_k"]).reshape(512),
        "gn_g": f(inputs["gn_g"]).reshape(512),
        "gn_b": f(inputs["gn_b"]).reshape(512),
        "w_out": f(inputs["w_out"]),
        "ln_g": f(inputs["ln_g"]),
        "ln_b": f(inputs["ln_b"]),
    }
    maps = []
    for b in range(n_cores):
        m = dict(shared)
        m["x"] = f(x[b])
        m["xT"] = f(x[b].T)
        m["cT"] = f(c[b].reshape(8, 128).T)
        maps.append(m)
    return maps


def kernel(**inputs):
    maps = make_in_maps(inputs, 8)
    nc = build(NT=32)
    res = run_bass_kernel_spmd(nc, maps, core_ids=list(range(8)))
    out = np.stack([np.asarray(r["y"], dtype=np.float32) for r in res.results], axis=0)
    return out
```
